# Optimizing a Trainium2 kernel written in Bass

```python
import math
import jax, jax.numpy as jnp
from jax import lax
import numpy as np

D_MODEL = 1024
BATCH = 8
SEQ = 2048
DEPTH = 2

D_MIX = D_MODEL
D_GROUP = D_MIX // 4
HEAD_DIM = 64
N_HEADS_GROUP = D_GROUP // HEAD_DIM

RWKV_DECAY_LORA = 32
RWKV_AAA_LORA = 32
RWKV_MV_LORA = 32
RWKV_GATE_LORA = 64
RWKV_GN_EPS = HEAD_DIM * 1e-5
RWKV_SIZES = (D_GROUP, D_GROUP, D_GROUP, RWKV_DECAY_LORA, RWKV_AAA_LORA, RWKV_GATE_LORA)
RWKV_COLS = sum(RWKV_SIZES)

DILATED_BRANCHES = ((128, 1), (512, 4), (2048, 16))
ALIBI_SLOPES = tuple(2.0 ** (-8.0 * (h + 1) / N_HEADS_GROUP) for h in range(N_HEADS_GROUP))
ATTN_COLS = 3 * D_GROUP

SSD_STATE = 128
SSD_GROUPS = 2
SSD_CONV = 4
SSD_CHUNK = 128
SSD_XBC = D_GROUP + 2 * SSD_GROUPS * SSD_STATE
SSD_COLS = D_GROUP + SSD_XBC + N_HEADS_GROUP

HGRN_CHUNK = 16
HGRN_COLS = 4 * D_GROUP

IN_COLS = RWKV_COLS + ATTN_COLS + SSD_COLS + HGRN_COLS

D_FF = 4 * D_MODEL
ALPHA = (2.0 * DEPTH) ** 0.25
BETA = (8.0 * DEPTH) ** -0.25
LN_EPS = 1e-5
RMS_EPS = 1e-5

kernel_name = "hymba_rwkv7_dilated_ssd_hgrn2_deepnorm"


def _split(t, sizes):
    out, o = [], 0
    for s in sizes:
        out.append(t[..., o:o + s])
        o += s
    return out


def _layer_norm(x, w, b):
    x32 = x.astype(jnp.float32)
    mu = jnp.mean(x32, -1, keepdims=True)
    var = jnp.mean(jnp.square(x32 - mu), -1, keepdims=True)
    return ((x32 - mu) * lax.rsqrt(var + LN_EPS) * w + b).astype(x.dtype)


def _rms(t):
    return t * lax.rsqrt(jnp.mean(jnp.square(t), -1, keepdims=True) + RMS_EPS)


def _token_shift_lerp(f, mu):
    prev = jnp.pad(f, ((0, 0), (1, 0), (0, 0)))[:, :-1]
    return f + (prev - f) * mu


def rwkv7_time_mix(feat, w0, w2, a0, a2, g2, k_k, k_a, r_k, lnx_w, lnx_b,
                   v_first, v_feat, v0, v2):
    bsz, slen, _ = feat.shape
    H, N = N_HEADS_GROUP, HEAD_DIM
    r, k, v, fw, fa, fg = _split(feat, RWKV_SIZES)
    w_log = -jax.nn.softplus(-(w0 + jnp.tanh(fw) @ w2)) - 0.5
    decay = jnp.exp(-jnp.exp(w_log))
    a = jax.nn.sigmoid(a0 + fa @ a2)
    g = jax.nn.sigmoid(fg) @ g2
    if v_first is None:
        v_first = v
    else:
        v = v + (v_first - v) * jax.nn.sigmoid(v0 + v_feat @ v2)

    def heads(t):
        return t.reshape(bsz, slen, H, N)

    kk = heads(k * k_k)
    kk = kk / jnp.maximum(jnp.sqrt(jnp.sum(jnp.square(kk), -1, keepdims=True)), 1e-12)
    k = k * (1.0 + (a - 1.0) * k_a)

    def step(state, inp):
        r_t, w_t, k_t, v_t, kk_t, a_t = inp
        sa = jnp.einsum('bhvk,bhk->bhv', state, -kk_t)
        state = (state * w_t[:, :, None, :]
                 + sa[..., None] * (kk_t * a_t)[:, :, None, :]
                 + v_t[..., None] * k_t[:, :, None, :])
        return state, jnp.einsum('bhvk,bhk->bhv', state, r_t)

    xs = tuple(jnp.swapaxes(heads(t), 0, 1) for t in (r, decay, k, v, kk, a))
    _, y = lax.scan(step, jnp.zeros((bsz, H, N, N), jnp.float32), xs)
    y = jnp.swapaxes(y, 0, 1)
    mu = jnp.mean(y, -1, keepdims=True)
    var = jnp.mean(jnp.square(y - mu), -1, keepdims=True)
    y = ((y - mu) * lax.rsqrt(var + RWKV_GN_EPS)).reshape(bsz, slen, D_GROUP) * lnx_w + lnx_b
    bonus = jnp.sum(heads(r) * heads(k) * r_k, -1, keepdims=True) * heads(v)
    return (y + bonus.reshape(bsz, slen, D_GROUP)) * g, v_first


def _dilated_branch(q, k, v, window, dilation):
    bsz, slen, H, Dh = q.shape
    L = slen // dilation
    blk = window // dilation
    nb = -(-L // blk)
    Lp = nb * blk

    def to_sub(t):
        t = t.reshape(bsz, L, dilation, H, Dh).transpose(0, 2, 1, 3, 4)
        t = t.reshape(bsz * dilation, L, H, Dh)
        return jnp.pad(t, ((0, 0), (0, Lp - L), (0, 0), (0, 0)))

    qs, ks, vs = to_sub(q), to_sub(k), to_sub(v)
    qb = qs.reshape(-1, nb, blk, H, Dh)

    def band(t):
        tp = jnp.pad(t, ((0, 0), (blk, 0), (0, 0), (0, 0))).reshape(-1, nb + 1, blk, H, Dh)
        return jnp.concatenate([tp[:, :-1], tp[:, 1:]], axis=2)

    kb, vb = band(ks), band(vs)
    i = jnp.arange(blk)[:, None]
    j = jnp.arange(2 * blk)[None, :]
    dist = blk + i - j
    n = jnp.arange(nb)[:, None, None]
    valid = (dist >= 0) & (dist <= blk) & ((n > 0) | (j >= blk))
    slopes = jnp.asarray(ALIBI_SLOPES, jnp.float32)
    bias = -slopes[:, None, None] * (dist * dilation).astype(jnp.float32)
    s = jnp.einsum('znqhd,znkhd->znhqk', qb, kb) * (Dh ** -0.5) + bias[None, None]
    s = jnp.where(valid[None, :, None], s, -jnp.inf)
    m = jnp.max(s, -1, keepdims=True)
    p = jnp.exp(s - m)
    l = jnp.sum(p, -1, keepdims=True)
    o = jnp.einsum('znhqk,znkhd->znqhd', p, vb) / jnp.transpose(l[..., 0], (0, 1, 3, 2))[..., None]
    lse = jnp.transpose((m + jnp.log(l))[..., 0], (0, 1, 3, 2))

    def from_sub(t):
        t = t.reshape(bsz * dilation, Lp, *t.shape[3:])[:, :L]
        t = t.reshape(bsz, dilation, L, *t.shape[2:])
        t = jnp.swapaxes(t, 1, 2)
        return t.reshape(bsz, slen, *t.shape[3:])

    return from_sub(o), from_sub(lse)


def dilated_attention(q, k, v):
    outs, lses = [], []
    for window, dilation in DILATED_BRANCHES:
        o, lse = _dilated_branch(q, k, v, window, dilation)
        outs.append(o)
        lses.append(lse)
    wts = jax.nn.softmax(jnp.stack(lses), axis=0)
    return jnp.sum(jnp.stack(outs) * wts[..., None], axis=0)


def _causal_depthwise_conv(x, w, b):
    y = lax.conv_general_dilated(x, w[:, None, :], window_strides=(1,),
                                 padding=[(w.shape[0] - 1, 0)],
                                 dimension_numbers=('NWC', 'WIO', 'NWC'),
                                 feature_group_count=x.shape[-1])
    return y + b


def _segsum_exp(a):
    cs = jnp.cumsum(a, -1)
    diff = cs[..., :, None] - cs[..., None, :]
    mask = jnp.tril(jnp.ones((a.shape[-1], a.shape[-1]), bool))
    return jnp.where(mask, jnp.exp(jnp.where(mask, diff, 0.0)), 0.0)


def ssd_mixer(feat, conv_w, conv_b, dt_bias, A_log, D, norm_w):
    bsz, slen, _ = feat.shape
    H, P, G, N = N_HEADS_GROUP, HEAD_DIM, SSD_GROUPS, SSD_STATE
    z, xbc, dt = _split(feat, (D_GROUP, SSD_XBC, H))
    xbc = jax.nn.silu(_causal_depthwise_conv(xbc, conv_w.astype(jnp.float32), conv_b))
    xs, Bm, Cm = _split(xbc, (D_GROUP, G * N, G * N))
    dt = jax.nn.softplus(dt + dt_bias)
    A = -jnp.exp(A_log.astype(jnp.float32))
    Lc = SSD_CHUNK
    nc = slen // Lc
    hpg = H // G
    x = xs.reshape(bsz, nc, Lc, H, P)
    Bh = jnp.repeat(Bm.reshape(bsz, nc, Lc, G, N), hpg, axis=3)
    Ch = jnp.repeat(Cm.reshape(bsz, nc, Lc, G, N), hpg, axis=3)
    dtc = dt.reshape(bsz, nc, Lc, H)
    dA = jnp.transpose(dtc * A, (0, 3, 1, 2))
    cs = jnp.cumsum(dA, -1)
    xdt = x * dtc[..., None]
    scores = jnp.einsum('bclhn,bcshn->bhcls', Ch, Bh) * _segsum_exp(dA)
    y_diag = jnp.einsum('bhcls,bcshp->bclhp', scores, xdt)
    decay_states = jnp.exp(cs[..., -1:] - cs)
    chunk_states = jnp.einsum('bclhn,bhcl,bclhp->cbhpn', Bh, decay_states, xdt)
    chunk_decay = jnp.transpose(jnp.exp(cs[..., -1]), (2, 0, 1))

    def step(s, inp):
        st, dec = inp
        return s * dec[..., None, None] + st, s

    _, prev = lax.scan(step, jnp.zeros((bsz, H, P, N), jnp.float32), (chunk_states, chunk_decay))
    y_off = jnp.einsum('bclhn,cbhpn,bhcl->bclhp', Ch, prev, jnp.exp(cs))
    y = (y_diag + y_off).reshape(bsz, slen, H, P) + x.reshape(bsz, slen, H, P) * D[:, None]
    y = y.reshape(bsz, slen, D_GROUP) * jax.nn.silu(z)
    y = _rms(y.reshape(bsz, slen, G, D_GROUP // G)).reshape(bsz, slen, D_GROUP)
    return y * norm_w


def hgrn2_mixer(feat, lower_bound, norm_w):
    bsz, slen, _ = feat.shape
    H, K, V = N_HEADS_GROUP, HEAD_DIM, HEAD_DIM
    q, f, i, g = _split(feat, (D_GROUP,) * 4)
    forget = lower_bound + (1.0 - lower_bound) * jax.nn.sigmoid(f)
    log_f = jnp.log(forget)
    k = 1.0 - forget
    q = jax.nn.silu(q)
    C = HGRN_CHUNK
    nc = slen // C
    q, k, log_f = (t.reshape(bsz, nc, C, H, K) for t in (q, k, log_f))
    v = i.reshape(bsz, nc, C, H, V)
    b = jnp.cumsum(log_f, axis=2)
    diff = b[:, :, :, None] - b[:, :, None, :]
    causal = jnp.tril(jnp.ones((C, C), bool))[None, None, :, :, None, None]
    dec = jnp.exp(jnp.where(causal, diff, -jnp.inf))
    att = jnp.einsum('bnthk,bnshk,bntshk->bnhts', q, k, dec)
    o_intra = jnp.einsum('bnhts,bnshv->bnthv', att, v)
    kdec = k * jnp.exp(b[:, :, -1:] - b)
    U = jnp.einsum('bnshk,bnshv->nbhkv', kdec, v)
    tot = jnp.transpose(jnp.exp(b[:, :, -1]), (1, 0, 2, 3))

    def step(s, inp):
        u, d = inp
        return s * d[..., None] + u, s

    _, prev = lax.scan(step, jnp.zeros((bsz, H, K, V), jnp.float32), (U, tot))
    o_inter = jnp.einsum('bnthk,nbhkv->bnthv', q * jnp.exp(b), prev)
    o = _rms((o_intra + o_inter).reshape(bsz, slen, H, V)).reshape(bsz, slen, D_GROUP)
    return o * norm_w * jax.nn.silu(g)


def setup_inputs(seed: int = 0) -> dict:
    key = jax.random.key(seed)
    ks = iter(jax.random.split(key, 40))
    nrm = lambda shape, scale: jax.random.normal(next(ks), shape, jnp.float32) * scale
    uni = lambda shape, lo, hi: jax.random.uniform(next(ks), shape, jnp.float32, lo, hi)
    L1 = DEPTH - 1
    H = N_HEADS_GROUP
    dt0 = jnp.exp(uni((DEPTH, H), math.log(1e-3), math.log(1e-1)))
    return {
        "x": nrm((BATCH, SEQ, D_MODEL), 1.0),
        "lower_bounds": nrm((DEPTH, D_GROUP), 0.5),
        "w_in": nrm((DEPTH, D_MODEL, IN_COLS), D_MODEL ** -0.5),
        "w_in_vres": nrm((L1, D_MODEL, RWKV_MV_LORA), D_MODEL ** -0.5),
        "mu_shift": uni((DEPTH, RWKV_COLS), 0.0, 1.0),
        "mu_vres": uni((L1, RWKV_MV_LORA), 0.0, 1.0),
        "rwkv_w0": uni((DEPTH, D_GROUP), -6.0, 1.0),
        "rwkv_w2": nrm((DEPTH, RWKV_DECAY_LORA, D_GROUP), RWKV_DECAY_LORA ** -0.5),
        "rwkv_a0": nrm((DEPTH, D_GROUP), 0.1),
        "rwkv_a2": nrm((DEPTH, RWKV_AAA_LORA, D_GROUP), RWKV_AAA_LORA ** -0.5),
        "rwkv_g2": nrm((DEPTH, RWKV_GATE_LORA, D_GROUP), RWKV_GATE_LORA ** -0.5),
        "rwkv_k_k": 0.85 + nrm((DEPTH, D_GROUP), 0.05),
        "rwkv_k_a": 1.0 + nrm((DEPTH, D_GROUP), 0.05),
        "rwkv_r_k": nrm((DEPTH, H, HEAD_DIM), 0.1),
        "rwkv_lnx_w": 1.0 + nrm((DEPTH, D_GROUP), 0.05),
        "rwkv_lnx_b": nrm((DEPTH, D_GROUP), 0.01),
        "rwkv_v0": nrm((L1, D_GROUP), 0.5),
        "rwkv_v2": nrm((L1, RWKV_MV_LORA, D_GROUP), RWKV_MV_LORA ** -0.5),
        "ssd_conv_w": nrm((DEPTH, SSD_CONV, SSD_XBC), SSD_CONV ** -0.5),
        "ssd_conv_b": nrm((DEPTH, SSD_XBC), 0.01),
        "ssd_dt_bias": dt0 + jnp.log(-jnp.expm1(-dt0)),
        "ssd_A_log": jnp.log(uni((DEPTH, H), 1.0, 16.0)),
        "ssd_D": 1.0 + nrm((DEPTH, H), 0.05),
        "ssd_norm_w": 1.0 + nrm((DEPTH, D_GROUP), 0.05),
        "hgrn_norm_w": 1.0 + nrm((DEPTH, D_GROUP), 0.05),
        "w_out": nrm((DEPTH, D_MIX, D_MODEL), BETA * D_MIX ** -0.5),
        "ln1_w": 1.0 + nrm((DEPTH, D_MODEL), 0.05),
        "ln1_b": nrm((DEPTH, D_MODEL), 0.01),
        "w_up": nrm((DEPTH, D_MODEL, D_FF), D_MODEL ** -0.5),
        "w_down": nrm((DEPTH, D_FF, D_MODEL), BETA * D_FF ** -0.5),
        "ln2_w": 1.0 + nrm((DEPTH, D_MODEL), 0.05),
        "ln2_b": nrm((DEPTH, D_MODEL), 0.01),
    }


def reference(x, lower_bounds, w_in, w_in_vres, mu_shift, mu_vres, rwkv_w0, rwkv_w2,
              rwkv_a0, rwkv_a2, rwkv_g2, rwkv_k_k, rwkv_k_a, rwkv_r_k, rwkv_lnx_w,
              rwkv_lnx_b, rwkv_v0, rwkv_v2, ssd_conv_w, ssd_conv_b, ssd_dt_bias, ssd_A_log,
              ssd_D, ssd_norm_w, hgrn_norm_w, w_out, ln1_w, ln1_b, w_up, w_down, ln2_w, ln2_b):
    bsz, slen, _ = x.shape
    lb = jax.nn.softmax(lower_bounds.astype(jnp.float32), axis=0)
    lb = jnp.cumsum(lb, axis=0) - lb[0]
    v_first = None
    for l in range(DEPTH):
        if l == 0:
            proj = x @ w_in[l]
        else:
            proj = x @ jnp.concatenate([w_in[l], w_in_vres[l - 1]], axis=1)
        proj = proj.astype(jnp.float32)
        parts = _split(proj, (RWKV_COLS, ATTN_COLS, SSD_COLS, HGRN_COLS))
        f_rwkv, f_attn, f_ssd, f_hgrn = parts
        f_rwkv = _token_shift_lerp(f_rwkv, mu_shift[l])
        if l == 0:
            y_a, v_first = rwkv7_time_mix(f_rwkv, rwkv_w0[l], rwkv_w2[l], rwkv_a0[l], rwkv_a2[l],
                                          rwkv_g2[l], rwkv_k_k[l], rwkv_k_a[l], rwkv_r_k[l],
                                          rwkv_lnx_w[l], rwkv_lnx_b[l], None, None, None, None)
        else:
            f_vres = _token_shift_lerp(proj[..., IN_COLS:], mu_vres[l - 1])
            y_a, v_first = rwkv7_time_mix(f_rwkv, rwkv_w0[l], rwkv_w2[l], rwkv_a0[l], rwkv_a2[l],
                                          rwkv_g2[l], rwkv_k_k[l], rwkv_k_a[l], rwkv_r_k[l],
                                          rwkv_lnx_w[l], rwkv_lnx_b[l], v_first, f_vres,
                                          rwkv_v0[l - 1], rwkv_v2[l - 1])
        q, k, v = (t.reshape(bsz, slen, N_HEADS_GROUP, HEAD_DIM)
                   for t in _split(f_attn, (D_GROUP,) * 3))
        y_b = dilated_attention(q, k, v).reshape(bsz, slen, D_GROUP)
        y_c = ssd_mixer(f_ssd, ssd_conv_w[l], ssd_conv_b[l], ssd_dt_bias[l], ssd_A_log[l],
                        ssd_D[l], ssd_norm_w[l])
        y_d = hgrn2_mixer(f_hgrn, lb[l], hgrn_norm_w[l])
        mix = jnp.concatenate([y_a, y_b, y_c, y_d], axis=-1).astype(x.dtype) @ w_out[l]
        x = _layer_norm(ALPHA * x + mix, ln1_w[l], ln1_b[l])
        h = jnp.square(jax.nn.relu(x @ w_up[l]))
        x = _layer_norm(ALPHA * x + h @ w_down[l], ln2_w[l], ln2_b[l])
    return x
```

```python
import numpy as np
import concourse.bass as bass
import concourse.mybir as mybir
from concourse.bass_utils import run_bass_kernel_spmd

F32 = mybir.dt.float32
BF16 = mybir.dt.bfloat16
AF = mybir.ActivationFunctionType
ALU = mybir.AluOpType
AX = mybir.AxisListType

T = 2048
D = 1024
NT = 16
DEPTH = 2
ALPHA = (2.0 * DEPTH) ** 0.25
LN_EPS = 1e-5
RMS_EPS = 1e-5
IN_COLS = 3716
C_RWKV, C_ATT, C_SSD, C_HGRN = 0, 896, 1664, 2692
SLOPES = [2.0 ** (-8.0 * (h + 1) / 4) for h in range(4)]
NEG = -30000.0
STRICT = False


class Sched:
    LAT = 700.0

    def __init__(self, nc):
        self.nc = nc
        self.eng = {"pe": nc.tensor, "dve": nc.vector, "act": nc.scalar,
                    "pool": nc.gpsimd, "sp": nc.sync}
        self.ops = []
        self.info = []
        self.lastw = {}
        self.reads = {}
        self.fences = []
        self.reorder = True
        self.strict = STRICT

    @staticmethod
    def _key(a):
        if isinstance(a, (str, tuple)):
            return a
        return a.name

    def op(self, engine, fn, reads=(), writes=(), dma=False, cost=100.0):
        idx = len(self.ops)
        sem = set()
        order = set()
        rk = [self._key(a) for a in reads]
        wk = [self._key(a) for a in writes]
        for k in rk:
            if k in self.lastw:
                sem.add(self.lastw[k])
            if isinstance(k, str) and k.startswith("ps_"):
                for (e, i, d) in self.reads.get(k, ()):
                    if e != engine:
                        sem.add(i)
        for k in wk:
            if k in self.lastw:
                j = self.lastw[k]
                if dma or self.ops[j][3] or self.ops[j][0] != engine or (self.strict and engine != "pe"):
                    sem.add(j)
                else:
                    order.add(j)
            for (e, i, d) in self.reads.get(k, ()):
                if dma or d or e != engine or (self.strict and engine != "pe"):
                    sem.add(i)
                else:
                    order.add(i)
        sem.discard(idx)
        order.discard(idx)
        order -= sem
        self.info.append((engine, "dma" if dma else "", rk, wk))
        self.ops.append((engine, fn, sem, dma, order, float(cost)))
        for k in wk:
            self.lastw[k] = idx
            self.reads[k] = []
        for k in rk:
            self.reads.setdefault(k, []).append((engine, idx, dma))
        return idx

    def barrier(self):
        self.fences.append(len(self.ops))
        self.lastw = {}
        self.reads = {}

    def _schedule(self, seg):
        if not self.reorder or len(seg) < 3:
            return list(seg)
        ops = self.ops
        segset = set(seg)
        preds = {}
        succs = {i: [] for i in seg}
        indeg = {}
        for i in seg:
            p = [d for d in (ops[i][2] | ops[i][4]) if d in segset]
            preds[i] = p
            indeg[i] = len(p)
            for d in p:
                succs[d].append(i)
        finish = {}
        free = {e: 0.0 for e in self.eng}
        ready = {e: [] for e in self.eng}
        for i in seg:
            if indeg[i] == 0:
                ready[ops[i][0]].append((0.0, i))
        order = []
        nleft = len(seg)
        while nleft:
            best = None
            for e, lst in ready.items():
                if not lst:
                    continue
                f = free[e]
                cand = None
                for (dr, i) in lst:
                    st = dr if dr > f else f
                    key = (st, i)
                    if cand is None or key < cand:
                        cand = key
                if best is None or cand < best[0]:
                    best = (cand, e)
            (st, i), e = best
            ready[e] = [x for x in ready[e] if x[1] != i]
            eng, fn, sem, dma, od, cost = ops[i]
            if dma:
                issue = 1500.0 if e == "pool" else 150.0
                free[e] = st + issue
                finish[i] = st + issue + cost
            else:
                free[e] = st + cost
                finish[i] = st + cost
            order.append(i)
            nleft -= 1
            for s_ in succs[i]:
                indeg[s_] -= 1
                if indeg[s_] == 0:
                    dr = 0.0
                    for p in preds[s_]:
                        t = finish[p] + (self.LAT if (ops[p][0] != ops[s_][0] or p in ops[s_][2]) else 0.0)
                        if t > dr:
                            dr = t
                    ready[ops[s_][0]].append((dr, s_))
        mk = max(list(finish.values()) + [0.0])
        self.est_ns = getattr(self, "est_ns", 0.0) + mk
        busy = {e: 0.0 for e in self.eng}
        for i in seg:
            if not ops[i][3]:
                busy[ops[i][0]] += ops[i][5]
        self.seglog = getattr(self, "seglog", [])
        self.seglog.append((len(seg), round(mk / 1e3, 1), {e: round(b / 1e3, 1) for e, b in busy.items()}))
        return order

    def emit(self, limit=None):
        nc = self.nc
        ops = self.ops
        n = len(ops)
        elimit = limit
        emitted = 0
        bounds = [0] + [f for f in self.fences if f < n] + [n]
        segs = [list(range(bounds[i], bounds[i + 1])) for i in range(len(bounds) - 1)]
        sched = [self._schedule(sg) for sg in segs]
        need = [False] * len(ops)
        for i in range(n):
            for d in ops[i][2]:
                need[d] = True
        for od in sched:
            last = {}
            for i in od:
                if not ops[i][3]:
                    last[ops[i][0]] = i
            for i in last.values():
                need[i] = True
        NQ = {"sp": 16, "pool": 8, "act": 4}

        def run(dry, need):
            csem = dsem = None
            if not dry:
                csem = {e: nc.alloc_semaphore(f"sem_{e}") for e in self.eng}
                dsem = {q: [nc.alloc_semaphore(f"dsem_{q}_{i}") for i in range(k)] for q, k in NQ.items()}
            dcount = {q: [0] * k for q, k in NQ.items()}
            ndma = {q: 0 for q in NQ}
            sig = [None] * len(ops)
            sigop = {}
            used = set()
            ccount = {e: 0 for e in self.eng}
            seen = {e: {} for e in self.eng}
            nw = [0]
            emitted = 0

            def wait(e, key, val):
                if val <= 0 or seen[e].get(key, 0) >= val:
                    return
                seen[e][key] = val
                nw[0] += 1
                if (key, val) in sigop:
                    used.add(sigop[(key, val)])
                if not dry:
                    semh = dsem[key[1]][key[2]] if key[0] == "d" else csem[key[1]]
                    self.eng[e].wait_ge(semh, val)

            for si_, od in enumerate(sched):
                if si_ > 0:
                    for e in self.eng:
                        for e2 in self.eng:
                            wait(e, ("c", e2), ccount[e2])
                        for q in NQ:
                            for k in range(NQ[q]):
                                wait(e, ("d", q, k), dcount[q][k])
                for i in od:
                    if elimit is not None and emitted >= elimit:
                        break
                    emitted += 1
                    e, fn, deps, dma, _, _ = ops[i]
                    wants = {}
                    for d in deps:
                        s_ = sig[d]
                        if s_ is None:
                            continue
                        key, val = s_
                        if wants.get(key, 0) < val:
                            wants[key] = val
                    if dma:
                        si = ndma[e] % NQ[e]
                        if dcount[e][si] > 0:
                            key = ("d", e, si)
                            if wants.get(key, 0) < dcount[e][si]:
                                wants[key] = dcount[e][si]
                    for key, val in wants.items():
                        wait(e, key, val)
                    inst = None if dry else fn()
                    if dma:
                        si = ndma[e] % NQ[e]
                        ndma[e] += 1
                        dcount[e][si] += 16
                        if not dry:
                            inst.then_inc(dsem[e][si], 16)
                        sig[i] = (("d", e, si), dcount[e][si])
                    elif need[i]:
                        ccount[e] += 1
                        if not dry:
                            inst.then_inc(csem[e], 1)
                        sig[i] = (("c", e), ccount[e])
                        sigop[sig[i]] = i
            if not dry:
                for q in NQ:
                    for si in range(NQ[q]):
                        if dcount[q][si] > 0:
                            nc.sync.wait_ge(dsem[q][si], dcount[q][si])
            return used, nw[0], ndma

        used, _, _ = run(True, need)
        need2 = [False] * len(ops)
        for i in used:
            need2[i] = True
        used2, nwaits, ndma = run(False, need2)
        self.nsig = sum(need2)
        self.stats = dict(n=n, nwaits=nwaits, nsig=self.nsig, ndma=ndma, est_us=getattr(self, "est_ns", 0.0) / 1e3)


def _fs(ap):
    n = 1
    for d in ap.shape[1:]:
        n *= d
    return n


class KB:
    def __init__(self, nc):
        self.nc = nc
        self.S = Sched(nc)
        self.uid = 0
        self.stack = None
        self.marks = []

    def mark(self, name):
        self.marks.append((name, len(self.S.ops)))

    def sb(self, name, shape, dt=F32):
        if self.stack is None:
            return self.nc.alloc_sbuf_tensor(name, list(shape), dt)
        self.uid += 1
        return self.stack.enter_context(self.nc.sbuf_tensor(f"{name}_u{self.uid}", list(shape), dt))

    def phase_begin(self):
        import contextlib
        assert self.stack is None
        self.stack = contextlib.ExitStack()

    def phase_end(self):
        self.S.barrier()
        self.stack.close()
        self.stack = None

    def ps(self, name, shape, dt=F32):
        return self.nc.alloc_psum_tensor(name, list(shape), dt)

    def mm(self, out, lhsT, rhs, start=True, stop=True, rk=None, wk=None):
        nc = self.nc
        cost = max(64, _fs(rhs)) / 2.4 * (4 if lhsT.dtype == F32 else 1) + 45
        self.S.op("pe", lambda: nc.tensor.matmul(out, lhsT=lhsT, rhs=rhs, start=start, stop=stop),
                  reads=rk if rk is not None else [lhsT, rhs], writes=wk if wk is not None else [out], cost=cost)

    def tr(self, out, in_, ident, rk=None, wk=None):
        nc = self.nc
        self.S.op("pe", lambda: nc.tensor.transpose(out, in_, ident),
                  reads=rk if rk is not None else [in_, ident], writes=wk if wk is not None else [out], cost=110)

    def act(self, out, in_, func, bias=None, scale=None, accum=None, rk=None, wk=None):
        nc = self.nc
        kw = {}
        reads = [in_]
        if bias is not None:
            kw["bias"] = bias
            if not isinstance(bias, (int, float)):
                reads.append(bias)
        if scale is not None:
            kw["scale"] = scale
            if not isinstance(scale, (int, float)):
                reads.append(scale)
        writes = [out]
        if accum is not None:
            kw["accum_out"] = accum
            writes.append(accum)
        self.S.op("act", lambda: nc.scalar.activation(out=out, in_=in_, func=func, **kw),
                  reads=rk if rk is not None else reads, writes=wk if wk is not None else writes,
                  cost=(224 + _fs(out)) / 1.2)

    def tt(self, out, in0, in1, op, eng="dve", rk=None, wk=None):
        E = self.S.eng[eng]
        cost = (170 + _fs(out)) / 0.96 if eng == "dve" else (350 + 2.0 * _fs(out)) / 1.2
        self.S.op(eng, lambda: E.tensor_tensor(out=out, in0=in0, in1=in1, op=op),
                  reads=rk if rk is not None else [in0, in1], writes=wk if wk is not None else [out], cost=cost)

    def ts(self, out, in0, s1, op0, s2=None, op1=None, eng="dve", rk=None, wk=None):
        E = self.S.eng[eng]
        reads = [in0] + [s for s in (s1, s2) if s is not None and not isinstance(s, (int, float))]
        if op1 is None:
            fn = lambda: E.tensor_scalar(out=out, in0=in0, scalar1=s1, scalar2=None, op0=op0)
        else:
            fn = lambda: E.tensor_scalar(out=out, in0=in0, scalar1=s1, scalar2=s2, op0=op0, op1=op1)
        cost = (170 + 0.7 * _fs(out)) / 0.96 if eng == "dve" else (350 + 2.0 * _fs(out)) / 1.2
        self.S.op(eng, fn, reads=rk if rk is not None else reads, writes=wk if wk is not None else [out], cost=cost)

    def stt(self, out, in0, scalar, in1, op0, op1, rk=None, wk=None):
        nc = self.nc
        reads = [in0, in1] + ([] if isinstance(scalar, (int, float)) else [scalar])
        self.S.op("dve", lambda: nc.vector.scalar_tensor_tensor(out=out, in0=in0, scalar=scalar, in1=in1, op0=op0, op1=op1),
                  reads=rk if rk is not None else reads, writes=wk if wk is not None else [out],
                  cost=(170 + _fs(out)) / 0.96)

    def cp(self, out, in_, eng="dve", rk=None, wk=None):
        nc = self.nc
        if eng == "act":
            fn = lambda: nc.scalar.activation(out=out, in_=in_, func=AF.Copy)
        else:
            E = self.S.eng[eng]
            fn = lambda: E.tensor_copy(out=out, in_=in_)
        if eng == "act":
            cost = (224 + _fs(out)) / 1.2
        elif eng == "dve":
            cost = (170 + 0.7 * _fs(out)) / 0.96
        else:
            cost = (350 + 2.0 * _fs(out)) / 1.2
        self.S.op(eng, fn, reads=rk if rk is not None else [in_], writes=wk if wk is not None else [out], cost=cost)

    def recip(self, out, in_, rk=None, wk=None):
        nc = self.nc
        self.S.op("dve", lambda: nc.vector.reciprocal(out=out, in_=in_),
                  reads=rk if rk is not None else [in_], writes=wk if wk is not None else [out],
                  cost=(62 + 8 * _fs(out)) / 0.96)

    def memset(self, ap, val, eng="dve"):
        E = self.S.eng[eng]
        self.S.op(eng, lambda: E.memset(ap, val), writes=[ap], cost=(62 + _fs(ap)) / 0.96)

    def dma(self, out, in_, eng="sp", rk=(), wk=()):
        E = self.S.eng[eng]
        nbytes = out.shape[0] * _fs(out) * 4
        self.S.op(eng, lambda: E.dma_start(out=out, in_=in_), reads=list(rk), writes=list(wk), dma=True,
                  cost=2000 + nbytes / 120.0)


def build_program(debug=False, nlayers=DEPTH, parts="abcd", limit=None):
    nc = bass.Bass("TRN2", target_bir_lowering=False)
    K = KB(nc)

    def din(name, shape):
        return nc.dram_tensor(name, list(shape), F32, kind="ExternalInput").ap()

    x_d = din("x", [T, D])
    w_in_d = din("w_in", [DEPTH, D, IN_COLS])
    w_out_d = din("w_out", [DEPTH, D, D])
    w_up_d = din("w_up", [DEPTH, D, 4 * D])
    w_down_d = din("w_down", [DEPTH, 4 * D, D])
    ln1w_d = din("ln1_w", [DEPTH, D]); ln1b_d = din("ln1_b", [DEPTH, D])
    ln2w_d = din("ln2_w", [DEPTH, D]); ln2b_d = din("ln2_b", [DEPTH, D])
    ident_d = din("c_ident", [128, 128])
    amask_d = din("c_amask", [128, 5 * 4 * 128])
    triT_d = din("c_triT", [128, 128]); maskneg_d = din("c_maskneg", [128, 128])
    ssdconv_d = din("c_ssdconv", [DEPTH, 128, 30]); ssdD_d = din("c_ssdD", [DEPTH, 256])
    msl_d = din("c_msl", [128, 128]); msu_d = din("c_msu", [128, 128])
    shift_d = din("c_shift", [128, 128]); carry_d = din("c_carry", [128, 128])
    rmucol_d = din("c_rmucol", [DEPTH, 128, 1]); vmucol_d = din("c_vmucol", [128, 1])
    mush_d = din("mu_shift", [DEPTH, 896])
    rw0_d = din("rwkv_w0", [DEPTH, 256]); ra0_d = din("rwkv_a0", [DEPTH, 256]); rkk_d = din("rwkv_k_k", [DEPTH, 256])
    rka_d = din("rwkv_k_a", [DEPTH, 256]); rlw_d = din("rwkv_lnx_w", [DEPTH, 256]); rlb_d = din("rwkv_lnx_b", [DEPTH, 256])
    rrk_d = din("c_rrk", [DEPTH, 256]); rw2_d = din("rwkv_w2", [DEPTH, 32, 256]); ra2_d = din("rwkv_a2", [DEPTH, 32, 256])
    rg2_d = din("rwkv_g2", [DEPTH, 64, 256]); rv0_d = din("rwkv_v0", [1, 256]); rv2_d = din("rwkv_v2", [1, 32, 256])
    wvres_d = din("w_in_vres", [1, D, 32])
    vfirst_d = nc.dram_tensor("vfirst", [T, 256], F32, kind="Internal").ap()
    lscr_d = nc.dram_tensor("lscr", [128, T], BF16, kind="Internal").ap()
    vrscr_d = nc.dram_tensor("vrscr", [32, T], BF16, kind="Internal").ap()
    tri16_d = din("c_tri16", [128, 128]); blk16_d = din("c_blk16", [128, 128]); bm_d = din("c_bm", [128, 8])
    bones64_d = din("c_bones64", [128, 128]); hgnw_d = din("c_hgnw", [DEPTH, 128, 2]); lowb_d = din("lower_bounds", [DEPTH, 256])
    dtb_d = din("ssd_dt_bias", [DEPTH, 4]); alog_d = din("ssd_A_log", [DEPTH, 4]); ssdnw_d = din("ssd_norm_w", [DEPTH, 256])
    out_d = nc.dram_tensor("out", [T, D], F32, kind="ExternalOutput").ap()
    vscr_d = nc.dram_tensor("vscr", [T, 256], BF16, kind="Internal").ap()
    dbg_d = None
    if debug:
        dbg_d = nc.dram_tensor("dbg", [4, 128, 2 * T], BF16, kind="ExternalOutput").ap()

    xres = [K.sb(f"xres{t}", [128, D]) for t in range(NT)]
    xT = K.sb("xT", [128, 8, T], BF16)
    ident = K.sb("ident", [128, 128])
    ones_bf = K.sb("ones_bf", [128, 64], BF16)
    wbuf = [K.sb(f"wbuf{i}", [128, 8192], BF16) for i in range(2)]
    wstate = {"i": 0}
    psb = [K.ps(f"ps_{i}", [128, 512]) for i in range(7)]
    pbf = K.ps("ps_bf", [128, 1024], BF16)
    ident_bf = K.sb("ident_bf", [128, 128], BF16)
    triT = K.sb("triT", [128, 128]); ones_f = K.sb("ones_f", [128, 128]); maskneg = K.sb("maskneg", [128, 128])

    def xk(t0, t1):
        return [("xT", b) for b in range(t0 // 512, (t1 - 1) // 512 + 1)]
    XALL = [("xT", b) for b in range(4)]

    pre = {}

    def prefetch(tag, src_ap, kc, ncols):
        pre[tag] = wload(src_ap, kc, ncols)

    def getw(tag, src_ap, kc, ncols):
        if tag in pre:
            return pre.pop(tag)
        return wload(src_ap, kc, ncols)

    def wsmall(name, src_ap, kc, ncols):
        t_ = K.sb(name, [128, kc * ncols], BF16)
        view = t_[:, :].rearrange("p (k c) -> p k c", k=kc)
        K.dma(view, src_ap.rearrange("(k p) c -> p k c", p=128), eng="pool", rk=[t_], wk=[t_])
        return t_, view

    def wload(src_ap, kc, ncols, off=0, new=True):
        if new:
            wstate["i"] += 1
        wb = wbuf[wstate["i"] % 2]
        view = wb[:, off:off + kc * ncols].rearrange("p (k c) -> p k c", k=kc)
        K.dma(view, src_ap.rearrange("(k p) c -> p k c", p=128), eng="pool", rk=[wb], wk=[wb])
        return wb, view

    K.dma(ident[:], ident_d, wk=[ident])
    K.cp(ident_bf[:], ident[:])
    K.dma(triT[:], triT_d, wk=[triT])
    K.dma(maskneg[:], maskneg_d, wk=[maskneg])
    K.memset(ones_f[:], 1.0)
    K.memset(ones_bf[:], 1.0)
    mhalf = K.sb("mhalf", [128, 4])
    K.memset(mhalf[:], -0.5)

    for t in range(NT):
        K.dma(xres[t][:], x_d[t * 128:(t + 1) * 128, :], wk=[xres[t]])

    def build_xT(t):
        for half in range(2):
            pb = psb[5 + half]
            for j in range(4):
                kc = half * 4 + j
                K.tr(pb[:, j * 128:(j + 1) * 128], xres[t][:, kc * 128:(kc + 1) * 128], ident[:])
            K.cp(xT[:, half * 4:(half + 1) * 4, t * 128:(t + 1) * 128],
                 pb[:].rearrange("p (j c) -> p j c", j=4), eng=("act" if half else "dve"),
                 wk=xk(t * 128, (t + 1) * 128))

    for t in range(NT):
        build_xT(t)

    def layer_norm(t, w_t, b_t):
        xt = xres[t]
        stats = K.sb(f"lnst{K.uid}", [128, 12]); mv = K.sb(f"lnmv{K.uid}", [128, 2]); rs = K.sb(f"lnrs{K.uid}", [128, 1])
        K.uid += 1
        nc_ = nc
        K.S.op("dve", lambda: nc_.vector.bn_stats(out=stats[:, 0:6], in_=xt[:, 0:512]), reads=[xt], writes=[stats])
        K.S.op("dve", lambda: nc_.vector.bn_stats(out=stats[:, 6:12], in_=xt[:, 512:1024]), reads=[xt], writes=[stats])
        K.S.op("dve", lambda: nc_.vector.bn_aggr(out=mv[:], in_=stats[:]), reads=[stats], writes=[mv])
        K.ts(rs[:], mv[:, 1:2], LN_EPS, ALU.add)
        K.tt(rs[:], rs[:], mhalf[:, 0:1], ALU.pow, eng="pool")
        K.stt(mv[:, 1:2], mv[:, 0:1], -1.0, rs[:, 0:1], ALU.mult, ALU.mult)
        K.act(xt[:], xt[:], AF.Identity, bias=mv[:, 1:2], scale=rs[:, 0:1])
        K.tt(xt[:], xt[:], w_t[:], ALU.mult)
        K.tt(xt[:], xt[:], b_t[:], ALU.add, eng="pool")

    yT = K.sb("yT", [128, 2, T], BF16)
    wo_t = K.sb("wo_t", [128, 2 * D], BF16)

    def out_proj(l, m):
        wb = wo_t
        wv = wo_t[:, :].rearrange("p (k c) -> p k c", k=2)
        K.dma(wv, w_out_d[l, m * 256:(m + 1) * 256, :].rearrange("(k p) c -> p k c", p=128), eng="pool", rk=[wo_t], wk=[wo_t])
        for t in range(NT):
            for nb in range(2):
                pb = psb[4 + (t * 2 + nb) % 2]
                for c in range(2):
                    K.mm(pb[:, :], lhsT=yT[:, c, t * 128:(t + 1) * 128], rhs=wv[:, c, nb * 512:(nb + 1) * 512],
                         start=(c == 0), stop=(c == 1), rk=[yT, wb])
                K.tt(xres[t][:, nb * 512:(nb + 1) * 512], pb[:, :], xres[t][:, nb * 512:(nb + 1) * 512], ALU.add)

    def ffn(l):
        hT = [K.sb(f"hT{i}", [128, 4, T], BF16) for i in range(2)]
        rtmp = [K.sb(f"rtmp{i}", [128, 512]) for i in range(2)]
        for t in range(NT):
            K.act(xres[t][:], xres[t][:], AF.Copy, scale=ALPHA)
        for j in range(8):
            if j == 0 and ("f0", l) in pre:
                (wub, wuv), (wdb, wdv) = pre.pop(("f0", l))
            else:
                wub, wuv = wload(w_up_d[l, :, j * 512:(j + 1) * 512], 8, 512)
                wdb, wdv = wload(w_down_d[l, j * 512:(j + 1) * 512, :], 4, D, off=4096, new=False)
            h = hT[j % 2]
            cnt = 0
            for m in range(4):
                for nb in range(4):
                    pb = psb[cnt % 2]
                    rt = rtmp[cnt % 2]
                    cnt += 1
                    for kc in range(8):
                        K.mm(pb[:, :], lhsT=wuv[:, kc, m * 128:(m + 1) * 128], rhs=xT[:, kc, nb * 512:(nb + 1) * 512],
                             start=(kc == 0), stop=(kc == 7), rk=[wub, ("xT", nb)])
                    K.act(rt[:], pb[:, :], AF.Relu)
                    K.tt(h[:, m, nb * 512:(nb + 1) * 512], rt[:], rt[:], ALU.mult, eng="pool", wk=[(h.name, nb)])
            for t in range(NT):
                for nb in range(2):
                    pb = psb[2 + (t * 2 + nb) % 2]
                    for c in range(4):
                        K.mm(pb[:, :], lhsT=h[:, c, t * 128:(t + 1) * 128], rhs=wdv[:, c, nb * 512:(nb + 1) * 512],
                             start=(c == 0), stop=(c == 3), rk=[(h.name, t // 4), wdb])
                    K.tt(xres[t][:, nb * 512:(nb + 1) * 512], pb[:, :], xres[t][:, nb * 512:(nb + 1) * 512], ALU.add)

    def attention(l):
        K.mark("attn_begin")
        K.phase_begin()
        qT = K.sb("qT", [128, 2, T], BF16); kT = K.sb("kT", [128, 2, T], BF16)
        amask = K.sb("amask", [128, 2, 512])
        accn = K.sb("accn", [128, 2, T]); accd = K.sb("accd", [128, 2, T], BF16)
        Vb = K.sb("Vb", [128, 16, 256], BF16)
        Vn = [K.sb(f"Vn{i}", [128, 256], BF16) for i in range(2)]
        Ebuf = [K.sb(f"Ebuf{i}", [128, 512]) for i in range(2)]
        Pbuf = [K.sb(f"Pbuf{i}", [128, 512], BF16) for i in range(4)]
        amask_v = amask_d.rearrange("p (m f) -> p m f", m=5)
        wb, wv = getw(("a", l), w_in_d[l, :, C_ATT:C_ATT + 768], 8, 768)
        if "c" in parts:
            prefetch(("s", l), w_in_d[l, :, C_SSD:C_SSD + 1024], 8, 1024)
        cnt = 0
        for which, dst, scale in ((0, qT, 0.125), (1, kT, 1.0)):
            for c in range(2):
                for nb in range(4):
                    pb = psb[cnt % 2]; cnt += 1
                    for kc in range(8):
                        K.mm(pb[:, :], lhsT=wv[:, kc, which * 256 + c * 128: which * 256 + (c + 1) * 128],
                             rhs=xT[:, kc, nb * 512:(nb + 1) * 512], start=(kc == 0), stop=(kc == 7),
                             rk=[wb, ("xT", nb)])
                    K.act(dst[:, c, nb * 512:(nb + 1) * 512], pb[:, :], AF.Copy, scale=scale)
        K.mark("attn_V")
        for t in range(NT):
            pb = psb[t % 2]
            for kc in range(8):
                K.mm(pb[:, 0:256], lhsT=xT[:, kc, t * 128:(t + 1) * 128], rhs=wv[:, kc, 512:768],
                     start=(kc == 0), stop=(kc == 7), rk=[wb, ("xT", t // 4)])
            K.cp(Vn[t % 2][:], pb[:, 0:256], eng="act")
            K.dma(vscr_d[t * 128:(t + 1) * 128, :], Vn[t % 2][:], rk=[Vn[t % 2]], wk=["vscr"])
        ecnt = 0; qcnt = 0
        opbanks = [psb[6], pbf[:].bitcast(F32), psb[0], psb[1]]
        for bi, d in enumerate((1, 4, 16)):
            K.mark(f"attn_branch{bi}")
            nblk = T // d // 128
            if bi < 2:
                K.dma(amask[:], amask_v[:, 2 * bi:2 * bi + 2, :], rk=[amask], wk=[amask])
            else:
                K.dma(amask[:, 1, :], amask_v[:, 4, :], rk=[amask], wk=[amask])

            def tsl(r, n, d=d):
                st = r + d * 128 * n
                return slice(st, st + d * 127 + 1, d)
            if d == 1:
                for q in range(4):
                    K.dma(Vb[:, q * 4:(q + 1) * 4, :], vscr_d.rearrange("(n j) f -> j n f", j=128)[:, q * 4:(q + 1) * 4, :],
                          rk=["vscr", Vb], wk=[Vb])
            elif d == 4:
                for r in range(4):
                    K.dma(Vb[:, r * 4:(r + 1) * 4, :], vscr_d.rearrange("(n j r) f -> r j n f", n=4, j=128, r=4)[r],
                          rk=["vscr", Vb], wk=[Vb])
            else:
                for q in range(4):
                    K.dma(Vb[:, q * 4:(q + 1) * 4, :], vscr_d.rearrange("(j r) f -> j r f", r=16)[:, q * 4:(q + 1) * 4, :],
                          rk=["vscr", Vb], wk=[Vb])
            for r in range(d):
                for n in range(nblk):
                    qsl = tsl(r, n)
                    kbl = []
                    if n > 0:
                        kbl.append((r * nblk + n - 1, tsl(r, n - 1), 0))
                    kbl.append((r * nblk + n, qsl, 1))
                    Ps = []
                    for ki, (vidx, ksl, mid) in enumerate(kbl):
                        spa = psb[2 + 2 * ki]; spb = psb[3 + 2 * ki]
                        for h in range(4):
                            po = (h % 2) * 64
                            sp = spa if h % 2 == 0 else spb
                            K.mm(sp[:, (h // 2) * 128:(h // 2 + 1) * 128], lhsT=kT[po:po + 64, h // 2, ksl],
                                 rhs=qT[po:po + 64, h // 2, qsl])
                        E = Ebuf[ecnt % 2]; P = Pbuf[ecnt % 4]; ecnt += 1
                        K.tt(E[:, 0:256], spa[:, 0:256], amask[:, mid, 0:256], ALU.add)
                        K.tt(E[:, 256:512], spb[:, 0:256], amask[:, mid, 256:512], ALU.add)
                        K.act(P[:], E[:], AF.Exp)
                        Ps.append((vidx, P))
                    op = opbanks[qcnt % 4]; qcnt += 1
                    for h in range(4):
                        po = (h % 2) * 64; hh = h // 2
                        pc = (h % 2) * 256 + hh * 128
                        for ki, (vidx, P) in enumerate(Ps):
                            K.mm(op[po:po + 64, hh * 128:(hh + 1) * 128], lhsT=Vb[:, vidx, h * 64:(h + 1) * 64],
                                 rhs=P[:, pc:pc + 128], start=(ki == 0), stop=(ki == len(Ps) - 1))
                        for ki, (vidx, P) in enumerate(Ps):
                            K.mm(op[po:po + 64, (2 + hh) * 128:(3 + hh) * 128], lhsT=ones_bf[:, 0:64],
                                 rhs=P[:, pc:pc + 128], start=(ki == 0), stop=(ki == len(Ps) - 1))
                    nv = op[:, 0:256].rearrange("p (a q) -> p a q", a=2)
                    dv = op[:, 256:512].rearrange("p (a q) -> p a q", a=2)
                    if bi == 0:
                        K.cp(accn[:, :, qsl], nv, eng="act")
                        K.cp(accd[:, :, qsl], dv, eng="dve")
                    else:
                        K.tt(accn[:, :, qsl], nv, accn[:, :, qsl], ALU.add)
                        K.tt(accd[:, :, qsl], dv, accd[:, :, qsl], ALU.add)
        K.mark("attn_final")
        chunks = [(hh, q4) for hh in range(2) for q4 in range(4)]
        for p0 in range(0, 8, 2):
            pair = chunks[p0:p0 + 2]
            for i_, (hh, q4) in enumerate(pair):
                K.act(Ebuf[i_][:], accd[:, hh, q4 * 512:(q4 + 1) * 512], AF.Ln)
            for i_, (hh, q4) in enumerate(pair):
                K.act(Ebuf[i_][:], Ebuf[i_][:], AF.Exp, scale=-1.0)
            for i_, (hh, q4) in enumerate(pair):
                sl_ = slice(q4 * 512, (q4 + 1) * 512)
                K.tt(yT[:, hh, sl_], accn[:, hh, sl_], Ebuf[i_][:], ALU.mult)
        K.phase_end()


    def ssd(l):
        K.phase_begin()
        wb, wv = getw(("s", l), w_in_d[l, :, C_SSD:C_SSD + 1024], 8, 1024)
        wb2, wv2 = wsmall("wdt", w_in_d[l, :, C_SSD + 1024:C_SSD + 1028], 8, 4)
        if "d" in parts:
            prefetch(("h", l), w_in_d[l, :, C_HGRN:C_HGRN + 1024], 8, 1024)
        cpar = K.sb("cpar", [128, 30]); dtb = K.sb("dtb", [128, 4]); aneg = K.sb("aneg", [128, 4])
        Dfull = K.sb("Dfull", [128, 256]); normw = K.sb("normw", [128, 256])
        K.dma(cpar[:], ssdconv_d[l], wk=[cpar])
        K.dma(dtb[:], dtb_d[l:l + 1, :].partition_broadcast(128), wk=[dtb])
        K.dma(aneg[:], alog_d[l:l + 1, :].partition_broadcast(128), wk=[aneg])
        K.dma(Dfull[:], ssdD_d[l:l + 1, :].partition_broadcast(128), wk=[Dfull])
        K.dma(normw[:], ssdnw_d[l:l + 1, :].partition_broadcast(128), wk=[normw])
        K.act(aneg[:], aneg[:], AF.Exp)
        K.ts(aneg[:], aneg[:], -1.0, ALU.mult)
        xpads = [K.sb(f"xpad{i}", [128, T + 3], BF16) for i in range(2)]
        cdiag = K.sb("cdiag", [128, 24, 128], BF16)
        for cj in range(24):
            c_, j_ = divmod(cj, 4)
            K.act(cdiag[:, cj, :], ident[:], AF.Copy, scale=cpar[:, c_ * 5 + j_:c_ * 5 + j_ + 1])
        xsT = K.sb("xsT", [128, 2, T], BF16); BT = K.sb("BT", [128, 2, T], BF16); CT = K.sb("CT", [128, 2, T], BF16)
        for xp_ in xpads:
            K.memset(xp_[:, 0:3], 0.0)
        for c in range(6):
            xpad = xpads[c % 2]
            for nb in range(4):
                pb = psb[nb % 2]
                for kc in range(8):
                    K.mm(pb[:, :], lhsT=wv[:, kc, 256 + c * 128:256 + (c + 1) * 128], rhs=xT[:, kc, nb * 512:(nb + 1) * 512],
                         start=(kc == 0), stop=(kc == 7), rk=[wb, ("xT", nb)])
                K.act(xpad[:, 3 + nb * 512:3 + (nb + 1) * 512], pb[:, :], AF.Copy)
            dst = (xsT, BT, CT)[c // 2]
            for nb in range(4):
                pc = psb[2 + nb % 2]
                for j in range(4):
                    K.mm(pc[:, :], lhsT=cdiag[:, c * 4 + j, :], rhs=xpad[:, nb * 512 + j:nb * 512 + j + 512],
                         start=(j == 0), stop=(j == 3))
                K.act(dst[:, c % 2, nb * 512:(nb + 1) * 512], pc[:, :], AF.Silu, bias=cpar[:, c * 5 + 4:c * 5 + 5])
        dt = K.sb("dt", [128, 64]); dA = K.sb("dA", [128, 64]); cs = K.sb("cs", [128, 64]); ncs = K.sb("ncs", [128, 64])
        expcs = K.sb("expcs", [128, 64]); dst_ = K.sb("dsts", [128, 64]); cdec = K.sb("cdec", [128, 64])
        for t in range(NT):
            pb = psb[t % 2]
            for kc in range(8):
                K.mm(pb[:, 0:4], lhsT=xT[:, kc, t * 128:(t + 1) * 128], rhs=wv2[:, kc, 0:4],
                     start=(kc == 0), stop=(kc == 7), rk=[wb2, ("xT", t // 4)])
            K.tt(dt[:, t * 4:(t + 1) * 4], pb[:, 0:4], dtb[:], ALU.add)
        K.act(dt[:], dt[:], AF.Exp)
        K.act(dt[:], dt[:], AF.Ln, bias=1.0)
        K.tt(dA[:].rearrange("p (c h) -> p c h", h=4), dt[:].rearrange("p (c h) -> p c h", h=4),
             aneg[:].unsqueeze(1).to_broadcast([128, 16, 4]), ALU.mult)
        K.mm(psb[2][:, 0:64], lhsT=triT[:], rhs=dA[:])
        K.mm(psb[2][:, 64:128], lhsT=ones_f[:], rhs=dA[:])
        K.cp(cs[:], psb[2][:, 0:64])
        K.ts(ncs[:], cs[:], -1.0, ALU.mult)
        K.act(expcs[:], cs[:], AF.Exp)
        K.tt(dst_[:], psb[2][:, 64:128], cs[:], ALU.subtract)
        K.act(dst_[:], dst_[:], AF.Exp)
        K.act(cdec[:], psb[2][:, 64:128], AF.Exp)
        state = K.sb("sstate", [128, 256]); K.memset(state[:], 0.0)
        prevb = [K.sb(f"prevb{i}", [128, 256], BF16) for i in range(2)]
        Rb = [K.sb(f"Rb{i}", [128, 128]) for i in range(2)]
        decT = [K.sb(f"decT{i}", [128, 128]) for i in range(2)]
        MT = [K.sb(f"MT{i}", [128, 4, 128], BF16) for i in range(2)]
        xs_tm = [K.sb(f"xstm{i}", [128, 256]) for i in range(2)]
        xdt = [K.sb(f"xdt{i}", [128, 256], BF16) for i in range(2)]
        xdt2 = [K.sb(f"xdt2{i}", [128, 256], BF16) for i in range(2)]
        B_tm = [K.sb(f"Btm{i}", [128, 256], BF16) for i in range(2)]
        zs = [K.sb(f"zs{i}", [128, 256]) for i in range(2)]
        yy = [K.sb(f"yy{i}", [128, 256]) for i in range(2)]
        y2 = [K.sb(f"y2{i}", [128, 256]) for i in range(2)]
        ss = [K.sb(f"ssq{i}", [128, 2]) for i in range(2)]
        rc = 0
        for c in range(NT):
            b = c % 2
            tk = slice(c * 128, (c + 1) * 128)
            for g in range(2):
                K.mm(psb[3][:, g * 128:(g + 1) * 128], lhsT=BT[:, g, tk], rhs=CT[:, g, tk])
            for h in range(4):
                col = c * 4 + h
                R = Rb[rc % 2]; dT = decT[rc % 2]; rc += 1
                K.act(R[:], triT[:], AF.Copy, scale=dA[:, col:col + 1])
                K.mm(psb[4][:, 0:128], lhsT=ones_f[:], rhs=R[:], start=True, stop=False)
                K.mm(psb[4][:, 0:128], lhsT=ident[:], rhs=maskneg[:], start=False, stop=True)
                K.act(dT[:], psb[4][:, 0:128], AF.Exp, bias=ncs[:, col:col + 1])
                K.tt(MT[b][:, h, :], psb[3][:, (h // 2) * 128:(h // 2 + 1) * 128], dT[:], ALU.mult)
            for i, (src, cc) in enumerate(((xsT, 0), (xsT, 1), (BT, 0), (BT, 1))):
                K.tr(pbf[:, i * 128:(i + 1) * 128], src[:, cc, tk], ident_bf[:])
            K.cp(xs_tm[b][:], pbf[:, 0:256], eng="act")
            K.cp(B_tm[b][:], pbf[:, 256:512], eng="act")
            K.tt(xdt[b][:].rearrange("p (h e) -> p h e", h=4), xs_tm[b][:].rearrange("p (h e) -> p h e", h=4),
                 dt[:, c * 4:(c + 1) * 4].unsqueeze(2).to_broadcast([128, 4, 64]), ALU.mult)
            K.tt(xdt2[b][:].rearrange("p (h e) -> p h e", h=4), xdt[b][:].rearrange("p (h e) -> p h e", h=4),
                 dst_[:, c * 4:(c + 1) * 4].unsqueeze(2).to_broadcast([128, 4, 64]), ALU.mult)
            K.cp(prevb[b][:], state[:], eng="act")
            for h in range(4):
                K.mm(psb[5][:, h * 64:(h + 1) * 64], lhsT=MT[b][:, h, :], rhs=xdt[b][:, h * 64:(h + 1) * 64])
            for h in range(4):
                K.mm(psb[5][:, 256 + h * 64:256 + (h + 1) * 64], lhsT=CT[:, h // 2, tk], rhs=prevb[b][:, h * 64:(h + 1) * 64])
            for h in range(4):
                K.mm(psb[6][:, h * 64:(h + 1) * 64], lhsT=B_tm[b][:, (h // 2) * 128:(h // 2 + 1) * 128],
                     rhs=xdt2[b][:, h * 64:(h + 1) * 64])
            K.tt(state[:].rearrange("p (h e) -> p h e", h=4), state[:].rearrange("p (h e) -> p h e", h=4),
                 cdec[:, c * 4:(c + 1) * 4].unsqueeze(2).to_broadcast([128, 4, 64]), ALU.mult)
            K.tt(state[:], psb[6][:, 0:256], state[:], ALU.add)
            y = yy[b]
            K.tt(y[:].rearrange("p (h e) -> p h e", h=4), psb[5][:, 256:512].rearrange("p (h e) -> p h e", h=4),
                 expcs[:, c * 4:(c + 1) * 4].unsqueeze(2).to_broadcast([128, 4, 64]), ALU.mult)
            K.tt(y[:], psb[5][:, 0:256], y[:], ALU.add)
            K.tt(y2[b][:], xs_tm[b][:], Dfull[:], ALU.mult, eng="pool")
            K.tt(y[:], y[:], y2[b][:], ALU.add)
            pb = psb[c % 2]
            for kc in range(8):
                K.mm(pb[:, 0:256], lhsT=xT[:, kc, tk], rhs=wv[:, kc, 0:256], start=(kc == 0), stop=(kc == 7),
                     rk=[wb, ("xT", c // 4)])
            K.act(zs[b][:], pb[:, 0:256], AF.Tanh, scale=0.5)
            K.stt(zs[b][:], zs[b][:], 1.0, pb[:, 0:256], ALU.add, ALU.mult)
            K.stt(y[:], y[:], 0.5, zs[b][:], ALU.mult, ALU.mult)
            K.act(y2[b][:], y[:], AF.Square)
            nc_ = nc
            sq = y2[b]; sso = ss[b]
            K.S.op("dve", lambda sq=sq, sso=sso: nc_.vector.tensor_reduce(out=sso[:], in_=sq[:].rearrange("p (g e) -> p g e", g=2), axis=AX.X, op=ALU.add),
                   reads=[sq], writes=[sso])
            K.ts(sso[:], sso[:], 1.0 / 128, ALU.mult, RMS_EPS, ALU.add)
            K.tt(sso[:], sso[:], mhalf[:, 0:2], ALU.pow, eng="pool")
            for g in range(2):
                K.ts(y[:, g * 128:(g + 1) * 128], y[:, g * 128:(g + 1) * 128], sso[:, g:g + 1], ALU.mult)
            K.tt(y[:], y[:], normw[:], ALU.mult)
            for g in range(2):
                K.tr(psb[2][:, g * 128:(g + 1) * 128], y[:, g * 128:(g + 1) * 128], ident[:])
            K.cp(yT[:, :, tk], psb[2][:, 0:256].rearrange("p (g e) -> p g e", g=2))
        K.phase_end()


    def hgrn(l):
        K.phase_begin()
        wb, wv = getw(("h", l), w_in_d[l, :, C_HGRN:C_HGRN + 1024], 8, 1024)
        pre[("f0", l)] = (wload(w_up_d[l, :, 0:512], 8, 512), wload(w_down_d[l, 0:512, :], 4, D, off=4096, new=False))
        tri16 = K.sb("tri16", [128, 128]); blk16 = K.sb("blk16", [128, 128]); bm = K.sb("bm", [128, 8])
        bones = K.sb("bones", [128, 128]); hgnw = K.sb("hgnw", [128, 2])
        lbb = K.sb("lbb", [128, 256]); omlb = K.sb("omlb", [128, 256])
        K.dma(tri16[:], tri16_d, wk=[tri16]); K.dma(blk16[:], blk16_d, wk=[blk16]); K.dma(bm[:], bm_d, wk=[bm])
        dif16 = K.sb("dif16", [128, 128])
        K.tt(dif16[:], blk16[:], tri16[:], ALU.subtract)
        K.dma(bones[:], bones64_d, wk=[bones]); K.dma(hgnw[:], hgnw_d[l], wk=[hgnw])
        if l == 0:
            K.memset(lbb[:], 0.0)
        else:
            K.dma(lbb[:], lowb_d[1:2, :].partition_broadcast(128), wk=[lbb])
            K.dma(omlb[:], lowb_d[0:1, :].partition_broadcast(128), wk=[omlb])
            K.tt(lbb[:], lbb[:], omlb[:], ALU.subtract)
            K.act(lbb[:], lbb[:], AF.Sigmoid)
        K.ts(omlb[:], lbb[:], -0.5, ALU.mult, 0.5, ALU.add)
        K.tt(lbb[:], lbb[:], omlb[:], ALU.add)
        K.ts(hgnw[:], hgnw[:], 0.5, ALU.mult)
        epsc = K.sb("epsc", [128, 1]); K.memset(epsc[:], RMS_EPS)
        Sst = [K.sb(f"Sst{i}", [128, 9, 64]) for i in range(2)]; SbBD = K.sb("SbBD", [128, 8, 2, 128], BF16)
        for i in range(2):
            K.memset(Sst[i][:, 0, :], 0.0)
        K.memset(SbBD[:], 0.0, eng="pool")
        A = lambda nm, shp, dt_=F32: [K.sb(f"{nm}{i}", shp, dt_) for i in range(2)]
        sg = A("hsg", [128, 256]); fg = A("hfg", [128, 256]); logf = A("hlogf", [128, 256]); kk = A("hkk", [128, 256])
        qs = A("hqs", [128, 256]); V = A("hV", [128, 256], BF16); bb = A("hb", [128, 256]); eb = A("heb", [128, 256])
        enb = A("henb", [128, 256]); ed = A("hed", [128, 256]); tot = A("htot", [128, 256])
        qbar = A("hqbar", [128, 256]); kbar = A("hkbar", [128, 256]); kdec = A("hkdec", [128, 256], BF16)
        qkT = A("hqkT", [128, 4, 128], BF16); totT = A("htotT", [128, 2, 128]); attT = A("hattT", [128, 4, 128], BF16)
        kdb = A("hkdb", [128, 8, 256], BF16); sq = A("hsq", [128, 256]); rstd = A("hrstd", [128, 256]); oo = A("hoo", [128, 256])
        for t in range(NT):
            b = t % 2
            tk = slice(t * 128, (t + 1) * 128)
            if t > 0:
                for i in range(2):
                    K.cp(Sst[i][:, 0, :], Sst[i][:, 8, :], eng=("dve" if i == 0 else "pool"))
            for kc in range(8):
                K.mm(psb[0][:, :], lhsT=xT[:, kc, tk], rhs=wv[:, kc, 0:512], start=(kc == 0), stop=(kc == 7),
                     rk=[wb, ("xT", t // 4)])
            for kc in range(8):
                K.mm(psb[1][:, 0:256], lhsT=xT[:, kc, tk], rhs=wv[:, kc, 512:768], start=(kc == 0), stop=(kc == 7),
                     rk=[wb, ("xT", t // 4)])
            for c2 in range(2):
                for kc in range(8):
                    K.mm(psb[1][:, 256 + c2 * 128:256 + (c2 + 1) * 128], lhsT=wv[:, kc, 768 + c2 * 128:768 + (c2 + 1) * 128],
                         rhs=xT[:, kc, tk], start=(kc == 0), stop=(kc == 7), rk=[wb, ("xT", t // 4)])
            K.act(sg[b][:], psb[1][:, 256:512], AF.Tanh, scale=0.5)
            K.stt(sg[b][:], sg[b][:], 1.0, psb[1][:, 256:512], ALU.add, ALU.mult)
            for hh in range(2):
                K.ts(sg[b][:, hh * 128:(hh + 1) * 128], sg[b][:, hh * 128:(hh + 1) * 128], hgnw[:, hh:hh + 1], ALU.mult)
            K.act(fg[b][:], psb[0][:, 256:512], AF.Tanh, scale=0.5)
            K.tt(fg[b][:], fg[b][:], omlb[:], ALU.mult)
            K.tt(fg[b][:], fg[b][:], lbb[:], ALU.add)
            K.act(logf[b][:], fg[b][:], AF.Ln)
            K.ts(kk[b][:], fg[b][:], -1.0, ALU.mult, 1.0, ALU.add)
            K.act(qs[b][:], psb[0][:, 0:256], AF.Tanh, scale=0.5)
            K.stt(qs[b][:], qs[b][:], 1.0, psb[0][:, 0:256], ALU.add, ALU.mult)
            K.cp(V[b][:], psb[1][:, 0:256], eng="act")
            K.mm(psb[2][:, 0:256], lhsT=tri16[:], rhs=logf[b][:])
            K.mm(psb[2][:, 256:512], lhsT=blk16[:], rhs=logf[b][:])
            K.mm(psb[3][:, 0:256], lhsT=dif16[:], rhs=logf[b][:])
            K.act(eb[b][:], psb[2][:, 0:256], AF.Exp)
            K.act(enb[b][:], psb[2][:, 0:256], AF.Exp, scale=-1.0)
            K.act(ed[b][:], psb[3][:, 0:256], AF.Exp)
            K.act(tot[b][:], psb[2][:, 256:512], AF.Exp)
            K.stt(qbar[b][:], qs[b][:], 0.5, eb[b][:], ALU.mult, ALU.mult)
            K.tt(kbar[b][:], kk[b][:], enb[b][:], ALU.mult, eng="pool")
            K.tt(kdec[b][:], kk[b][:], ed[b][:], ALU.mult)
            for i, src in enumerate((qbar[b], qbar[b], kbar[b], kbar[b])):
                K.tr(psb[3][:, i * 128:(i + 1) * 128], src[:, (i % 2) * 128:(i % 2 + 1) * 128], ident[:])
            for i in range(2):
                K.tr(psb[2][:, i * 128:(i + 1) * 128], tot[b][:, i * 128:(i + 1) * 128], ident[:])
            K.cp(qkT[b][:].rearrange("p a t -> p (a t)"), psb[3][:, :], eng="act")
            K.cp(totT[b][:].rearrange("p a t -> p (a t)"), psb[2][:, 0:256], eng="act")
            for h in range(4):
                po = (h % 2) * 64; hh = h // 2
                bank = psb[4] if h % 2 == 0 else psb[5]
                K.mm(bank[:, hh * 128:(hh + 1) * 128], lhsT=qkT[b][po:po + 64, 2 + hh, :], rhs=qkT[b][po:po + 64, hh, :])
            for hl in range(2):
                bank = psb[4] if hl == 0 else psb[5]
                K.tt(attT[b][:, hl * 2:(hl + 1) * 2, :], bank[:, 0:256].rearrange("p (a t) -> p a t", a=2),
                     tri16[:].unsqueeze(1).to_broadcast([128, 2, 128]), ALU.mult)
            K.tt(kdb[b][:], kdec[b][:].unsqueeze(1).to_broadcast([128, 8, 256]),
                 bm[:].unsqueeze(2).to_broadcast([128, 8, 256]), ALU.mult)
            for half in range(2):
                for c in range(half * 4, half * 4 + 4):
                    for h in range(4):
                        hl = h % 2; hh = h // 2
                        co = ((c % 4) * 2 + hh) * 64
                        K.mm(psb[6][hl * 64:(hl + 1) * 64, co:co + 64], lhsT=kdb[b][:, c, h * 64:(h + 1) * 64],
                             rhs=V[b][:, h * 64:(h + 1) * 64])
                for c in range(half * 4, half * 4 + 4):
                    for hh in range(2):
                        co = ((c % 4) * 2 + hh) * 64
                        K.stt(Sst[hh][:, c + 1, :], Sst[hh][:, c, :], totT[b][:, hh, c * 16:c * 16 + 1], psb[6][:, co:co + 64],
                              ALU.mult, ALU.add)
            for hh in range(2):
                K.cp(SbBD[0:64, :, hh, 0:64], Sst[hh][0:64, 0:8, :], eng="act")
                K.cp(SbBD[64:128, :, hh, 64:128], Sst[hh][64:128, 0:8, :])
            for hh in range(2):
                for hl in range(2):
                    h = 2 * hh + hl
                    K.mm(psb[4][hl * 64:(hl + 1) * 64, 256 + hh * 128:256 + (hh + 1) * 128], lhsT=V[b][:, h * 64:(h + 1) * 64],
                         rhs=attT[b][:, hl * 2 + hh, :], start=True, stop=False)
                for c in range(8):
                    K.mm(psb[4][:, 256 + hh * 128 + c * 16:256 + hh * 128 + (c + 1) * 16], lhsT=SbBD[:, c, hh, :],
                         rhs=qkT[b][:, hh, c * 16:(c + 1) * 16], start=False, stop=(c == 7))
            K.act(sq[b][:], psb[4][:, 256:512], AF.Square)
            K.mm(psb[5][:, 256:512], lhsT=bones[:], rhs=sq[b][:])
            K.act(rstd[b][:], psb[5][:, 256:512], AF.Ln, bias=epsc[:, 0:1], scale=1.0 / 64)
            K.act(rstd[b][:], rstd[b][:], AF.Exp, scale=-0.5)
            K.tt(oo[b][:], psb[4][:, 256:512], rstd[b][:], ALU.mult)
            K.tt(yT[:, :, tk], oo[b][:].rearrange("p (a t) -> p a t", a=2), sg[b][:].rearrange("p (a t) -> p a t", a=2), ALU.mult)
        K.phase_end()


    def rwkv(l):
        K.phase_begin()
        v4 = lambda ap: ap.rearrange("p (h e) -> p h e", h=4)
        b4 = lambda ap: ap.unsqueeze(2).to_broadcast([128, 4, 64])
        wb, wv = getw(("r", l), w_in_d[l, :, 0:896], 8, 896)
        if "b" in parts:
            prefetch(("a", l), w_in_d[l, :, C_ATT:C_ATT + 768], 8, 768)
        names = {}

        def bc(name, src, n=256):
            t_ = K.sb(name, [128, n]); K.dma(t_[:], src.partition_broadcast(128), wk=[t_]); return t_
        mucol = K.sb("mucol", [128, 1]); omucol = K.sb("omucol", [128, 1])
        K.dma(mucol[:], rmucol_d[l], wk=[mucol])
        K.ts(omucol[:], mucol[:], -1.0, ALU.mult, 1.0, ALU.add)
        fT = K.sb("fT", [128, T + 1]); loraT = K.sb("loraT", [128, T], BF16); ftmp = K.sb("ftmp", [128, T])
        K.memset(fT[:, 0:1], 0.0)
        for nb in range(4):
            pb = psb[nb % 2]
            for kc in range(8):
                K.mm(pb[:, :], lhsT=wv[:, kc, 768:896], rhs=xT[:, kc, nb * 512:(nb + 1) * 512], start=(kc == 0), stop=(kc == 7),
                     rk=[wb, ("xT", nb)])
            K.act(fT[:, 1 + nb * 512:1 + (nb + 1) * 512], pb[:, :], AF.Copy)
        K.ts(ftmp[:], fT[:, 0:T], mucol[:, 0:1], ALU.mult)
        K.stt(ftmp[:], fT[:, 1:T + 1], omucol[:, 0:1], ftmp[:], ALU.mult, ALU.add)
        K.act(loraT[0:32, :], ftmp[0:32, :], AF.Tanh)
        K.act(loraT[32:64, :], ftmp[32:64, :], AF.Copy)
        K.act(ftmp[64:128, :], ftmp[64:128, :], AF.Tanh, scale=0.5)
        K.ts(loraT[64:128, :], ftmp[64:128, :], 0.5, ALU.mult, 0.5, ALU.add)
        K.dma(lscr_d, loraT[:], rk=[loraT], wk=["lscr"])
        if l > 0:
            wb2, wv2 = wsmall("wvres", wvres_d[0], 8, 32)
            vmu = K.sb("vmu", [128, 1]); ovmu = K.sb("ovmu", [128, 1])
            K.dma(vmu[:], vmucol_d, wk=[vmu])
            K.ts(ovmu[:], vmu[:], -1.0, ALU.mult, 1.0, ALU.add)
            vrT = K.sb("vrT", [32, T], BF16)
            for nb in range(4):
                pb = psb[nb % 2]
                for kc in range(8):
                    K.mm(pb[0:32, :], lhsT=wv2[:, kc, 0:32], rhs=xT[:, kc, nb * 512:(nb + 1) * 512], start=(kc == 0), stop=(kc == 7),
                         rk=[wb2, ("xT", nb)])
                K.act(fT[0:32, 1 + nb * 512:1 + (nb + 1) * 512], pb[0:32, :], AF.Copy)
            K.ts(ftmp[0:32, :], fT[0:32, 0:T], vmu[0:32, 0:1], ALU.mult)
            K.stt(ftmp[0:32, :], fT[0:32, 1:T + 1], ovmu[0:32, 0:1], ftmp[0:32, :], ALU.mult, ALU.add)
            K.act(vrT[:], ftmp[0:32, :], AF.Copy)
            K.dma(vrscr_d, vrT[:], rk=[vrT], wk=["vrscr"])
        K.phase_end()
        K.phase_begin()
        mu_b = bc("mu_b", mush_d[l:l + 1, 0:768], 768)
        w0_b = bc("w0_b", rw0_d[l:l + 1, :]); a0_b = bc("a0_b", ra0_d[l:l + 1, :]); kk_b = bc("kk_b", rkk_d[l:l + 1, :])
        ka_b = bc("ka_b", rka_d[l:l + 1, :]); lw_b = bc("lnxw_b", rlw_d[l:l + 1, :]); lb_b = bc("lnxb_b", rlb_d[l:l + 1, :])
        rk_b = bc("rk_b", rrk_d[l:l + 1, :])
        msl = K.sb("msl", [128, 128]); msu = K.sb("msu", [128, 128])
        shiftM = K.sb("shiftM", [128, 128], BF16); carryM = K.sb("carryM", [128, 128], BF16)
        for t_, d_ in ((msl, msl_d), (msu, msu_d)):
            K.dma(t_[:], d_, wk=[t_])
        for t_, d_ in ((shiftM, shift_d), (carryM, carry_d)):
            K.dma(t_[:], d_, eng="pool", wk=[t_])
        loraW = K.sb("loraW", [128, 768], BF16)
        K.memset(loraW[:], 0.0, eng="pool")
        K.dma(loraW[0:32, 0:256], rw2_d[l], eng="pool", rk=[loraW], wk=[loraW])
        K.dma(loraW[32:64, 256:512], ra2_d[l], eng="pool", rk=[loraW], wk=[loraW])
        K.dma(loraW[64:128, 512:768], rg2_d[l], eng="pool", rk=[loraW], wk=[loraW])
        loraTc = [K.sb(f"loraTc{i}", [128, 128], BF16) for i in range(2)]
        if l > 0:
            v0_b = bc("v0_b", rv0_d[0:1, :])
            v2W = K.sb("v2W", [32, 256], BF16)
            K.dma(v2W[:], rv2_d[0], eng="pool", wk=[v2W])
            vrTc = [K.sb(f"vrTc{i}", [32, 128], BF16) for i in range(2)]
        F = lambda nm, n=256, dt_=F32: K.sb(nm, [128, n], dt_)
        D2 = lambda nm, n=256, dt_=F32: [K.sb(f"{nm}{i}", [128, n], dt_) for i in range(2)]
        fsb = D2("fsb", 768, BF16)
        fl = F("fl", 768)
        lw = F("lw"); aa = F("aa"); vv = F("vv"); kkn = F("kkn"); t1 = F("t1"); t2 = F("t2"); t3 = F("t3"); kt = F("kt"); be = F("be")
        s4 = K.sb("s4", [128, 4]); s4b = K.sb("s4b", [128, 4])
        Lsb = F("Lsb"); E1 = F("E1"); E2 = F("E2"); E3 = F("E3"); E4 = F("E4")
        gg2 = D2("gg"); bonus2 = D2("bonus")
        btil2 = D2("btil", 256, BF16); ktil2 = D2("ktil", 256, BF16); Vb2 = D2("rVb", 256, BF16)
        gam2 = [K.sb(f"gam{i}", [128, 2]) for i in range(2)]
        abkrT2 = [K.sb(f"abkrT{i}", [128, 8, 128], BF16) for i in range(2)]
        Am = [K.sb(f"Am{i}", [128, 4, 128], BF16) for i in range(2)]
        Bm = [K.sb(f"Bm{i}", [128, 4, 128], BF16) for i in range(2)]
        AKT2 = [K.sb(f"AKT{i}", [128, 4, 128], BF16) for i in range(2)]
        RBT2 = [K.sb(f"RBT{i}", [128, 4, 128], BF16) for i in range(2)]
        RKT2 = [K.sb(f"RKT{i}", [128, 4, 128], BF16) for i in range(2)]
        Pb2 = [K.sb(f"Pb{i}", [128, 4, 128], BF16) for i in range(2)]
        Sf = K.sb("Sf", [128, 2, 64]); SBD = K.sb("SBD", [128, 2, 128], BF16)
        K.memset(Sf[:], 0.0); K.memset(SBD[:], 0.0)
        Wb_ = F("Wb", 256, BF16); Ub_ = F("Ub", 256, BF16); yv = F("yv"); ysq = F("ysq", 256, BF16)
        mean4 = K.sb("mean4", [128, 4]); var4 = K.sb("var4", [128, 4])
        m3 = lambda mk: mk[:].unsqueeze(1).to_broadcast([128, 2, 128])
        X = pbf[:].bitcast(F32)
        for c in range(NT):
            par = c % 2
            tk = slice(c * 128, (c + 1) * 128)
            f = fsb[c % 2]; fp_ = fsb[(c + 1) % 2]
            gg = gg2[par]; bonus = bonus2[par]; btil = btil2[par]; ktil = ktil2[par]; Vb_ = Vb2[par]; gam = gam2[par]
            abkrT = abkrT2[par]; AKT = AKT2[par]; RBT = RBT2[par]; RKT = RKT2[par]; Pb = Pb2[par]
            for (bank, o0, c0, c1) in ((psb[0], 0, 0, 512), (psb[1], 0, 512, 768)):
                for kc in range(8):
                    K.mm(bank[:, o0:o0 + c1 - c0], lhsT=xT[:, kc, tk], rhs=wv[:, kc, c0:c1], start=(kc == 0), stop=(kc == 7),
                         rk=[wb, ("xT", c // 4)])
            K.cp(f[:, 0:512], psb[0][:, :], eng="act")
            K.cp(f[:, 512:768], psb[1][:, 0:256], eng="act")
            for (bank, o0, c0, c1) in ((psb[2], 0, 0, 512), (psb[1], 256, 512, 768)):
                K.mm(bank[:, o0:o0 + c1 - c0], lhsT=shiftM[:], rhs=f[:, c0:c1], start=True, stop=(c == 0))
                if c > 0:
                    K.mm(bank[:, o0:o0 + c1 - c0], lhsT=carryM[:], rhs=fp_[:, c0:c1], start=False, stop=True)
            K.tt(fl[:, 0:512], psb[2][:, :], mu_b[:, 0:512], ALU.mult)
            K.tt(fl[:, 512:768], psb[1][:, 256:512], mu_b[:, 512:768], ALU.mult)
            K.tt(fl[:], fl[:], f[:], ALU.add, eng="pool")
            r_ = fl[:, 0:256]; k_ = fl[:, 256:512]; v_ = fl[:, 512:768]
            lt = loraTc[c % 2]
            K.dma(lt[:], lscr_d[:, tk], rk=["lscr", lt], wk=[lt])
            K.mm(psb[3][:, 0:512], lhsT=lt[:], rhs=loraW[:, 0:512])
            K.mm(psb[0][:, 0:256], lhsT=lt[:], rhs=loraW[:, 512:768])
            K.tt(lw[:], psb[3][:, 0:256], w0_b[:], ALU.add)
            K.act(lw[:], lw[:], AF.Tanh, scale=0.5)
            K.ts(lw[:], lw[:], -0.3032653298563167, ALU.mult, -0.3032653298563167, ALU.add, eng="pool")
            K.tt(aa[:], psb[3][:, 256:512], a0_b[:], ALU.add)
            K.act(aa[:], aa[:], AF.Tanh, scale=0.5)
            K.ts(aa[:], aa[:], 0.5, ALU.mult, 0.5, ALU.add, eng="pool")
            K.cp(gg[:], psb[0][:, 0:256], eng="act")
            if l == 0:
                K.cp(vv[:], v_, eng="pool")
                K.dma(vfirst_d[tk, :], vv[:], rk=[vv], wk=[("vfirst", c)])
            else:
                vt_ = vrTc[c % 2]
                K.dma(vt_[:], vrscr_d[:, tk], rk=["vrscr", vt_], wk=[vt_])
                K.mm(psb[0][:, 256:512], lhsT=vt_[0:32, :], rhs=v2W[0:32, :])
                K.tt(t1[:], psb[0][:, 256:512], v0_b[:], ALU.add)
                K.act(t1[:], t1[:], AF.Tanh, scale=0.5)
                K.ts(t1[:], t1[:], 0.5, ALU.mult, 0.5, ALU.add, eng="pool")
                K.dma(t2[:], vfirst_d[tk, :], rk=[("vfirst", c), t2], wk=[t2])
                K.tt(t2[:], t2[:], v_, ALU.subtract, eng="pool")
                K.tt(t2[:], t2[:], t1[:], ALU.mult, eng="pool")
                K.tt(vv[:], t2[:], v_, ALU.add, eng="pool")
            K.cp(Vb_[:], vv[:], eng="act")
            K.tt(kkn[:], k_, kk_b[:], ALU.mult, eng="pool")
            K.act(t1[:], kkn[:], AF.Square)
            K.S.op("dve", lambda t1=t1: nc.vector.tensor_reduce(out=s4[:], in_=v4(t1[:]), axis=AX.X, op=ALU.add), reads=[t1], writes=[s4])
            K.ts(s4[:], s4[:], 1e-24, ALU.max)
            K.tt(s4[:], s4[:], mhalf[:, 0:4], ALU.pow, eng="pool")
            K.tt(v4(kkn[:]), v4(kkn[:]), b4(s4[:]), ALU.mult)
            K.stt(t3[:], aa[:], -1.0, ka_b[:], ALU.add, ALU.mult)
            K.stt(kt[:], t3[:], 1.0, k_, ALU.add, ALU.mult)
            K.tt(be[:], kkn[:], aa[:], ALU.mult, eng="pool")
            K.tt(t3[:], r_, kt[:], ALU.mult, eng="pool")
            K.tt(t3[:], t3[:], rk_b[:], ALU.mult, eng="pool")
            K.S.op("dve", lambda t3=t3: nc.vector.tensor_reduce(out=s4b[:], in_=v4(t3[:]), axis=AX.X, op=ALU.add), reads=[t3], writes=[s4b])
            K.tt(v4(bonus[:]), v4(vv[:]), b4(s4b[:]), ALU.mult)
            K.mm(psb[2][:, 0:256], lhsT=triT[:], rhs=lw[:])
            K.mm(psb[2][:, 256:512], lhsT=msu[:], rhs=lw[:])
            K.mm(psb[3][:, 256:512], lhsT=msl[:], rhs=lw[:])
            for hh in range(2):
                K.mm(psb[3][:, hh:hh + 1], lhsT=lw[:, hh * 128:(hh + 1) * 128], rhs=ones_f[:, 0:1])
            K.act(gam[:], psb[3][:, 0:2], AF.Exp)
            K.act(E1[:], psb[2][:, 256:512], AF.Exp)
            K.act(E2[:], psb[2][:, 0:256], AF.Exp, scale=-1.0)
            K.act(E3[:], psb[2][:, 0:256], AF.Exp)
            K.act(E4[:], psb[3][:, 256:512], AF.Exp)
            K.tt(btil[:], be[:], E4[:], ALU.mult, eng="pool")
            K.tt(ktil[:], kt[:], E4[:], ALU.mult)
            K.stt(E1[:], kkn[:], -1.0, E1[:], ALU.mult, ALU.mult)
            K.tt(be[:], be[:], E2[:], ALU.mult, eng="pool")
            K.tt(kt[:], kt[:], E2[:], ALU.mult)
            K.tt(E3[:], r_, E3[:], ALU.mult, eng="pool")
            for qi, src in enumerate((E1, be, kt, E3)):
                bank = psb[0] if qi < 2 else psb[1]
                for hh in range(2):
                    K.tr(bank[:, ((qi % 2) * 2 + hh) * 128:((qi % 2) * 2 + hh + 1) * 128], src[:, hh * 128:(hh + 1) * 128], ident[:])
            K.cp(abkrT[:, 0:4, :].rearrange("p a t -> p (a t)"), psb[0][:, :], eng="act")
            K.cp(abkrT[:, 4:8, :].rearrange("p a t -> p (a t)"), psb[1][:, :])
            aT = lambda h, abkrT=abkrT: abkrT[(h % 2) * 64:(h % 2) * 64 + 64, 0 + h // 2, :]
            bT = lambda h, abkrT=abkrT: abkrT[(h % 2) * 64:(h % 2) * 64 + 64, 2 + h // 2, :]
            kT_ = lambda h, abkrT=abkrT: abkrT[(h % 2) * 64:(h % 2) * 64 + 64, 4 + h // 2, :]
            rT = lambda h, abkrT=abkrT: abkrT[(h % 2) * 64:(h % 2) * 64 + 64, 6 + h // 2, :]

            def amat(L_, R_, mask, dst):
                for h in range(4):
                    bank = psb[4] if h % 2 == 0 else psb[5]
                    K.mm(bank[:, (h // 2) * 128:(h // 2 + 1) * 128], lhsT=L_(h), rhs=R_(h))
                for hl in range(2):
                    bank = psb[4] if hl == 0 else psb[5]
                    K.tt(dst[:, hl * 2:hl * 2 + 2, :], bank[:, 0:256].rearrange("p (a t) -> p a t", a=2), m3(mask), ALU.mult)
            hidx = lambda h: (h % 2) * 2 + h // 2
            amat(aT, bT, msl, Am[0])
            amat(bT, aT, msu, Bm[0])
            amat(kT_, aT, msu, AKT)
            amat(bT, rT, triT, RBT)
            amat(kT_, rT, triT, RKT)
            K.tt(Pb[:], Bm[0][:], ident[:].unsqueeze(1).to_broadcast([128, 4, 128]), ALU.add, eng="pool")
            cur = 0
            for j in range(1, 7):
                nxt = 1 - cur
                for hi in range(4):
                    K.mm(psb[4][:, hi * 128:(hi + 1) * 128], lhsT=Bm[cur][:, hi, :], rhs=Am[cur][:, hi, :])
                if j < 6:
                    for hi in range(4):
                        K.mm(psb[5][:, hi * 128:(hi + 1) * 128], lhsT=Am[cur][:, hi, :], rhs=Bm[cur][:, hi, :])
                K.cp(Am[nxt][:].rearrange("p a t -> p (a t)"), psb[4][:, :], eng="act")
                if j < 6:
                    K.cp(Bm[nxt][:].rearrange("p a t -> p (a t)"), psb[5][:, :])
                for hi in range(4):
                    K.mm(psb[6][:, hi * 128:(hi + 1) * 128], lhsT=Am[nxt][:, hi, :], rhs=Pb[:, hi, :])
                K.tt(Pb[:].rearrange("p a t -> p (a t)"), psb[6][:, :], Pb[:].rearrange("p a t -> p (a t)"), ALU.add)
                cur = nxt
            for hh in range(2):
                K.mm(X[:, hh * 128:(hh + 1) * 128], lhsT=abkrT[:, 0 + hh, :], rhs=SBD[:, hh, :], start=True, stop=False)
                for hl in range(2):
                    h = 2 * hh + hl
                    K.mm(X[:, h * 64:(h + 1) * 64], lhsT=AKT[:, hidx(h), :], rhs=Vb_[:, h * 64:(h + 1) * 64], start=False, stop=(hl == 1))
            K.cp(Wb_[:], X[:, 0:256], eng="act")
            for h in range(4):
                K.mm(X[:, 256 + h * 64:256 + (h + 1) * 64], lhsT=Pb[:, hidx(h), :], rhs=Wb_[:, h * 64:(h + 1) * 64])
            K.cp(Ub_[:], X[:, 256:512], eng="act")
            for hh in range(2):
                K.mm(X[:, hh * 128:(hh + 1) * 128], lhsT=abkrT[:, 6 + hh, :], rhs=SBD[:, hh, :], start=True, stop=False)
                for hl in range(2):
                    h = 2 * hh + hl
                    K.mm(X[:, h * 64:(h + 1) * 64], lhsT=RBT[:, hidx(h), :], rhs=Ub_[:, h * 64:(h + 1) * 64], start=False, stop=False)
                    K.mm(X[:, h * 64:(h + 1) * 64], lhsT=RKT[:, hidx(h), :], rhs=Vb_[:, h * 64:(h + 1) * 64], start=False, stop=(hl == 1))
            for h in range(4):
                hl = h % 2; hh = h // 2
                K.mm(X[hl * 64:(hl + 1) * 64, 256 + hh * 64:256 + (hh + 1) * 64], lhsT=btil[:, h * 64:(h + 1) * 64],
                     rhs=Ub_[:, h * 64:(h + 1) * 64], start=True, stop=False)
                K.mm(X[hl * 64:(hl + 1) * 64, 256 + hh * 64:256 + (hh + 1) * 64], lhsT=ktil[:, h * 64:(h + 1) * 64],
                     rhs=Vb_[:, h * 64:(h + 1) * 64], start=False, stop=True)
            K.cp(yv[:], X[:, 0:256], eng="act")
            for hh in range(2):
                K.stt(Sf[:, hh, :], Sf[:, hh, :], gam[:, hh:hh + 1], X[:, 256 + hh * 64:256 + (hh + 1) * 64], ALU.mult, ALU.add)
            K.cp(SBD[0:64, :, 0:64], Sf[0:64, :, :], eng="act")
            K.cp(SBD[64:128, :, 64:128], Sf[64:128, :, :])
            K.S.op("dve", lambda: nc.vector.tensor_reduce(out=mean4[:], in_=v4(yv[:]), axis=AX.X, op=ALU.add), reads=[yv], writes=[mean4])
            K.ts(mean4[:], mean4[:], 1.0 / 64, ALU.mult)
            K.tt(v4(yv[:]), v4(yv[:]), b4(mean4[:]), ALU.subtract)
            K.act(ysq[:], yv[:], AF.Square)
            K.S.op("dve", lambda: nc.vector.tensor_reduce(out=var4[:], in_=v4(ysq[:]), axis=AX.X, op=ALU.add), reads=[ysq], writes=[var4])
            K.ts(var4[:], var4[:], 1.0 / 64, ALU.mult, 64e-5, ALU.add)
            K.tt(var4[:], var4[:], mhalf[:, 0:4], ALU.pow, eng="pool")
            K.tt(v4(yv[:]), v4(yv[:]), b4(var4[:]), ALU.mult)
            K.tt(yv[:], yv[:], lw_b[:], ALU.mult, eng="pool")
            K.tt(yv[:], yv[:], lb_b[:], ALU.add, eng="pool")
            K.tt(yv[:], yv[:], bonus[:], ALU.add, eng="pool")
            K.tt(yv[:], yv[:], gg[:], ALU.mult, eng="pool")
            for g in range(2):
                K.tr(X[:, g * 128:(g + 1) * 128], yv[:, g * 128:(g + 1) * 128], ident[:])
            K.cp(yT[:, :, tk], X[:, 0:256].rearrange("p (g e) -> p g e", g=2))
        K.phase_end()
        K.mark("rwkv_end")

    def dump(m):
        if debug:
            K.dma(dbg_d[m], yT[:].rearrange("p c t -> p (c t)"), rk=[yT], wk=[("dbg", m)])

    for l in range(nlayers):
        for t in range(NT):
            K.act(xres[t][:], xres[t][:], AF.Copy, scale=ALPHA)
        if "a" in parts:
            rwkv(l); dump(0); out_proj(l, 0)
        if "b" in parts:
            attention(l); dump(1); out_proj(l, 1)
        if "c" in parts:
            ssd(l); dump(2); out_proj(l, 2)
        if "d" in parts:
            hgrn(l); dump(3); out_proj(l, 3)
        K.phase_begin()
        lnw = K.sb("lnw", [128, D]); lnb = K.sb("lnb", [128, D])
        K.dma(lnw[:], ln1w_d[l:l + 1, :].partition_broadcast(128), wk=[lnw])
        K.dma(lnb[:], ln1b_d[l:l + 1, :].partition_broadcast(128), wk=[lnb])
        for t in range(NT):
            layer_norm(t, lnw, lnb)
            build_xT(t)
        ffn(l)
        if l + 1 < nlayers and "a" in parts:
            prefetch(("r", l + 1), w_in_d[l + 1, :, 0:896], 8, 896)
        lnw = K.sb("lnw2", [128, D]); lnb = K.sb("lnb2", [128, D])
        K.dma(lnw[:], ln2w_d[l:l + 1, :].partition_broadcast(128), wk=[lnw])
        K.dma(lnb[:], ln2b_d[l:l + 1, :].partition_broadcast(128), wk=[lnb])
        for t in range(NT):
            layer_norm(t, lnw, lnb)
            if l == nlayers - 1:
                K.dma(out_d[t * 128:(t + 1) * 128, :], xres[t][:], rk=[xres[t]], wk=[("out", t)])
            else:
                build_xT(t)
        K.phase_end()
    K.S.emit(limit)
    K.S.stats["marks"] = dict(K.marks)
    return nc, K.S.stats


def make_consts():
    c = {}
    c["c_ident"] = np.eye(128, dtype=np.float32)
    j = np.arange(128)[:, None].astype(np.float64)
    i = np.arange(128)[None, :].astype(np.float64)
    am = np.zeros((128, 5, 4, 128), np.float32)
    for mid, (d, prev) in enumerate([(1, True), (1, False), (4, True), (4, False), (16, False)]):
        for h in range(4):
            if prev:
                dist = 128 + i - j
                valid = dist <= 128
            else:
                dist = i - j
                valid = dist >= 0
            am[:, mid, (h % 2) * 2 + h // 2, :] = np.where(valid, -SLOPES[h] * d * dist, NEG)
    c["c_amask"] = am.reshape(128, -1)
    ii = np.arange(128)
    c["c_triT"] = (ii[:, None] <= ii[None, :]).astype(np.float32)
    c["c_maskneg"] = np.where(ii[None, :] >= ii[:, None], 0.0, NEG).astype(np.float32)
    c["c_msl"] = (ii[None, :] < ii[:, None]).astype(np.float32)
    c["c_msu"] = (ii[:, None] < ii[None, :]).astype(np.float32)
    c["c_shift"] = (ii[None, :] == ii[:, None] + 1).astype(np.float32) - np.eye(128, dtype=np.float32)
    cm = np.zeros((128, 128), np.float32); cm[127, 0] = 1.0
    c["c_carry"] = cm
    same16 = (ii[:, None] // 16) == (ii[None, :] // 16)
    c["c_tri16"] = (same16 & (ii[:, None] <= ii[None, :])).astype(np.float32)
    c["c_blk16"] = same16.astype(np.float32)
    c["c_bm"] = ((ii[:, None] // 16) == np.arange(8)[None, :]).astype(np.float32)
    c["c_bones64"] = ((ii[:, None] // 64) == (ii[None, :] // 64)).astype(np.float32)
    return c


def make_params(inputs):
    p = {}
    cw = np.asarray(inputs["ssd_conv_w"], np.float32)
    cb = np.asarray(inputs["ssd_conv_b"], np.float32)
    pk = np.zeros((DEPTH, 128, 6, 5), np.float32)
    pk[:, :, :, 0:4] = cw.reshape(DEPTH, 4, 6, 128).transpose(0, 3, 2, 1)
    pk[:, :, :, 4] = cb.reshape(DEPTH, 6, 128).transpose(0, 2, 1)
    p["c_ssdconv"] = pk.reshape(DEPTH, 128, 30)
    p["c_rmucol"] = np.ascontiguousarray(np.asarray(inputs["mu_shift"], np.float32)[:, 768:896].reshape(DEPTH, 128, 1))
    vm = np.zeros((128, 1), np.float32); vm[0:32, 0] = np.asarray(inputs["mu_vres"], np.float32)[0]
    p["c_vmucol"] = vm
    p["c_rrk"] = np.ascontiguousarray(np.asarray(inputs["rwkv_r_k"], np.float32).reshape(DEPTH, 256))
    p["c_hgnw"] = np.ascontiguousarray(np.asarray(inputs["hgrn_norm_w"], np.float32).reshape(DEPTH, 2, 128).transpose(0, 2, 1))
    p["c_ssdD"] = np.repeat(np.asarray(inputs["ssd_D"], np.float32), 64, axis=1)
    return p


_CACHE = {}


SHARED = ("w_in", "w_out", "w_up", "w_down", "ln1_w", "ln1_b", "ln2_w", "ln2_b",
          "ssd_dt_bias", "ssd_A_log", "ssd_norm_w", "lower_bounds",
          "mu_shift", "rwkv_w0", "rwkv_a0", "rwkv_k_k", "rwkv_k_a", "rwkv_lnx_w", "rwkv_lnx_b", "rwkv_w2", "rwkv_a2",
          "rwkv_g2", "rwkv_v0", "rwkv_v2", "w_in_vres")


def make_inmap(inputs, b, consts=None, shared=None):
    if consts is None:
        consts = make_consts()
    if shared is None:
        shared = {k: np.ascontiguousarray(inputs[k], dtype=np.float32) for k in SHARED}
        shared.update(make_params(inputs))
    m = {"x": np.ascontiguousarray(inputs["x"][b], dtype=np.float32)}
    m.update(shared)
    m.update(consts)
    return m


def kernel(**inputs):
    if "prog" not in _CACHE:
        _CACHE["prog"] = build_program()
    nc, stats = _CACHE["prog"]
    consts = make_consts()
    shared = {k: np.ascontiguousarray(inputs[k], dtype=np.float32) for k in SHARED}
    shared.update(make_params(inputs))
    in_maps = [make_inmap(inputs, b, consts, shared) for b in range(8)]
    res = run_bass_kernel_spmd(nc, in_maps, core_ids=list(range(8)))
    return np.stack([np.asarray(r["out"], dtype=np.float32) for r in res.results], axis=0)
```

```python
import numpy as np
import concourse.bass as bass
import concourse.mybir as mybir
from concourse.bass_utils import run_bass_kernel_spmd

F32 = mybir.dt.float32
BF16 = mybir.dt.bfloat16
AF = mybir.ActivationFunctionType
ALU = mybir.AluOpType
AX = mybir.AxisListType

T = 2048
D = 1024
NT = 16
DEPTH = 2
ALPHA = (2.0 * DEPTH) ** 0.25
LN_EPS = 1e-5
RMS_EPS = 1e-5
IN_COLS = 3716
C_RWKV, C_ATT, C_SSD, C_HGRN = 0, 896, 1664, 2692
SLOPES = [2.0 ** (-8.0 * (h + 1) / 4) for h in range(4)]
NEG = -30000.0
STRICT = False


class Sched:
    LAT = 450.0

    def __init__(self, nc):
        self.nc = nc
        self.eng = {"pe": nc.tensor, "dve": nc.vector, "act": nc.scalar,
                    "pool": nc.gpsimd, "sp": nc.sync}
        self.ops = []
        self.info = []
        self.lastw = {}
        self.reads = {}
        self.fences = []
        self.reorder = True
        self.strict = STRICT

    @staticmethod
    def _key(a):
        if isinstance(a, (str, tuple)):
            return a
        return a.name

    def op(self, engine, fn, reads=(), writes=(), dma=False, cost=100.0):
        idx = len(self.ops)
        sem = set()
        order = set()
        rk = [self._key(a) for a in reads]
        wk = [self._key(a) for a in writes]
        for k in rk:
            if k in self.lastw:
                sem.add(self.lastw[k])
            if isinstance(k, str) and k.startswith("ps_"):
                for (e, i, d) in self.reads.get(k, ()):
                    if e != engine:
                        sem.add(i)
        for k in wk:
            if k in self.lastw:
                j = self.lastw[k]
                if dma or self.ops[j][3] or self.ops[j][0] != engine or (self.strict and engine != "pe"):
                    sem.add(j)
                else:
                    order.add(j)
            for (e, i, d) in self.reads.get(k, ()):
                if dma or d or e != engine or (self.strict and engine != "pe"):
                    sem.add(i)
                else:
                    order.add(i)
        sem.discard(idx)
        order.discard(idx)
        order -= sem
        self.info.append((engine, "dma" if dma else "", rk, wk))
        self.ops.append((engine, fn, sem, dma, order, float(cost)))
        for k in wk:
            self.lastw[k] = idx
            self.reads[k] = []
        for k in rk:
            self.reads.setdefault(k, []).append((engine, idx, dma))
        return idx

    def barrier(self):
        self.fences.append(len(self.ops))
        self.lastw = {}
        self.reads = {}

    def _schedule(self, seg):
        if not self.reorder or len(seg) < 3:
            return list(seg)
        ops = self.ops
        segset = set(seg)
        preds = {}
        succs = {i: [] for i in seg}
        indeg = {}
        for i in seg:
            p = [d for d in (ops[i][2] | ops[i][4]) if d in segset]
            preds[i] = p
            indeg[i] = len(p)
            for d in p:
                succs[d].append(i)
        finish = {}
        free = {e: 0.0 for e in self.eng}
        ready = {e: [] for e in self.eng}
        for i in seg:
            if indeg[i] == 0:
                ready[ops[i][0]].append((0.0, i))
        order = []
        nleft = len(seg)
        while nleft:
            best = None
            for e, lst in ready.items():
                if not lst:
                    continue
                f = free[e]
                cand = None
                for (dr, i) in lst:
                    st = dr if dr > f else f
                    key = (st, i)
                    if cand is None or key < cand:
                        cand = key
                if best is None or cand < best[0]:
                    best = (cand, e)
            (st, i), e = best
            ready[e] = [x for x in ready[e] if x[1] != i]
            eng, fn, sem, dma, od, cost = ops[i]
            if dma:
                issue = 1500.0 if e == "pool" else 150.0
                free[e] = st + issue
                finish[i] = st + issue + cost
            else:
                free[e] = st + cost
                finish[i] = st + cost
            order.append(i)
            nleft -= 1
            for s_ in succs[i]:
                indeg[s_] -= 1
                if indeg[s_] == 0:
                    dr = 0.0
                    for p in preds[s_]:
                        t = finish[p] + (self.LAT if (ops[p][0] != ops[s_][0] or p in ops[s_][2]) else 0.0)
                        if t > dr:
                            dr = t
                    ready[ops[s_][0]].append((dr, s_))
        mk = max(list(finish.values()) + [0.0])
        self.est_ns = getattr(self, "est_ns", 0.0) + mk
        busy = {e: 0.0 for e in self.eng}
        for i in seg:
            if not ops[i][3]:
                busy[ops[i][0]] += ops[i][5]
        self.seglog = getattr(self, "seglog", [])
        self.seglog.append((len(seg), round(mk / 1e3, 1), {e: round(b / 1e3, 1) for e, b in busy.items()}))
        return order

    def emit(self, limit=None):
        nc = self.nc
        ops = self.ops
        n = len(ops)
        elimit = limit
        emitted = 0
        bounds = [0] + [f for f in self.fences if f < n] + [n]
        segs = [list(range(bounds[i], bounds[i + 1])) for i in range(len(bounds) - 1)]
        sched = [self._schedule(sg) for sg in segs]
        need = [False] * len(ops)
        for i in range(n):
            for d in ops[i][2]:
                need[d] = True
        for od in sched:
            last = {}
            for i in od:
                if not ops[i][3]:
                    last[ops[i][0]] = i
            for i in last.values():
                need[i] = True
        NQ = {"sp": 16, "pool": 8, "act": 4}

        def run(dry, need):
            csem = dsem = None
            if not dry:
                csem = {e: nc.alloc_semaphore(f"sem_{e}") for e in self.eng}
                dsem = {q: [nc.alloc_semaphore(f"dsem_{q}_{i}") for i in range(k)] for q, k in NQ.items()}
            dcount = {q: [0] * k for q, k in NQ.items()}
            ndma = {q: 0 for q in NQ}
            sig = [None] * len(ops)
            sigop = {}
            used = set()
            ccount = {e: 0 for e in self.eng}
            seen = {e: {} for e in self.eng}
            nw = [0]
            emitted = 0

            def wait(e, key, val):
                if val <= 0 or seen[e].get(key, 0) >= val:
                    return
                seen[e][key] = val
                nw[0] += 1
                if (key, val) in sigop:
                    used.add(sigop[(key, val)])
                if not dry:
                    semh = dsem[key[1]][key[2]] if key[0] == "d" else csem[key[1]]
                    self.eng[e].wait_ge(semh, val)

            for si_, od in enumerate(sched):
                if si_ > 0:
                    for e in self.eng:
                        for e2 in self.eng:
                            wait(e, ("c", e2), ccount[e2])
                        for q in NQ:
                            for k in range(NQ[q]):
                                wait(e, ("d", q, k), dcount[q][k])
                for i in od:
                    if elimit is not None and emitted >= elimit:
                        break
                    emitted += 1
                    e, fn, deps, dma, _, _ = ops[i]
                    wants = {}
                    for d in deps:
                        s_ = sig[d]
                        if s_ is None:
                            continue
                        key, val = s_
                        if wants.get(key, 0) < val:
                            wants[key] = val
                    if dma:
                        si = ndma[e] % NQ[e]
                        if dcount[e][si] > 0:
                            key = ("d", e, si)
                            if wants.get(key, 0) < dcount[e][si]:
                                wants[key] = dcount[e][si]
                    for key, val in wants.items():
                        wait(e, key, val)
                    inst = None if dry else fn()
                    if dma:
                        si = ndma[e] % NQ[e]
                        ndma[e] += 1
                        dcount[e][si] += 16
                        if not dry:
                            inst.then_inc(dsem[e][si], 16)
                        sig[i] = (("d", e, si), dcount[e][si])
                    elif need[i]:
                        ccount[e] += 1
                        if not dry:
                            inst.then_inc(csem[e], 1)
                        sig[i] = (("c", e), ccount[e])
                        sigop[sig[i]] = i
            if not dry:
                for q in NQ:
                    for si in range(NQ[q]):
                        if dcount[q][si] > 0:
                            nc.sync.wait_ge(dsem[q][si], dcount[q][si])
            return used, nw[0], ndma

        used, _, _ = run(True, need)
        need2 = [False] * len(ops)
        for i in used:
            need2[i] = True
        used2, nwaits, ndma = run(False, need2)
        self.nsig = sum(need2)
        self.stats = dict(n=n, nwaits=nwaits, nsig=self.nsig, ndma=ndma, est_us=getattr(self, "est_ns", 0.0) / 1e3)


def _fs(ap):
    n = 1
    for d in ap.shape[1:]:
        n *= d
    return n


class KB:
    def __init__(self, nc):
        self.nc = nc
        self.S = Sched(nc)
        self.uid = 0
        self.stack = None
        self.marks = []

    def mark(self, name):
        self.marks.append((name, len(self.S.ops)))

    def sb(self, name, shape, dt=F32):
        if self.stack is None:
            return self.nc.alloc_sbuf_tensor(name, list(shape), dt)
        self.uid += 1
        return self.stack.enter_context(self.nc.sbuf_tensor(f"{name}_u{self.uid}", list(shape), dt))

    def phase_begin(self):
        import contextlib
        assert self.stack is None
        self.stack = contextlib.ExitStack()

    def phase_end(self):
        self.S.barrier()
        self.stack.close()
        self.stack = None

    def ps(self, name, shape, dt=F32):
        return self.nc.alloc_psum_tensor(name, list(shape), dt)

    def mm(self, out, lhsT, rhs, start=True, stop=True, rk=None, wk=None):
        nc = self.nc
        cost = max(64, _fs(rhs)) / 2.4 * (4 if lhsT.dtype == F32 else 1) + 45
        self.S.op("pe", lambda: nc.tensor.matmul(out, lhsT=lhsT, rhs=rhs, start=start, stop=stop),
                  reads=rk if rk is not None else [lhsT, rhs], writes=wk if wk is not None else [out], cost=cost)

    def tr(self, out, in_, ident, rk=None, wk=None):
        nc = self.nc
        self.S.op("pe", lambda: nc.tensor.transpose(out, in_, ident),
                  reads=rk if rk is not None else [in_, ident], writes=wk if wk is not None else [out], cost=110)

    def act(self, out, in_, func, bias=None, scale=None, accum=None, rk=None, wk=None):
        nc = self.nc
        kw = {}
        reads = [in_]
        if bias is not None:
            kw["bias"] = bias
            if not isinstance(bias, (int, float)):
                reads.append(bias)
        if scale is not None:
            kw["scale"] = scale
            if not isinstance(scale, (int, float)):
                reads.append(scale)
        writes = [out]
        if accum is not None:
            kw["accum_out"] = accum
            writes.append(accum)
        self.S.op("act", lambda: nc.scalar.activation(out=out, in_=in_, func=func, **kw),
                  reads=rk if rk is not None else reads, writes=wk if wk is not None else writes,
                  cost=(224 + _fs(out)) / 1.2)

    def tt(self, out, in0, in1, op, eng="dve", rk=None, wk=None):
        E = self.S.eng[eng]
        cost = (170 + _fs(out)) / 0.96 if eng == "dve" else (350 + 2.0 * _fs(out)) / 1.2
        self.S.op(eng, lambda: E.tensor_tensor(out=out, in0=in0, in1=in1, op=op),
                  reads=rk if rk is not None else [in0, in1], writes=wk if wk is not None else [out], cost=cost)

    def ts(self, out, in0, s1, op0, s2=None, op1=None, eng="dve", rk=None, wk=None):
        E = self.S.eng[eng]
        reads = [in0] + [s for s in (s1, s2) if s is not None and not isinstance(s, (int, float))]
        if op1 is None:
            fn = lambda: E.tensor_scalar(out=out, in0=in0, scalar1=s1, scalar2=None, op0=op0)
        else:
            fn = lambda: E.tensor_scalar(out=out, in0=in0, scalar1=s1, scalar2=s2, op0=op0, op1=op1)
        cost = (170 + 0.7 * _fs(out)) / 0.96 if eng == "dve" else (350 + 2.0 * _fs(out)) / 1.2
        self.S.op(eng, fn, reads=rk if rk is not None else reads, writes=wk if wk is not None else [out], cost=cost)

    def stt(self, out, in0, scalar, in1, op0, op1, rk=None, wk=None):
        nc = self.nc
        reads = [in0, in1] + ([] if isinstance(scalar, (int, float)) else [scalar])
        self.S.op("dve", lambda: nc.vector.scalar_tensor_tensor(out=out, in0=in0, scalar=scalar, in1=in1, op0=op0, op1=op1),
                  reads=rk if rk is not None else reads, writes=wk if wk is not None else [out],
                  cost=(170 + _fs(out)) / 0.96)

    def cp(self, out, in_, eng="dve", rk=None, wk=None):
        nc = self.nc
        if eng == "act":
            fn = lambda: nc.scalar.activation(out=out, in_=in_, func=AF.Copy)
        else:
            E = self.S.eng[eng]
            fn = lambda: E.tensor_copy(out=out, in_=in_)
        if eng == "act":
            cost = (224 + _fs(out)) / 1.2
        elif eng == "dve":
            cost = (170 + 0.7 * _fs(out)) / 0.96
        else:
            cost = (350 + 2.0 * _fs(out)) / 1.2
        self.S.op(eng, fn, reads=rk if rk is not None else [in_], writes=wk if wk is not None else [out], cost=cost)

    def recip(self, out, in_, rk=None, wk=None):
        nc = self.nc
        self.S.op("dve", lambda: nc.vector.reciprocal(out=out, in_=in_),
                  reads=rk if rk is not None else [in_], writes=wk if wk is not None else [out],
                  cost=(62 + 8 * _fs(out)) / 0.96)

    def memset(self, ap, val, eng="dve"):
        E = self.S.eng[eng]
        self.S.op(eng, lambda: E.memset(ap, val), writes=[ap], cost=(62 + _fs(ap)) / 0.96)

    def dma(self, out, in_, eng="sp", rk=(), wk=()):
        E = self.S.eng[eng]
        nbytes = out.shape[0] * _fs(out) * 4
        self.S.op(eng, lambda: E.dma_start(out=out, in_=in_), reads=list(rk), writes=list(wk), dma=True,
                  cost=2000 + nbytes / 120.0)


def build_program(debug=False, nlayers=DEPTH, parts="abcd", limit=None):
    nc = bass.Bass("TRN2", target_bir_lowering=False)
    K = KB(nc)

    def din(name, shape):
        return nc.dram_tensor(name, list(shape), F32, kind="ExternalInput").ap()

    x_d = din("x", [T, D])
    w_in_d = din("w_in", [DEPTH, D, IN_COLS])
    w_out_d = din("w_out", [DEPTH, D, D])
    w_up_d = din("w_up", [DEPTH, D, 4 * D])
    w_down_d = din("w_down", [DEPTH, 4 * D, D])
    ln1w_d = din("ln1_w", [DEPTH, D]); ln1b_d = din("ln1_b", [DEPTH, D])
    ln2w_d = din("ln2_w", [DEPTH, D]); ln2b_d = din("ln2_b", [DEPTH, D])
    ident_d = din("c_ident", [128, 128])
    amask_d = din("c_amask", [128, 5 * 4 * 128])
    triT_d = din("c_triT", [128, 128]); maskneg_d = din("c_maskneg", [128, 128])
    ssdconv_d = din("c_ssdconv", [DEPTH, 128, 30]); ssdD_d = din("c_ssdD", [DEPTH, 256])
    msl_d = din("c_msl", [128, 128]); msu_d = din("c_msu", [128, 128])
    shift_d = din("c_shift", [128, 128]); carry_d = din("c_carry", [128, 128])
    rmucol_d = din("c_rmucol", [DEPTH, 128, 1]); vmucol_d = din("c_vmucol", [128, 1])
    mush_d = din("mu_shift", [DEPTH, 896])
    rw0_d = din("rwkv_w0", [DEPTH, 256]); ra0_d = din("rwkv_a0", [DEPTH, 256]); rkk_d = din("rwkv_k_k", [DEPTH, 256])
    rka_d = din("rwkv_k_a", [DEPTH, 256]); rlw_d = din("rwkv_lnx_w", [DEPTH, 256]); rlb_d = din("rwkv_lnx_b", [DEPTH, 256])
    rrk_d = din("c_rrk", [DEPTH, 256]); rw2_d = din("rwkv_w2", [DEPTH, 32, 256]); ra2_d = din("rwkv_a2", [DEPTH, 32, 256])
    rg2_d = din("rwkv_g2", [DEPTH, 64, 256]); rv0_d = din("rwkv_v0", [1, 256]); rv2_d = din("rwkv_v2", [1, 32, 256])
    wvres_d = din("w_in_vres", [1, D, 32])
    vfirst_d = nc.dram_tensor("vfirst", [T, 256], F32, kind="Internal").ap()
    lscr_d = nc.dram_tensor("lscr", [128, T], BF16, kind="Internal").ap()
    vrscr_d = nc.dram_tensor("vrscr", [32, T], BF16, kind="Internal").ap()
    tri16_d = din("c_tri16", [128, 128]); blk16_d = din("c_blk16", [128, 128]); bm_d = din("c_bm", [128, 8])
    bones64_d = din("c_bones64", [128, 128]); hgnw_d = din("c_hgnw", [DEPTH, 128, 2]); lowb_d = din("lower_bounds", [DEPTH, 256])
    dtb_d = din("ssd_dt_bias", [DEPTH, 4]); alog_d = din("ssd_A_log", [DEPTH, 4]); ssdnw_d = din("ssd_norm_w", [DEPTH, 256])
    out_d = nc.dram_tensor("out", [T, D], F32, kind="ExternalOutput").ap()
    vscr_d = nc.dram_tensor("vscr", [T, 256], BF16, kind="Internal").ap()
    dbg_d = None
    if debug:
        dbg_d = nc.dram_tensor("dbg", [4, 128, 2 * T], BF16, kind="ExternalOutput").ap()

    xres = [K.sb(f"xres{t}", [128, D]) for t in range(NT)]
    xT = K.sb("xT", [128, 8, T], BF16)
    ident = K.sb("ident", [128, 128])
    ones_bf = K.sb("ones_bf", [128, 64], BF16)
    wbuf = [K.sb(f"wbuf{i}", [128, 8192], BF16) for i in range(2)]
    wstate = {"i": 0}
    psb = [K.ps(f"ps_{i}", [128, 512]) for i in range(7)]
    pbf = K.ps("ps_bf", [128, 1024], BF16)
    ident_bf = K.sb("ident_bf", [128, 128], BF16)
    triT = K.sb("triT", [128, 128]); ones_f = K.sb("ones_f", [128, 128]); maskneg = K.sb("maskneg", [128, 128])

    def xk(t0, t1):
        return [("xT", b) for b in range(t0 // 512, (t1 - 1) // 512 + 1)]
    XALL = [("xT", b) for b in range(4)]

    pre = {}

    def prefetch(tag, src_ap, kc, ncols):
        pre[tag] = wload(src_ap, kc, ncols)

    def getw(tag, src_ap, kc, ncols):
        if tag in pre:
            return pre.pop(tag)
        return wload(src_ap, kc, ncols)

    def wsmall(name, src_ap, kc, ncols):
        t_ = K.sb(name, [128, kc * ncols], BF16)
        view = t_[:, :].rearrange("p (k c) -> p k c", k=kc)
        K.dma(view, src_ap.rearrange("(k p) c -> p k c", p=128), eng="pool", rk=[t_], wk=[t_])
        return t_, view

    def wload(src_ap, kc, ncols, off=0, new=True):
        if new:
            wstate["i"] += 1
        wb = wbuf[wstate["i"] % 2]
        view = wb[:, off:off + kc * ncols].rearrange("p (k c) -> p k c", k=kc)
        K.dma(view, src_ap.rearrange("(k p) c -> p k c", p=128), eng="pool", rk=[wb], wk=[wb])
        return wb, view

    K.dma(ident[:], ident_d, wk=[ident])
    K.cp(ident_bf[:], ident[:])
    K.dma(triT[:], triT_d, wk=[triT])
    K.dma(maskneg[:], maskneg_d, wk=[maskneg])
    K.memset(ones_f[:], 1.0)
    K.memset(ones_bf[:], 1.0)
    mhalf = K.sb("mhalf", [128, 4])
    K.memset(mhalf[:], -0.5)

    for t in range(NT):
        K.dma(xres[t][:], x_d[t * 128:(t + 1) * 128, :], wk=[xres[t]])

    def build_xT(t):
        for half in range(2):
            pb = psb[5 + half]
            for j in range(4):
                kc = half * 4 + j
                K.tr(pb[:, j * 128:(j + 1) * 128], xres[t][:, kc * 128:(kc + 1) * 128], ident[:])
            K.cp(xT[:, half * 4:(half + 1) * 4, t * 128:(t + 1) * 128],
                 pb[:].rearrange("p (j c) -> p j c", j=4), eng=("act" if half else "dve"),
                 wk=xk(t * 128, (t + 1) * 128))

    for t in range(NT):
        build_xT(t)

    def layer_norm(t, w_t, b_t):
        xt = xres[t]
        stats = K.sb(f"lnst{K.uid}", [128, 12]); mv = K.sb(f"lnmv{K.uid}", [128, 2]); rs = K.sb(f"lnrs{K.uid}", [128, 1])
        K.uid += 1
        nc_ = nc
        K.S.op("dve", lambda: nc_.vector.bn_stats(out=stats[:, 0:6], in_=xt[:, 0:512]), reads=[xt], writes=[stats])
        K.S.op("dve", lambda: nc_.vector.bn_stats(out=stats[:, 6:12], in_=xt[:, 512:1024]), reads=[xt], writes=[stats])
        K.S.op("dve", lambda: nc_.vector.bn_aggr(out=mv[:], in_=stats[:]), reads=[stats], writes=[mv])
        K.ts(rs[:], mv[:, 1:2], LN_EPS, ALU.add)
        K.tt(rs[:], rs[:], mhalf[:, 0:1], ALU.pow, eng="pool")
        K.stt(mv[:, 1:2], mv[:, 0:1], -1.0, rs[:, 0:1], ALU.mult, ALU.mult)
        K.act(xt[:], xt[:], AF.Identity, bias=mv[:, 1:2], scale=rs[:, 0:1])
        K.tt(xt[:], xt[:], w_t[:], ALU.mult)
        K.tt(xt[:], xt[:], b_t[:], ALU.add, eng="pool")

    yT = K.sb("yT", [128, 2, T], BF16)
    wo_t = K.sb("wo_t", [128, 2 * D], BF16)

    def out_proj(l, m):
        wb = wo_t
        wv = wo_t[:, :].rearrange("p (k c) -> p k c", k=2)
        K.dma(wv, w_out_d[l, m * 256:(m + 1) * 256, :].rearrange("(k p) c -> p k c", p=128), eng="pool", rk=[wo_t], wk=[wo_t])
        for t in range(NT):
            for nb in range(2):
                pb = psb[4 + (t * 2 + nb) % 2]
                for c in range(2):
                    K.mm(pb[:, :], lhsT=yT[:, c, t * 128:(t + 1) * 128], rhs=wv[:, c, nb * 512:(nb + 1) * 512],
                         start=(c == 0), stop=(c == 1), rk=[yT, wb])
                K.tt(xres[t][:, nb * 512:(nb + 1) * 512], pb[:, :], xres[t][:, nb * 512:(nb + 1) * 512], ALU.add)

    def ffn(l):
        hT = [K.sb(f"hT{i}", [128, 4, T], BF16) for i in range(2)]
        rtmp = [K.sb(f"rtmp{i}", [128, 512]) for i in range(2)]
        for t in range(NT):
            K.act(xres[t][:], xres[t][:], AF.Copy, scale=ALPHA)
        for j in range(8):
            if j == 0 and ("f0", l) in pre:
                (wub, wuv), (wdb, wdv) = pre.pop(("f0", l))
            else:
                wub, wuv = wload(w_up_d[l, :, j * 512:(j + 1) * 512], 8, 512)
                wdb, wdv = wload(w_down_d[l, j * 512:(j + 1) * 512, :], 4, D, off=4096, new=False)
            h = hT[j % 2]
            cnt = 0
            for m in range(4):
                for nb in range(4):
                    pb = psb[cnt % 2]
                    rt = rtmp[cnt % 2]
                    cnt += 1
                    for kc in range(8):
                        K.mm(pb[:, :], lhsT=wuv[:, kc, m * 128:(m + 1) * 128], rhs=xT[:, kc, nb * 512:(nb + 1) * 512],
                             start=(kc == 0), stop=(kc == 7), rk=[wub, ("xT", nb)])
                    K.act(rt[:], pb[:, :], AF.Relu)
                    K.tt(h[:, m, nb * 512:(nb + 1) * 512], rt[:], rt[:], ALU.mult, eng="pool", wk=[(h.name, nb)])
            for t in range(NT):
                for nb in range(2):
                    pb = psb[2 + (t * 2 + nb) % 2]
                    for c in range(4):
                        K.mm(pb[:, :], lhsT=h[:, c, t * 128:(t + 1) * 128], rhs=wdv[:, c, nb * 512:(nb + 1) * 512],
                             start=(c == 0), stop=(c == 3), rk=[(h.name, t // 4), wdb])
                    K.tt(xres[t][:, nb * 512:(nb + 1) * 512], pb[:, :], xres[t][:, nb * 512:(nb + 1) * 512], ALU.add)

    def attention(l):
        K.mark("attn_begin")
        K.phase_begin()
        qT = K.sb("qT", [128, 2, T], BF16); kT = K.sb("kT", [128, 2, T], BF16)
        amask = K.sb("amask", [128, 2, 512])
        accn = K.sb("accn", [128, 2, T]); accd = K.sb("accd", [128, 2, T], BF16)
        Vb = K.sb("Vb", [128, 16, 256], BF16)
        Vn = [K.sb(f"Vn{i}", [128, 256], BF16) for i in range(2)]
        Ebuf = [K.sb(f"Ebuf{i}", [128, 512]) for i in range(2)]
        Pbuf = [K.sb(f"Pbuf{i}", [128, 512], BF16) for i in range(4)]
        amask_v = amask_d.rearrange("p (m f) -> p m f", m=5)
        wb, wv = getw(("a", l), w_in_d[l, :, C_ATT:C_ATT + 768], 8, 768)
        if "c" in parts:
            prefetch(("s", l), w_in_d[l, :, C_SSD:C_SSD + 1024], 8, 1024)
        cnt = 0
        for which, dst, scale in ((0, qT, 0.125), (1, kT, 1.0)):
            for c in range(2):
                for nb in range(4):
                    pb = psb[cnt % 2]; cnt += 1
                    for kc in range(8):
                        K.mm(pb[:, :], lhsT=wv[:, kc, which * 256 + c * 128: which * 256 + (c + 1) * 128],
                             rhs=xT[:, kc, nb * 512:(nb + 1) * 512], start=(kc == 0), stop=(kc == 7),
                             rk=[wb, ("xT", nb)])
                    K.act(dst[:, c, nb * 512:(nb + 1) * 512], pb[:, :], AF.Copy, scale=scale)
        K.mark("attn_V")
        for t in range(NT):
            pb = psb[t % 2]
            for kc in range(8):
                K.mm(pb[:, 0:256], lhsT=xT[:, kc, t * 128:(t + 1) * 128], rhs=wv[:, kc, 512:768],
                     start=(kc == 0), stop=(kc == 7), rk=[wb, ("xT", t // 4)])
            K.cp(Vn[t % 2][:], pb[:, 0:256], eng="act")
            K.dma(vscr_d[t * 128:(t + 1) * 128, :], Vn[t % 2][:], rk=[Vn[t % 2]], wk=["vscr"])
        ecnt = 0; qcnt = 0
        opbanks = [psb[6], pbf[:].bitcast(F32), psb[0], psb[1]]
        for bi, d in enumerate((1, 4, 16)):
            K.mark(f"attn_branch{bi}")
            nblk = T // d // 128
            if bi < 2:
                K.dma(amask[:], amask_v[:, 2 * bi:2 * bi + 2, :], rk=[amask], wk=[amask])
            else:
                K.dma(amask[:, 1, :], amask_v[:, 4, :], rk=[amask], wk=[amask])

            def tsl(r, n, d=d):
                st = r + d * 128 * n
                return slice(st, st + d * 127 + 1, d)
            if d == 1:
                for q in range(4):
                    K.dma(Vb[:, q * 4:(q + 1) * 4, :], vscr_d.rearrange("(n j) f -> j n f", j=128)[:, q * 4:(q + 1) * 4, :],
                          rk=["vscr", Vb], wk=[Vb])
            elif d == 4:
                for r in range(4):
                    K.dma(Vb[:, r * 4:(r + 1) * 4, :], vscr_d.rearrange("(n j r) f -> r j n f", n=4, j=128, r=4)[r],
                          rk=["vscr", Vb], wk=[Vb])
            else:
                for q in range(4):
                    K.dma(Vb[:, q * 4:(q + 1) * 4, :], vscr_d.rearrange("(j r) f -> j r f", r=16)[:, q * 4:(q + 1) * 4, :],
                          rk=["vscr", Vb], wk=[Vb])
            for r in range(d):
                for n in range(nblk):
                    qsl = tsl(r, n)
                    kbl = []
                    if n > 0:
                        kbl.append((r * nblk + n - 1, tsl(r, n - 1), 0))
                    kbl.append((r * nblk + n, qsl, 1))
                    Ps = []
                    for ki, (vidx, ksl, mid) in enumerate(kbl):
                        spa = psb[2 + 2 * ki]; spb = psb[3 + 2 * ki]
                        for h in range(4):
                            po = (h % 2) * 64
                            sp = spa if h % 2 == 0 else spb
                            K.mm(sp[:, (h // 2) * 128:(h // 2 + 1) * 128], lhsT=kT[po:po + 64, h // 2, ksl],
                                 rhs=qT[po:po + 64, h // 2, qsl])
                        E = Ebuf[ecnt % 2]; P = Pbuf[ecnt % 4]; ecnt += 1
                        K.tt(E[:, 0:256], spa[:, 0:256], amask[:, mid, 0:256], ALU.add)
                        K.tt(E[:, 256:512], spb[:, 0:256], amask[:, mid, 256:512], ALU.add)
                        K.act(P[:], E[:], AF.Exp)
                        Ps.append((vidx, P))
                    op = opbanks[qcnt % 4]; qcnt += 1
                    for h in range(4):
                        po = (h % 2) * 64; hh = h // 2
                        pc = (h % 2) * 256 + hh * 128
                        for ki, (vidx, P) in enumerate(Ps):
                            K.mm(op[po:po + 64, hh * 128:(hh + 1) * 128], lhsT=Vb[:, vidx, h * 64:(h + 1) * 64],
                                 rhs=P[:, pc:pc + 128], start=(ki == 0), stop=(ki == len(Ps) - 1))
                        for ki, (vidx, P) in enumerate(Ps):
                            K.mm(op[po:po + 64, (2 + hh) * 128:(3 + hh) * 128], lhsT=ones_bf[:, 0:64],
                                 rhs=P[:, pc:pc + 128], start=(ki == 0), stop=(ki == len(Ps) - 1))
                    nv = op[:, 0:256].rearrange("p (a q) -> p a q", a=2)
                    dv = op[:, 256:512].rearrange("p (a q) -> p a q", a=2)
                    if bi == 0:
                        K.cp(accn[:, :, qsl], nv, eng="act")
                        K.cp(accd[:, :, qsl], dv, eng="dve")
                    else:
                        K.tt(accn[:, :, qsl], nv, accn[:, :, qsl], ALU.add)
                        K.tt(accd[:, :, qsl], dv, accd[:, :, qsl], ALU.add)
        K.mark("attn_final")
        chunks = [(hh, q4) for hh in range(2) for q4 in range(4)]
        for p0 in range(0, 8, 2):
            pair = chunks[p0:p0 + 2]
            for i_, (hh, q4) in enumerate(pair):
                K.act(Ebuf[i_][:], accd[:, hh, q4 * 512:(q4 + 1) * 512], AF.Ln)
            for i_, (hh, q4) in enumerate(pair):
                K.act(Ebuf[i_][:], Ebuf[i_][:], AF.Exp, scale=-1.0)
            for i_, (hh, q4) in enumerate(pair):
                sl_ = slice(q4 * 512, (q4 + 1) * 512)
                K.tt(yT[:, hh, sl_], accn[:, hh, sl_], Ebuf[i_][:], ALU.mult)
        K.phase_end()


    def ssd(l):
        K.phase_begin()
        wb, wv = getw(("s", l), w_in_d[l, :, C_SSD:C_SSD + 1024], 8, 1024)
        wb2, wv2 = wsmall("wdt", w_in_d[l, :, C_SSD + 1024:C_SSD + 1028], 8, 4)
        if "d" in parts:
            prefetch(("h", l), w_in_d[l, :, C_HGRN:C_HGRN + 1024], 8, 1024)
        cpar = K.sb("cpar", [128, 30]); dtb = K.sb("dtb", [128, 4]); aneg = K.sb("aneg", [128, 4])
        Dfull = K.sb("Dfull", [128, 256]); normw = K.sb("normw", [128, 256])
        K.dma(cpar[:], ssdconv_d[l], wk=[cpar])
        K.dma(dtb[:], dtb_d[l:l + 1, :].partition_broadcast(128), wk=[dtb])
        K.dma(aneg[:], alog_d[l:l + 1, :].partition_broadcast(128), wk=[aneg])
        K.dma(Dfull[:], ssdD_d[l:l + 1, :].partition_broadcast(128), wk=[Dfull])
        K.dma(normw[:], ssdnw_d[l:l + 1, :].partition_broadcast(128), wk=[normw])
        K.act(aneg[:], aneg[:], AF.Exp)
        K.ts(aneg[:], aneg[:], -1.0, ALU.mult)
        xpads = [K.sb(f"xpad{i}", [128, T + 3], BF16) for i in range(2)]
        cdiag = K.sb("cdiag", [128, 24, 128], BF16)
        for cj in range(24):
            c_, j_ = divmod(cj, 4)
            K.act(cdiag[:, cj, :], ident[:], AF.Copy, scale=cpar[:, c_ * 5 + j_:c_ * 5 + j_ + 1])
        xsT = K.sb("xsT", [128, 2, T], BF16); BT = K.sb("BT", [128, 2, T], BF16); CT = K.sb("CT", [128, 2, T], BF16)
        for xp_ in xpads:
            K.memset(xp_[:, 0:3], 0.0)
        for c in range(6):
            xpad = xpads[c % 2]
            for nb in range(4):
                pb = psb[nb % 2]
                for kc in range(8):
                    K.mm(pb[:, :], lhsT=wv[:, kc, 256 + c * 128:256 + (c + 1) * 128], rhs=xT[:, kc, nb * 512:(nb + 1) * 512],
                         start=(kc == 0), stop=(kc == 7), rk=[wb, ("xT", nb)])
                K.act(xpad[:, 3 + nb * 512:3 + (nb + 1) * 512], pb[:, :], AF.Copy)
            dst = (xsT, BT, CT)[c // 2]
            for nb in range(4):
                pc = psb[2 + nb % 2]
                for j in range(4):
                    K.mm(pc[:, :], lhsT=cdiag[:, c * 4 + j, :], rhs=xpad[:, nb * 512 + j:nb * 512 + j + 512],
                         start=(j == 0), stop=(j == 3))
                K.act(dst[:, c % 2, nb * 512:(nb + 1) * 512], pc[:, :], AF.Silu, bias=cpar[:, c * 5 + 4:c * 5 + 5])
        dt = K.sb("dt", [128, 64]); dA = K.sb("dA", [128, 64]); cs = K.sb("cs", [128, 64]); ncs = K.sb("ncs", [128, 64])
        expcs = K.sb("expcs", [128, 64]); dst_ = K.sb("dsts", [128, 64]); cdec = K.sb("cdec", [128, 64])
        for t in range(NT):
            pb = psb[t % 2]
            for kc in range(8):
                K.mm(pb[:, 0:4], lhsT=xT[:, kc, t * 128:(t + 1) * 128], rhs=wv2[:, kc, 0:4],
                     start=(kc == 0), stop=(kc == 7), rk=[wb2, ("xT", t // 4)])
            K.tt(dt[:, t * 4:(t + 1) * 4], pb[:, 0:4], dtb[:], ALU.add)
        K.act(dt[:], dt[:], AF.Exp)
        K.act(dt[:], dt[:], AF.Ln, bias=1.0)
        K.tt(dA[:].rearrange("p (c h) -> p c h", h=4), dt[:].rearrange("p (c h) -> p c h", h=4),
             aneg[:].unsqueeze(1).to_broadcast([128, 16, 4]), ALU.mult)
        K.mm(psb[2][:, 0:64], lhsT=triT[:], rhs=dA[:])
        K.mm(psb[2][:, 64:128], lhsT=ones_f[:], rhs=dA[:])
        K.cp(cs[:], psb[2][:, 0:64])
        K.ts(ncs[:], cs[:], -1.0, ALU.mult)
        K.act(expcs[:], cs[:], AF.Exp)
        K.tt(dst_[:], psb[2][:, 64:128], cs[:], ALU.subtract)
        K.act(dst_[:], dst_[:], AF.Exp)
        K.act(cdec[:], psb[2][:, 64:128], AF.Exp)
        state = K.sb("sstate", [128, 256]); K.memset(state[:], 0.0)
        prevb = [K.sb(f"prevb{i}", [128, 256], BF16) for i in range(2)]
        Rb = [K.sb(f"Rb{i}", [128, 128]) for i in range(2)]
        decT = [K.sb(f"decT{i}", [128, 128]) for i in range(2)]
        MT = [K.sb(f"MT{i}", [128, 4, 128], BF16) for i in range(2)]
        xs_tm = [K.sb(f"xstm{i}", [128, 256]) for i in range(2)]
        xdt = [K.sb(f"xdt{i}", [128, 256], BF16) for i in range(2)]
        xdt2 = [K.sb(f"xdt2{i}", [128, 256], BF16) for i in range(2)]
        B_tm = [K.sb(f"Btm{i}", [128, 256], BF16) for i in range(2)]
        zs = [K.sb(f"zs{i}", [128, 256]) for i in range(2)]
        yy = [K.sb(f"yy{i}", [128, 256]) for i in range(2)]
        y2 = [K.sb(f"y2{i}", [128, 256]) for i in range(2)]
        ss = [K.sb(f"ssq{i}", [128, 2]) for i in range(2)]
        rc = 0
        for c in range(NT):
            b = c % 2
            tk = slice(c * 128, (c + 1) * 128)
            for g in range(2):
                K.mm(psb[3][:, g * 128:(g + 1) * 128], lhsT=BT[:, g, tk], rhs=CT[:, g, tk])
            for h in range(4):
                col = c * 4 + h
                R = Rb[rc % 2]; dT = decT[rc % 2]; rc += 1
                K.act(R[:], triT[:], AF.Copy, scale=dA[:, col:col + 1])
                K.mm(psb[4][:, 0:128], lhsT=ones_f[:], rhs=R[:], start=True, stop=False)
                K.mm(psb[4][:, 0:128], lhsT=ident[:], rhs=maskneg[:], start=False, stop=True)
                K.act(dT[:], psb[4][:, 0:128], AF.Exp, bias=ncs[:, col:col + 1])
                K.tt(MT[b][:, h, :], psb[3][:, (h // 2) * 128:(h // 2 + 1) * 128], dT[:], ALU.mult)
            for i, (src, cc) in enumerate(((xsT, 0), (xsT, 1), (BT, 0), (BT, 1))):
                K.tr(pbf[:, i * 128:(i + 1) * 128], src[:, cc, tk], ident_bf[:])
            K.cp(xs_tm[b][:], pbf[:, 0:256], eng="act")
            K.cp(B_tm[b][:], pbf[:, 256:512], eng="act")
            K.tt(xdt[b][:].rearrange("p (h e) -> p h e", h=4), xs_tm[b][:].rearrange("p (h e) -> p h e", h=4),
                 dt[:, c * 4:(c + 1) * 4].unsqueeze(2).to_broadcast([128, 4, 64]), ALU.mult)
            K.tt(xdt2[b][:].rearrange("p (h e) -> p h e", h=4), xdt[b][:].rearrange("p (h e) -> p h e", h=4),
                 dst_[:, c * 4:(c + 1) * 4].unsqueeze(2).to_broadcast([128, 4, 64]), ALU.mult)
            K.cp(prevb[b][:], state[:], eng="act")
            for h in range(4):
                K.mm(psb[5][:, h * 64:(h + 1) * 64], lhsT=MT[b][:, h, :], rhs=xdt[b][:, h * 64:(h + 1) * 64])
            for h in range(4):
                K.mm(psb[5][:, 256 + h * 64:256 + (h + 1) * 64], lhsT=CT[:, h // 2, tk], rhs=prevb[b][:, h * 64:(h + 1) * 64])
            for h in range(4):
                K.mm(psb[6][:, h * 64:(h + 1) * 64], lhsT=B_tm[b][:, (h // 2) * 128:(h // 2 + 1) * 128],
                     rhs=xdt2[b][:, h * 64:(h + 1) * 64])
            K.tt(state[:].rearrange("p (h e) -> p h e", h=4), state[:].rearrange("p (h e) -> p h e", h=4),
                 cdec[:, c * 4:(c + 1) * 4].unsqueeze(2).to_broadcast([128, 4, 64]), ALU.mult)
            K.tt(state[:], psb[6][:, 0:256], state[:], ALU.add)
            y = yy[b]
            K.tt(y[:].rearrange("p (h e) -> p h e", h=4), psb[5][:, 256:512].rearrange("p (h e) -> p h e", h=4),
                 expcs[:, c * 4:(c + 1) * 4].unsqueeze(2).to_broadcast([128, 4, 64]), ALU.mult)
            K.tt(y[:], psb[5][:, 0:256], y[:], ALU.add)
            K.tt(y2[b][:], xs_tm[b][:], Dfull[:], ALU.mult, eng="pool")
            K.tt(y[:], y[:], y2[b][:], ALU.add)
            pb = psb[c % 2]
            for kc in range(8):
                K.mm(pb[:, 0:256], lhsT=xT[:, kc, tk], rhs=wv[:, kc, 0:256], start=(kc == 0), stop=(kc == 7),
                     rk=[wb, ("xT", c // 4)])
            K.act(zs[b][:], pb[:, 0:256], AF.Tanh, scale=0.5)
            K.stt(zs[b][:], zs[b][:], 1.0, pb[:, 0:256], ALU.add, ALU.mult)
            K.stt(y[:], y[:], 0.5, zs[b][:], ALU.mult, ALU.mult)
            K.act(y2[b][:], y[:], AF.Square)
            nc_ = nc
            sq = y2[b]; sso = ss[b]
            K.S.op("dve", lambda sq=sq, sso=sso: nc_.vector.tensor_reduce(out=sso[:], in_=sq[:].rearrange("p (g e) -> p g e", g=2), axis=AX.X, op=ALU.add),
                   reads=[sq], writes=[sso])
            K.ts(sso[:], sso[:], 1.0 / 128, ALU.mult, RMS_EPS, ALU.add)
            K.tt(sso[:], sso[:], mhalf[:, 0:2], ALU.pow, eng="pool")
            for g in range(2):
                K.ts(y[:, g * 128:(g + 1) * 128], y[:, g * 128:(g + 1) * 128], sso[:, g:g + 1], ALU.mult)
            K.tt(y[:], y[:], normw[:], ALU.mult)
            for g in range(2):
                K.tr(psb[2][:, g * 128:(g + 1) * 128], y[:, g * 128:(g + 1) * 128], ident[:])
            K.cp(yT[:, :, tk], psb[2][:, 0:256].rearrange("p (g e) -> p g e", g=2))
        K.phase_end()


    def hgrn(l):
        K.phase_begin()
        wb, wv = getw(("h", l), w_in_d[l, :, C_HGRN:C_HGRN + 1024], 8, 1024)
        pre[("f0", l)] = (wload(w_up_d[l, :, 0:512], 8, 512), wload(w_down_d[l, 0:512, :], 4, D, off=4096, new=False))
        tri16 = K.sb("tri16", [128, 128]); blk16 = K.sb("blk16", [128, 128]); bm = K.sb("bm", [128, 8])
        bones = K.sb("bones", [128, 128]); hgnw = K.sb("hgnw", [128, 2])
        lbb = K.sb("lbb", [128, 256]); omlb = K.sb("omlb", [128, 256])
        K.dma(tri16[:], tri16_d, wk=[tri16]); K.dma(blk16[:], blk16_d, wk=[blk16]); K.dma(bm[:], bm_d, wk=[bm])
        dif16 = K.sb("dif16", [128, 128])
        K.tt(dif16[:], blk16[:], tri16[:], ALU.subtract)
        K.dma(bones[:], bones64_d, wk=[bones]); K.dma(hgnw[:], hgnw_d[l], wk=[hgnw])
        if l == 0:
            K.memset(lbb[:], 0.0)
        else:
            K.dma(lbb[:], lowb_d[1:2, :].partition_broadcast(128), wk=[lbb])
            K.dma(omlb[:], lowb_d[0:1, :].partition_broadcast(128), wk=[omlb])
            K.tt(lbb[:], lbb[:], omlb[:], ALU.subtract)
            K.act(lbb[:], lbb[:], AF.Sigmoid)
        K.ts(omlb[:], lbb[:], -0.5, ALU.mult, 0.5, ALU.add)
        K.tt(lbb[:], lbb[:], omlb[:], ALU.add)
        K.ts(hgnw[:], hgnw[:], 0.5, ALU.mult)
        epsc = K.sb("epsc", [128, 1]); K.memset(epsc[:], RMS_EPS)
        Sst = [K.sb(f"Sst{i}", [128, 9, 64]) for i in range(2)]; SbBD = K.sb("SbBD", [128, 8, 2, 128], BF16)
        for i in range(2):
            K.memset(Sst[i][:, 0, :], 0.0)
        K.memset(SbBD[:], 0.0, eng="pool")
        A = lambda nm, shp, dt_=F32: [K.sb(f"{nm}{i}", shp, dt_) for i in range(2)]
        sg = A("hsg", [128, 256]); fg = A("hfg", [128, 256]); logf = A("hlogf", [128, 256]); kk = A("hkk", [128, 256])
        qs = A("hqs", [128, 256]); V = A("hV", [128, 256], BF16); bb = A("hb", [128, 256]); eb = A("heb", [128, 256])
        enb = A("henb", [128, 256]); ed = A("hed", [128, 256]); tot = A("htot", [128, 256])
        qbar = A("hqbar", [128, 256]); kbar = A("hkbar", [128, 256]); kdec = A("hkdec", [128, 256], BF16)
        qkT = A("hqkT", [128, 4, 128], BF16); totT = A("htotT", [128, 2, 128]); attT = A("hattT", [128, 4, 128], BF16)
        kdb = A("hkdb", [128, 8, 256], BF16); sq = A("hsq", [128, 256]); rstd = A("hrstd", [128, 256]); oo = A("hoo", [128, 256])
        for t in range(NT):
            b = t % 2
            tk = slice(t * 128, (t + 1) * 128)
            if t > 0:
                for i in range(2):
                    K.cp(Sst[i][:, 0, :], Sst[i][:, 8, :], eng=("dve" if i == 0 else "pool"))
            for kc in range(8):
                K.mm(psb[0][:, :], lhsT=xT[:, kc, tk], rhs=wv[:, kc, 0:512], start=(kc == 0), stop=(kc == 7),
                     rk=[wb, ("xT", t // 4)])
            for kc in range(8):
                K.mm(psb[1][:, 0:256], lhsT=xT[:, kc, tk], rhs=wv[:, kc, 512:768], start=(kc == 0), stop=(kc == 7),
                     rk=[wb, ("xT", t // 4)])
            for c2 in range(2):
                for kc in range(8):
                    K.mm(psb[1][:, 256 + c2 * 128:256 + (c2 + 1) * 128], lhsT=wv[:, kc, 768 + c2 * 128:768 + (c2 + 1) * 128],
                         rhs=xT[:, kc, tk], start=(kc == 0), stop=(kc == 7), rk=[wb, ("xT", t // 4)])
            K.act(sg[b][:], psb[1][:, 256:512], AF.Tanh, scale=0.5)
            K.stt(sg[b][:], sg[b][:], 1.0, psb[1][:, 256:512], ALU.add, ALU.mult)
            for hh in range(2):
                K.ts(sg[b][:, hh * 128:(hh + 1) * 128], sg[b][:, hh * 128:(hh + 1) * 128], hgnw[:, hh:hh + 1], ALU.mult)
            K.act(fg[b][:], psb[0][:, 256:512], AF.Tanh, scale=0.5)
            K.tt(fg[b][:], fg[b][:], omlb[:], ALU.mult)
            K.tt(fg[b][:], fg[b][:], lbb[:], ALU.add)
            K.act(logf[b][:], fg[b][:], AF.Ln)
            K.ts(kk[b][:], fg[b][:], -1.0, ALU.mult, 1.0, ALU.add)
            K.act(qs[b][:], psb[0][:, 0:256], AF.Tanh, scale=0.5)
            K.stt(qs[b][:], qs[b][:], 1.0, psb[0][:, 0:256], ALU.add, ALU.mult)
            K.cp(V[b][:], psb[1][:, 0:256], eng="act")
            K.mm(psb[2][:, 0:256], lhsT=tri16[:], rhs=logf[b][:])
            K.mm(psb[2][:, 256:512], lhsT=blk16[:], rhs=logf[b][:])
            K.mm(psb[3][:, 0:256], lhsT=dif16[:], rhs=logf[b][:])
            K.act(eb[b][:], psb[2][:, 0:256], AF.Exp)
            K.act(enb[b][:], psb[2][:, 0:256], AF.Exp, scale=-1.0)
            K.act(ed[b][:], psb[3][:, 0:256], AF.Exp)
            K.act(tot[b][:], psb[2][:, 256:512], AF.Exp)
            K.stt(qbar[b][:], qs[b][:], 0.5, eb[b][:], ALU.mult, ALU.mult)
            K.tt(kbar[b][:], kk[b][:], enb[b][:], ALU.mult, eng="pool")
            K.tt(kdec[b][:], kk[b][:], ed[b][:], ALU.mult)
            for i, src in enumerate((qbar[b], qbar[b], kbar[b], kbar[b])):
                K.tr(psb[3][:, i * 128:(i + 1) * 128], src[:, (i % 2) * 128:(i % 2 + 1) * 128], ident[:])
            for i in range(2):
                K.tr(psb[2][:, i * 128:(i + 1) * 128], tot[b][:, i * 128:(i + 1) * 128], ident[:])
            K.cp(qkT[b][:].rearrange("p a t -> p (a t)"), psb[3][:, :], eng="act")
            K.cp(totT[b][:].rearrange("p a t -> p (a t)"), psb[2][:, 0:256], eng="act")
            for h in range(4):
                po = (h % 2) * 64; hh = h // 2
                bank = psb[4] if h % 2 == 0 else psb[5]
                K.mm(bank[:, hh * 128:(hh + 1) * 128], lhsT=qkT[b][po:po + 64, 2 + hh, :], rhs=qkT[b][po:po + 64, hh, :])
            for hl in range(2):
                bank = psb[4] if hl == 0 else psb[5]
                K.tt(attT[b][:, hl * 2:(hl + 1) * 2, :], bank[:, 0:256].rearrange("p (a t) -> p a t", a=2),
                     tri16[:].unsqueeze(1).to_broadcast([128, 2, 128]), ALU.mult)
            K.tt(kdb[b][:], kdec[b][:].unsqueeze(1).to_broadcast([128, 8, 256]),
                 bm[:].unsqueeze(2).to_broadcast([128, 8, 256]), ALU.mult)
            for half in range(2):
                for c in range(half * 4, half * 4 + 4):
                    for h in range(4):
                        hl = h % 2; hh = h // 2
                        co = ((c % 4) * 2 + hh) * 64
                        K.mm(psb[6][hl * 64:(hl + 1) * 64, co:co + 64], lhsT=kdb[b][:, c, h * 64:(h + 1) * 64],
                             rhs=V[b][:, h * 64:(h + 1) * 64])
                for c in range(half * 4, half * 4 + 4):
                    for hh in range(2):
                        co = ((c % 4) * 2 + hh) * 64
                        K.stt(Sst[hh][:, c + 1, :], Sst[hh][:, c, :], totT[b][:, hh, c * 16:c * 16 + 1], psb[6][:, co:co + 64],
                              ALU.mult, ALU.add)
            for hh in range(2):
                K.cp(SbBD[0:64, :, hh, 0:64], Sst[hh][0:64, 0:8, :], eng="act")
                K.cp(SbBD[64:128, :, hh, 64:128], Sst[hh][64:128, 0:8, :])
            for hh in range(2):
                for hl in range(2):
                    h = 2 * hh + hl
                    K.mm(psb[4][hl * 64:(hl + 1) * 64, 256 + hh * 128:256 + (hh + 1) * 128], lhsT=V[b][:, h * 64:(h + 1) * 64],
                         rhs=attT[b][:, hl * 2 + hh, :], start=True, stop=False)
                for c in range(8):
                    K.mm(psb[4][:, 256 + hh * 128 + c * 16:256 + hh * 128 + (c + 1) * 16], lhsT=SbBD[:, c, hh, :],
                         rhs=qkT[b][:, hh, c * 16:(c + 1) * 16], start=False, stop=(c == 7))
            K.act(sq[b][:], psb[4][:, 256:512], AF.Square)
            K.mm(psb[5][:, 256:512], lhsT=bones[:], rhs=sq[b][:])
            K.act(rstd[b][:], psb[5][:, 256:512], AF.Ln, bias=epsc[:, 0:1], scale=1.0 / 64)
            K.act(rstd[b][:], rstd[b][:], AF.Exp, scale=-0.5)
            K.tt(oo[b][:], psb[4][:, 256:512], rstd[b][:], ALU.mult)
            K.tt(yT[:, :, tk], oo[b][:].rearrange("p (a t) -> p a t", a=2), sg[b][:].rearrange("p (a t) -> p a t", a=2), ALU.mult)
        K.phase_end()


    def rwkv(l):
        K.phase_begin()
        v4 = lambda ap: ap.rearrange("p (h e) -> p h e", h=4)
        b4 = lambda ap: ap.unsqueeze(2).to_broadcast([128, 4, 64])
        wb, wv = getw(("r", l), w_in_d[l, :, 0:896], 8, 896)
        if "b" in parts:
            prefetch(("a", l), w_in_d[l, :, C_ATT:C_ATT + 768], 8, 768)
        names = {}

        def bc(name, src, n=256):
            t_ = K.sb(name, [128, n]); K.dma(t_[:], src.partition_broadcast(128), wk=[t_]); return t_
        mucol = K.sb("mucol", [128, 1]); omucol = K.sb("omucol", [128, 1])
        K.dma(mucol[:], rmucol_d[l], wk=[mucol])
        K.ts(omucol[:], mucol[:], -1.0, ALU.mult, 1.0, ALU.add)
        fT = K.sb("fT", [128, T + 1]); loraT = K.sb("loraT", [128, T], BF16); ftmp = K.sb("ftmp", [128, T])
        K.memset(fT[:, 0:1], 0.0)
        for nb in range(4):
            pb = psb[nb % 2]
            for kc in range(8):
                K.mm(pb[:, :], lhsT=wv[:, kc, 768:896], rhs=xT[:, kc, nb * 512:(nb + 1) * 512], start=(kc == 0), stop=(kc == 7),
                     rk=[wb, ("xT", nb)])
            K.act(fT[:, 1 + nb * 512:1 + (nb + 1) * 512], pb[:, :], AF.Copy)
        K.ts(ftmp[:], fT[:, 0:T], mucol[:, 0:1], ALU.mult)
        K.stt(ftmp[:], fT[:, 1:T + 1], omucol[:, 0:1], ftmp[:], ALU.mult, ALU.add)
        K.act(loraT[0:32, :], ftmp[0:32, :], AF.Tanh)
        K.act(loraT[32:64, :], ftmp[32:64, :], AF.Copy)
        K.act(ftmp[64:128, :], ftmp[64:128, :], AF.Tanh, scale=0.5)
        K.ts(loraT[64:128, :], ftmp[64:128, :], 0.5, ALU.mult, 0.5, ALU.add)
        K.dma(lscr_d, loraT[:], rk=[loraT], wk=["lscr"])
        if l > 0:
            wb2, wv2 = wsmall("wvres", wvres_d[0], 8, 32)
            vmu = K.sb("vmu", [128, 1]); ovmu = K.sb("ovmu", [128, 1])
            K.dma(vmu[:], vmucol_d, wk=[vmu])
            K.ts(ovmu[:], vmu[:], -1.0, ALU.mult, 1.0, ALU.add)
            vrT = K.sb("vrT", [32, T], BF16)
            for nb in range(4):
                pb = psb[nb % 2]
                for kc in range(8):
                    K.mm(pb[0:32, :], lhsT=wv2[:, kc, 0:32], rhs=xT[:, kc, nb * 512:(nb + 1) * 512], start=(kc == 0), stop=(kc == 7),
                         rk=[wb2, ("xT", nb)])
                K.act(fT[0:32, 1 + nb * 512:1 + (nb + 1) * 512], pb[0:32, :], AF.Copy)
            K.ts(ftmp[0:32, :], fT[0:32, 0:T], vmu[0:32, 0:1], ALU.mult)
            K.stt(ftmp[0:32, :], fT[0:32, 1:T + 1], ovmu[0:32, 0:1], ftmp[0:32, :], ALU.mult, ALU.add)
            K.act(vrT[:], ftmp[0:32, :], AF.Copy)
            K.dma(vrscr_d, vrT[:], rk=[vrT], wk=["vrscr"])
        K.phase_end()
        K.phase_begin()
        mu_b = bc("mu_b", mush_d[l:l + 1, 0:768], 768)
        w0_b = bc("w0_b", rw0_d[l:l + 1, :]); a0_b = bc("a0_b", ra0_d[l:l + 1, :]); kk_b = bc("kk_b", rkk_d[l:l + 1, :])
        ka_b = bc("ka_b", rka_d[l:l + 1, :]); lw_b = bc("lnxw_b", rlw_d[l:l + 1, :]); lb_b = bc("lnxb_b", rlb_d[l:l + 1, :])
        rk_b = bc("rk_b", rrk_d[l:l + 1, :])
        msl = K.sb("msl", [128, 128]); msu = K.sb("msu", [128, 128])
        shiftM = K.sb("shiftM", [128, 128], BF16); carryM = K.sb("carryM", [128, 128], BF16)
        for t_, d_ in ((msl, msl_d), (msu, msu_d)):
            K.dma(t_[:], d_, wk=[t_])
        for t_, d_ in ((shiftM, shift_d), (carryM, carry_d)):
            K.dma(t_[:], d_, eng="pool", wk=[t_])
        loraW = K.sb("loraW", [128, 768], BF16)
        K.memset(loraW[:], 0.0, eng="pool")
        K.dma(loraW[0:32, 0:256], rw2_d[l], eng="pool", rk=[loraW], wk=[loraW])
        K.dma(loraW[32:64, 256:512], ra2_d[l], eng="pool", rk=[loraW], wk=[loraW])
        K.dma(loraW[64:128, 512:768], rg2_d[l], eng="pool", rk=[loraW], wk=[loraW])
        loraTc = [K.sb(f"loraTc{i}", [128, 128], BF16) for i in range(2)]
        if l > 0:
            v0_b = bc("v0_b", rv0_d[0:1, :])
            v2W = K.sb("v2W", [32, 256], BF16)
            K.dma(v2W[:], rv2_d[0], eng="pool", wk=[v2W])
            vrTc = [K.sb(f"vrTc{i}", [32, 128], BF16) for i in range(2)]
        F = lambda nm, n=256, dt_=F32: K.sb(nm, [128, n], dt_)
        D2 = lambda nm, n=256, dt_=F32: [K.sb(f"{nm}{i}", [128, n], dt_) for i in range(2)]
        fsb = D2("fsb", 768, BF16)
        fl = F("fl", 768)
        lw = F("lw"); aa = F("aa"); vv = F("vv"); kkn = F("kkn"); t1 = F("t1"); t2 = F("t2"); t3 = F("t3"); kt = F("kt"); be = F("be")
        s4 = K.sb("s4", [128, 4]); s4b = K.sb("s4b", [128, 4])
        Lsb = F("Lsb"); E1 = F("E1"); E2 = F("E2"); E3 = F("E3"); E4 = F("E4")
        gg2 = D2("gg"); bonus2 = D2("bonus")
        btil2 = D2("btil", 256, BF16); ktil2 = D2("ktil", 256, BF16); Vb2 = D2("rVb", 256, BF16)
        gam2 = [K.sb(f"gam{i}", [128, 2]) for i in range(2)]
        abkrT2 = [K.sb(f"abkrT{i}", [128, 8, 128], BF16) for i in range(2)]
        Am = [K.sb(f"Am{i}", [128, 4, 128], BF16) for i in range(2)]
        Bm = [K.sb(f"Bm{i}", [128, 4, 128], BF16) for i in range(2)]
        AKT2 = [K.sb(f"AKT{i}", [128, 4, 128], BF16) for i in range(2)]
        RBT2 = [K.sb(f"RBT{i}", [128, 4, 128], BF16) for i in range(2)]
        RKT2 = [K.sb(f"RKT{i}", [128, 4, 128], BF16) for i in range(2)]
        Pb2 = [K.sb(f"Pb{i}", [128, 4, 128], BF16) for i in range(2)]
        Sf = K.sb("Sf", [128, 2, 64]); SBD = K.sb("SBD", [128, 2, 128], BF16)
        K.memset(Sf[:], 0.0); K.memset(SBD[:], 0.0)
        Wb_ = F("Wb", 256, BF16); Ub_ = F("Ub", 256, BF16); yv = F("yv"); ysq = F("ysq", 256, BF16)
        mean4 = K.sb("mean4", [128, 4]); var4 = K.sb("var4", [128, 4])
        m3 = lambda mk: mk[:].unsqueeze(1).to_broadcast([128, 2, 128])
        X = pbf[:].bitcast(F32)
        for c in range(NT):
            par = c % 2
            tk = slice(c * 128, (c + 1) * 128)
            f = fsb[c % 2]; fp_ = fsb[(c + 1) % 2]
            gg = gg2[par]; bonus = bonus2[par]; btil = btil2[par]; ktil = ktil2[par]; Vb_ = Vb2[par]; gam = gam2[par]
            abkrT = abkrT2[par]; AKT = AKT2[par]; RBT = RBT2[par]; RKT = RKT2[par]; Pb = Pb2[par]
            for (bank, o0, c0, c1) in ((psb[0], 0, 0, 512), (psb[1], 0, 512, 768)):
                for kc in range(8):
                    K.mm(bank[:, o0:o0 + c1 - c0], lhsT=xT[:, kc, tk], rhs=wv[:, kc, c0:c1], start=(kc == 0), stop=(kc == 7),
                         rk=[wb, ("xT", c // 4)])
            K.cp(f[:, 0:512], psb[0][:, :], eng="act")
            K.cp(f[:, 512:768], psb[1][:, 0:256], eng="act")
            for (bank, o0, c0, c1) in ((psb[2], 0, 0, 512), (psb[1], 256, 512, 768)):
                K.mm(bank[:, o0:o0 + c1 - c0], lhsT=shiftM[:], rhs=f[:, c0:c1], start=True, stop=(c == 0))
                if c > 0:
                    K.mm(bank[:, o0:o0 + c1 - c0], lhsT=carryM[:], rhs=fp_[:, c0:c1], start=False, stop=True)
            K.tt(fl[:, 0:512], psb[2][:, :], mu_b[:, 0:512], ALU.mult)
            K.tt(fl[:, 512:768], psb[1][:, 256:512], mu_b[:, 512:768], ALU.mult)
            K.tt(fl[:], fl[:], f[:], ALU.add, eng="pool")
            r_ = fl[:, 0:256]; k_ = fl[:, 256:512]; v_ = fl[:, 512:768]
            lt = loraTc[c % 2]
            K.dma(lt[:], lscr_d[:, tk], rk=["lscr", lt], wk=[lt])
            K.mm(psb[3][:, 0:512], lhsT=lt[:], rhs=loraW[:, 0:512])
            K.mm(psb[0][:, 0:256], lhsT=lt[:], rhs=loraW[:, 512:768])
            K.tt(lw[:], psb[3][:, 0:256], w0_b[:], ALU.add)
            K.act(lw[:], lw[:], AF.Tanh, scale=0.5)
            K.ts(lw[:], lw[:], -0.3032653298563167, ALU.mult, -0.3032653298563167, ALU.add, eng="pool")
            K.tt(aa[:], psb[3][:, 256:512], a0_b[:], ALU.add)
            K.act(aa[:], aa[:], AF.Tanh, scale=0.5)
            K.ts(aa[:], aa[:], 0.5, ALU.mult, 0.5, ALU.add, eng="pool")
            K.cp(gg[:], psb[0][:, 0:256], eng="act")
            if l == 0:
                K.cp(vv[:], v_, eng="pool")
                K.dma(vfirst_d[tk, :], vv[:], rk=[vv], wk=[("vfirst", c)])
            else:
                vt_ = vrTc[c % 2]
                K.dma(vt_[:], vrscr_d[:, tk], rk=["vrscr", vt_], wk=[vt_])
                K.mm(psb[0][:, 256:512], lhsT=vt_[0:32, :], rhs=v2W[0:32, :])
                K.tt(t1[:], psb[0][:, 256:512], v0_b[:], ALU.add)
                K.act(t1[:], t1[:], AF.Tanh, scale=0.5)
                K.ts(t1[:], t1[:], 0.5, ALU.mult, 0.5, ALU.add, eng="pool")
                K.dma(t2[:], vfirst_d[tk, :], rk=[("vfirst", c), t2], wk=[t2])
                K.tt(t2[:], t2[:], v_, ALU.subtract, eng="pool")
                K.tt(t2[:], t2[:], t1[:], ALU.mult, eng="pool")
                K.tt(vv[:], t2[:], v_, ALU.add, eng="pool")
            K.cp(Vb_[:], vv[:], eng="act")
            K.tt(kkn[:], k_, kk_b[:], ALU.mult, eng="pool")
            K.act(t1[:], kkn[:], AF.Square)
            K.S.op("dve", lambda t1=t1: nc.vector.tensor_reduce(out=s4[:], in_=v4(t1[:]), axis=AX.X, op=ALU.add), reads=[t1], writes=[s4])
            K.ts(s4[:], s4[:], 1e-24, ALU.max)
            K.tt(s4[:], s4[:], mhalf[:, 0:4], ALU.pow, eng="pool")
            K.tt(v4(kkn[:]), v4(kkn[:]), b4(s4[:]), ALU.mult)
            K.stt(t3[:], aa[:], -1.0, ka_b[:], ALU.add, ALU.mult)
            K.stt(kt[:], t3[:], 1.0, k_, ALU.add, ALU.mult)
            K.tt(be[:], kkn[:], aa[:], ALU.mult, eng="pool")
            K.tt(t3[:], r_, kt[:], ALU.mult, eng="pool")
            K.tt(t3[:], t3[:], rk_b[:], ALU.mult, eng="pool")
            K.S.op("dve", lambda t3=t3: nc.vector.tensor_reduce(out=s4b[:], in_=v4(t3[:]), axis=AX.X, op=ALU.add), reads=[t3], writes=[s4b])
            K.tt(v4(bonus[:]), v4(vv[:]), b4(s4b[:]), ALU.mult)
            K.mm(psb[2][:, 0:256], lhsT=triT[:], rhs=lw[:])
            K.mm(psb[2][:, 256:512], lhsT=msu[:], rhs=lw[:])
            K.mm(psb[3][:, 256:512], lhsT=msl[:], rhs=lw[:])
            for hh in range(2):
                K.mm(psb[3][:, hh:hh + 1], lhsT=lw[:, hh * 128:(hh + 1) * 128], rhs=ones_f[:, 0:1])
            K.act(gam[:], psb[3][:, 0:2], AF.Exp)
            K.act(E1[:], psb[2][:, 256:512], AF.Exp)
            K.act(E2[:], psb[2][:, 0:256], AF.Exp, scale=-1.0)
            K.act(E3[:], psb[2][:, 0:256], AF.Exp)
            K.act(E4[:], psb[3][:, 256:512], AF.Exp)
            K.tt(btil[:], be[:], E4[:], ALU.mult, eng="pool")
            K.tt(ktil[:], kt[:], E4[:], ALU.mult)
            K.stt(E1[:], kkn[:], -1.0, E1[:], ALU.mult, ALU.mult)
            K.tt(be[:], be[:], E2[:], ALU.mult, eng="pool")
            K.tt(kt[:], kt[:], E2[:], ALU.mult)
            K.tt(E3[:], r_, E3[:], ALU.mult, eng="pool")
            for qi, src in enumerate((E1, be, kt, E3)):
                bank = psb[0] if qi < 2 else psb[1]
                for hh in range(2):
                    K.tr(bank[:, ((qi % 2) * 2 + hh) * 128:((qi % 2) * 2 + hh + 1) * 128], src[:, hh * 128:(hh + 1) * 128], ident[:])
            K.cp(abkrT[:, 0:4, :].rearrange("p a t -> p (a t)"), psb[0][:, :], eng="act")
            K.cp(abkrT[:, 4:8, :].rearrange("p a t -> p (a t)"), psb[1][:, :])
            aT = lambda h, abkrT=abkrT: abkrT[(h % 2) * 64:(h % 2) * 64 + 64, 0 + h // 2, :]
            bT = lambda h, abkrT=abkrT: abkrT[(h % 2) * 64:(h % 2) * 64 + 64, 2 + h // 2, :]
            kT_ = lambda h, abkrT=abkrT: abkrT[(h % 2) * 64:(h % 2) * 64 + 64, 4 + h // 2, :]
            rT = lambda h, abkrT=abkrT: abkrT[(h % 2) * 64:(h % 2) * 64 + 64, 6 + h // 2, :]

            def amat(L_, R_, mask, dst):
                for h in range(4):
                    bank = psb[4] if h % 2 == 0 else psb[5]
                    K.mm(bank[:, (h // 2) * 128:(h // 2 + 1) * 128], lhsT=L_(h), rhs=R_(h))
                for hl in range(2):
                    bank = psb[4] if hl == 0 else psb[5]
                    K.tt(dst[:, hl * 2:hl * 2 + 2, :], bank[:, 0:256].rearrange("p (a t) -> p a t", a=2), m3(mask), ALU.mult)
            hidx = lambda h: (h % 2) * 2 + h // 2
            amat(aT, bT, msl, Am[0])
            amat(bT, aT, msu, Bm[0])
            amat(kT_, aT, msu, AKT)
            amat(bT, rT, triT, RBT)
            amat(kT_, rT, triT, RKT)
            K.tt(Pb[:], Bm[0][:], ident[:].unsqueeze(1).to_broadcast([128, 4, 128]), ALU.add, eng="pool")
            cur = 0
            for j in range(1, 7):
                nxt = 1 - cur
                for hi in range(4):
                    K.mm(psb[4][:, hi * 128:(hi + 1) * 128], lhsT=Bm[cur][:, hi, :], rhs=Am[cur][:, hi, :])
                if j < 6:
                    for hi in range(4):
                        K.mm(psb[5][:, hi * 128:(hi + 1) * 128], lhsT=Am[cur][:, hi, :], rhs=Bm[cur][:, hi, :])
                K.cp(Am[nxt][:].rearrange("p a t -> p (a t)"), psb[4][:, :], eng="act")
                if j < 6:
                    K.cp(Bm[nxt][:].rearrange("p a t -> p (a t)"), psb[5][:, :])
                for hi in range(4):
                    K.mm(psb[6][:, hi * 128:(hi + 1) * 128], lhsT=Am[nxt][:, hi, :], rhs=Pb[:, hi, :])
                K.tt(Pb[:].rearrange("p a t -> p (a t)"), psb[6][:, :], Pb[:].rearrange("p a t -> p (a t)"), ALU.add)
                cur = nxt
            for hh in range(2):
                K.mm(X[:, hh * 128:(hh + 1) * 128], lhsT=abkrT[:, 0 + hh, :], rhs=SBD[:, hh, :], start=True, stop=False)
                for hl in range(2):
                    h = 2 * hh + hl
                    K.mm(X[:, h * 64:(h + 1) * 64], lhsT=AKT[:, hidx(h), :], rhs=Vb_[:, h * 64:(h + 1) * 64], start=False, stop=(hl == 1))
            K.cp(Wb_[:], X[:, 0:256], eng="act")
            for h in range(4):
                K.mm(X[:, 256 + h * 64:256 + (h + 1) * 64], lhsT=Pb[:, hidx(h), :], rhs=Wb_[:, h * 64:(h + 1) * 64])
            K.cp(Ub_[:], X[:, 256:512], eng="act")
            for hh in range(2):
                K.mm(X[:, hh * 128:(hh + 1) * 128], lhsT=abkrT[:, 6 + hh, :], rhs=SBD[:, hh, :], start=True, stop=False)
                for hl in range(2):
                    h = 2 * hh + hl
                    K.mm(X[:, h * 64:(h + 1) * 64], lhsT=RBT[:, hidx(h), :], rhs=Ub_[:, h * 64:(h + 1) * 64], start=False, stop=False)
                    K.mm(X[:, h * 64:(h + 1) * 64], lhsT=RKT[:, hidx(h), :], rhs=Vb_[:, h * 64:(h + 1) * 64], start=False, stop=(hl == 1))
            for h in range(4):
                hl = h % 2; hh = h // 2
                K.mm(X[hl * 64:(hl + 1) * 64, 256 + hh * 64:256 + (hh + 1) * 64], lhsT=btil[:, h * 64:(h + 1) * 64],
                     rhs=Ub_[:, h * 64:(h + 1) * 64], start=True, stop=False)
                K.mm(X[hl * 64:(hl + 1) * 64, 256 + hh * 64:256 + (hh + 1) * 64], lhsT=ktil[:, h * 64:(h + 1) * 64],
                     rhs=Vb_[:, h * 64:(h + 1) * 64], start=False, stop=True)
            K.cp(yv[:], X[:, 0:256], eng="act")
            for hh in range(2):
                K.stt(Sf[:, hh, :], Sf[:, hh, :], gam[:, hh:hh + 1], X[:, 256 + hh * 64:256 + (hh + 1) * 64], ALU.mult, ALU.add)
            K.cp(SBD[0:64, :, 0:64], Sf[0:64, :, :], eng="act")
            K.cp(SBD[64:128, :, 64:128], Sf[64:128, :, :])
            K.S.op("dve", lambda: nc.vector.tensor_reduce(out=mean4[:], in_=v4(yv[:]), axis=AX.X, op=ALU.add), reads=[yv], writes=[mean4])
            K.ts(mean4[:], mean4[:], 1.0 / 64, ALU.mult)
            K.tt(v4(yv[:]), v4(yv[:]), b4(mean4[:]), ALU.subtract)
            K.act(ysq[:], yv[:], AF.Square)
            K.S.op("dve", lambda: nc.vector.tensor_reduce(out=var4[:], in_=v4(ysq[:]), axis=AX.X, op=ALU.add), reads=[ysq], writes=[var4])
            K.ts(var4[:], var4[:], 1.0 / 64, ALU.mult, 64e-5, ALU.add)
            K.tt(var4[:], var4[:], mhalf[:, 0:4], ALU.pow, eng="pool")
            K.tt(v4(yv[:]), v4(yv[:]), b4(var4[:]), ALU.mult)
            K.tt(yv[:], yv[:], lw_b[:], ALU.mult, eng="pool")
            K.tt(yv[:], yv[:], lb_b[:], ALU.add, eng="pool")
            K.tt(yv[:], yv[:], bonus[:], ALU.add, eng="pool")
            K.tt(yv[:], yv[:], gg[:], ALU.mult, eng="pool")
            for g in range(2):
                K.tr(X[:, g * 128:(g + 1) * 128], yv[:, g * 128:(g + 1) * 128], ident[:])
            K.cp(yT[:, :, tk], X[:, 0:256].rearrange("p (g e) -> p g e", g=2))
        K.phase_end()
        K.mark("rwkv_end")

    def dump(m):
        if debug:
            K.dma(dbg_d[m], yT[:].rearrange("p c t -> p (c t)"), rk=[yT], wk=[("dbg", m)])

    for l in range(nlayers):
        for t in range(NT):
            K.act(xres[t][:], xres[t][:], AF.Copy, scale=ALPHA)
        if "a" in parts:
            rwkv(l); dump(0); out_proj(l, 0)
        if "b" in parts:
            attention(l); dump(1); out_proj(l, 1)
        if "c" in parts:
            ssd(l); dump(2); out_proj(l, 2)
        if "d" in parts:
            hgrn(l); dump(3); out_proj(l, 3)
        K.phase_begin()
        lnw = K.sb("lnw", [128, D]); lnb = K.sb("lnb", [128, D])
        K.dma(lnw[:], ln1w_d[l:l + 1, :].partition_broadcast(128), wk=[lnw])
        K.dma(lnb[:], ln1b_d[l:l + 1, :].partition_broadcast(128), wk=[lnb])
        for t in range(NT):
            layer_norm(t, lnw, lnb)
            build_xT(t)
        ffn(l)
        if l + 1 < nlayers and "a" in parts:
            prefetch(("r", l + 1), w_in_d[l + 1, :, 0:896], 8, 896)
        lnw = K.sb("lnw2", [128, D]); lnb = K.sb("lnb2", [128, D])
        K.dma(lnw[:], ln2w_d[l:l + 1, :].partition_broadcast(128), wk=[lnw])
        K.dma(lnb[:], ln2b_d[l:l + 1, :].partition_broadcast(128), wk=[lnb])
        for t in range(NT):
            layer_norm(t, lnw, lnb)
            if l == nlayers - 1:
                K.dma(out_d[t * 128:(t + 1) * 128, :], xres[t][:], rk=[xres[t]], wk=[("out", t)])
            else:
                build_xT(t)
        K.phase_end()
    K.S.emit(limit)
    K.S.stats["marks"] = dict(K.marks)
    return nc, K.S.stats


def make_consts():
    c = {}
    c["c_ident"] = np.eye(128, dtype=np.float32)
    j = np.arange(128)[:, None].astype(np.float64)
    i = np.arange(128)[None, :].astype(np.float64)
    am = np.zeros((128, 5, 4, 128), np.float32)
    for mid, (d, prev) in enumerate([(1, True), (1, False), (4, True), (4, False), (16, False)]):
        for h in range(4):
            if prev:
                dist = 128 + i - j
                valid = dist <= 128
            else:
                dist = i - j
                valid = dist >= 0
            am[:, mid, (h % 2) * 2 + h // 2, :] = np.where(valid, -SLOPES[h] * d * dist, NEG)
    c["c_amask"] = am.reshape(128, -1)
    ii = np.arange(128)
    c["c_triT"] = (ii[:, None] <= ii[None, :]).astype(np.float32)
    c["c_maskneg"] = np.where(ii[None, :] >= ii[:, None], 0.0, NEG).astype(np.float32)
    c["c_msl"] = (ii[None, :] < ii[:, None]).astype(np.float32)
    c["c_msu"] = (ii[:, None] < ii[None, :]).astype(np.float32)
    c["c_shift"] = (ii[None, :] == ii[:, None] + 1).astype(np.float32) - np.eye(128, dtype=np.float32)
    cm = np.zeros((128, 128), np.float32); cm[127, 0] = 1.0
    c["c_carry"] = cm
    same16 = (ii[:, None] // 16) == (ii[None, :] // 16)
    c["c_tri16"] = (same16 & (ii[:, None] <= ii[None, :])).astype(np.float32)
    c["c_blk16"] = same16.astype(np.float32)
    c["c_bm"] = ((ii[:, None] // 16) == np.arange(8)[None, :]).astype(np.float32)
    c["c_bones64"] = ((ii[:, None] // 64) == (ii[None, :] // 64)).astype(np.float32)
    return c


def make_params(inputs):
    p = {}
    cw = np.asarray(inputs["ssd_conv_w"], np.float32)
    cb = np.asarray(inputs["ssd_conv_b"], np.float32)
    pk = np.zeros((DEPTH, 128, 6, 5), np.float32)
    pk[:, :, :, 0:4] = cw.reshape(DEPTH, 4, 6, 128).transpose(0, 3, 2, 1)
    pk[:, :, :, 4] = cb.reshape(DEPTH, 6, 128).transpose(0, 2, 1)
    p["c_ssdconv"] = pk.reshape(DEPTH, 128, 30)
    p["c_rmucol"] = np.ascontiguousarray(np.asarray(inputs["mu_shift"], np.float32)[:, 768:896].reshape(DEPTH, 128, 1))
    vm = np.zeros((128, 1), np.float32); vm[0:32, 0] = np.asarray(inputs["mu_vres"], np.float32)[0]
    p["c_vmucol"] = vm
    p["c_rrk"] = np.ascontiguousarray(np.asarray(inputs["rwkv_r_k"], np.float32).reshape(DEPTH, 256))
    p["c_hgnw"] = np.ascontiguousarray(np.asarray(inputs["hgrn_norm_w"], np.float32).reshape(DEPTH, 2, 128).transpose(0, 2, 1))
    p["c_ssdD"] = np.repeat(np.asarray(inputs["ssd_D"], np.float32), 64, axis=1)
    return p


_CACHE = {}


SHARED = ("w_in", "w_out", "w_up", "w_down", "ln1_w", "ln1_b", "ln2_w", "ln2_b",
          "ssd_dt_bias", "ssd_A_log", "ssd_norm_w", "lower_bounds",
          "mu_shift", "rwkv_w0", "rwkv_a0", "rwkv_k_k", "rwkv_k_a", "rwkv_lnx_w", "rwkv_lnx_b", "rwkv_w2", "rwkv_a2",
          "rwkv_g2", "rwkv_v0", "rwkv_v2", "w_in_vres")


def make_inmap(inputs, b, consts=None, shared=None):
    if consts is None:
        consts = make_consts()
    if shared is None:
        shared = {k: np.ascontiguousarray(inputs[k], dtype=np.float32) for k in SHARED}
        shared.update(make_params(inputs))
    m = {"x": np.ascontiguousarray(inputs["x"][b], dtype=np.float32)}
    m.update(shared)
    m.update(consts)
    return m


def kernel(**inputs):
    if "prog" not in _CACHE:
        _CACHE["prog"] = build_program()
    nc, stats = _CACHE["prog"]
    consts = make_consts()
    shared = {k: np.ascontiguousarray(inputs[k], dtype=np.float32) for k in SHARED}
    shared.update(make_params(inputs))
    in_maps = [make_inmap(inputs, b, consts, shared) for b in range(8)]
    res = run_bass_kernel_spmd(nc, in_maps, core_ids=list(range(8)))
    return np.stack([np.asarray(r["out"], dtype=np.float32) for r in res.results], axis=0)
```

```python
import numpy as np
import concourse.bass as bass
import concourse.mybir as mybir
from concourse.bass_utils import run_bass_kernel_spmd

F32 = mybir.dt.float32
BF16 = mybir.dt.bfloat16
AF = mybir.ActivationFunctionType
ALU = mybir.AluOpType
AX = mybir.AxisListType

T = 2048
D = 1024
NT = 16
DEPTH = 2
ALPHA = (2.0 * DEPTH) ** 0.25
LN_EPS = 1e-5
RMS_EPS = 1e-5
IN_COLS = 3716
C_RWKV, C_ATT, C_SSD, C_HGRN = 0, 896, 1664, 2692
SLOPES = [2.0 ** (-8.0 * (h + 1) / 4) for h in range(4)]
NEG = -30000.0
STRICT = False


class Sched:
    LAT = 300.0

    def __init__(self, nc):
        self.nc = nc
        self.eng = {"pe": nc.tensor, "dve": nc.vector, "act": nc.scalar,
                    "pool": nc.gpsimd, "sp": nc.sync}
        self.ops = []
        self.info = []
        self.lastw = {}
        self.reads = {}
        self.fences = []
        self.reorder = True
        self.strict = STRICT

    @staticmethod
    def _key(a):
        if isinstance(a, (str, tuple)):
            return a
        return a.name

    def op(self, engine, fn, reads=(), writes=(), dma=False, cost=100.0):
        idx = len(self.ops)
        sem = set()
        order = set()
        rk = [self._key(a) for a in reads]
        wk = [self._key(a) for a in writes]
        for k in rk:
            if k in self.lastw:
                sem.add(self.lastw[k])
            if isinstance(k, str) and k.startswith("ps_"):
                for (e, i, d) in self.reads.get(k, ()):
                    if e != engine:
                        sem.add(i)
        for k in wk:
            if k in self.lastw:
                j = self.lastw[k]
                if dma or self.ops[j][3] or self.ops[j][0] != engine or (self.strict and engine != "pe"):
                    sem.add(j)
                else:
                    order.add(j)
            for (e, i, d) in self.reads.get(k, ()):
                if dma or d or e != engine or (self.strict and engine != "pe"):
                    sem.add(i)
                else:
                    order.add(i)
        sem.discard(idx)
        order.discard(idx)
        order -= sem
        self.info.append((engine, "dma" if dma else "", rk, wk))
        self.ops.append((engine, fn, sem, dma, order, float(cost)))
        for k in wk:
            self.lastw[k] = idx
            self.reads[k] = []
        for k in rk:
            self.reads.setdefault(k, []).append((engine, idx, dma))
        return idx

    def barrier(self):
        self.fences.append(len(self.ops))
        self.lastw = {}
        self.reads = {}

    def _schedule(self, seg):
        if not self.reorder or len(seg) < 3:
            return list(seg)
        ops = self.ops
        segset = set(seg)
        preds = {}
        succs = {i: [] for i in seg}
        indeg = {}
        for i in seg:
            p = [d for d in (ops[i][2] | ops[i][4]) if d in segset]
            preds[i] = p
            indeg[i] = len(p)
            for d in p:
                succs[d].append(i)
        finish = {}
        free = {e: 0.0 for e in self.eng}
        ready = {e: [] for e in self.eng}
        for i in seg:
            if indeg[i] == 0:
                ready[ops[i][0]].append((0.0, i))
        order = []
        nleft = len(seg)
        while nleft:
            best = None
            for e, lst in ready.items():
                if not lst:
                    continue
                f = free[e]
                cand = None
                for (dr, i) in lst:
                    st = dr if dr > f else f
                    key = (st, i)
                    if cand is None or key < cand:
                        cand = key
                if best is None or cand < best[0]:
                    best = (cand, e)
            (st, i), e = best
            ready[e] = [x for x in ready[e] if x[1] != i]
            eng, fn, sem, dma, od, cost = ops[i]
            if dma:
                issue = 1500.0 if e == "pool" else 150.0
                free[e] = st + issue
                finish[i] = st + issue + cost
            else:
                free[e] = st + cost
                finish[i] = st + cost
            order.append(i)
            nleft -= 1
            for s_ in succs[i]:
                indeg[s_] -= 1
                if indeg[s_] == 0:
                    dr = 0.0
                    for p in preds[s_]:
                        t = finish[p] + (self.LAT if (ops[p][0] != ops[s_][0] or p in ops[s_][2]) else 0.0)
                        if t > dr:
                            dr = t
                    ready[ops[s_][0]].append((dr, s_))
        mk = max(list(finish.values()) + [0.0])
        self.est_ns = getattr(self, "est_ns", 0.0) + mk
        busy = {e: 0.0 for e in self.eng}
        for i in seg:
            if not ops[i][3]:
                busy[ops[i][0]] += ops[i][5]
        self.seglog = getattr(self, "seglog", [])
        self.seglog.append((len(seg), round(mk / 1e3, 1), {e: round(b / 1e3, 1) for e, b in busy.items()}))
        return order

    def emit(self, limit=None):
        nc = self.nc
        ops = self.ops
        n = len(ops)
        elimit = limit
        emitted = 0
        bounds = [0] + [f for f in self.fences if f < n] + [n]
        segs = [list(range(bounds[i], bounds[i + 1])) for i in range(len(bounds) - 1)]
        sched = [self._schedule(sg) for sg in segs]
        need = [False] * len(ops)
        for i in range(n):
            for d in ops[i][2]:
                need[d] = True
        for od in sched:
            last = {}
            for i in od:
                if not ops[i][3]:
                    last[ops[i][0]] = i
            for i in last.values():
                need[i] = True
        NQ = {"sp": 16, "pool": 8, "act": 4}

        def run(dry, need):
            csem = dsem = None
            if not dry:
                csem = {e: nc.alloc_semaphore(f"sem_{e}") for e in self.eng}
                dsem = {q: [nc.alloc_semaphore(f"dsem_{q}_{i}") for i in range(k)] for q, k in NQ.items()}
            dcount = {q: [0] * k for q, k in NQ.items()}
            ndma = {q: 0 for q in NQ}
            sig = [None] * len(ops)
            sigop = {}
            used = set()
            ccount = {e: 0 for e in self.eng}
            seen = {e: {} for e in self.eng}
            nw = [0]
            emitted = 0

            def wait(e, key, val):
                if val <= 0 or seen[e].get(key, 0) >= val:
                    return
                seen[e][key] = val
                nw[0] += 1
                if (key, val) in sigop:
                    used.add(sigop[(key, val)])
                if not dry:
                    semh = dsem[key[1]][key[2]] if key[0] == "d" else csem[key[1]]
                    self.eng[e].wait_ge(semh, val)

            for si_, od in enumerate(sched):
                if si_ > 0:
                    for e in self.eng:
                        for e2 in self.eng:
                            wait(e, ("c", e2), ccount[e2])
                        for q in NQ:
                            for k in range(NQ[q]):
                                wait(e, ("d", q, k), dcount[q][k])
                for i in od:
                    if elimit is not None and emitted >= elimit:
                        break
                    emitted += 1
                    e, fn, deps, dma, _, _ = ops[i]
                    wants = {}
                    for d in deps:
                        s_ = sig[d]
                        if s_ is None:
                            continue
                        key, val = s_
                        if wants.get(key, 0) < val:
                            wants[key] = val
                    if dma:
                        si = ndma[e] % NQ[e]
                        if dcount[e][si] > 0:
                            key = ("d", e, si)
                            if wants.get(key, 0) < dcount[e][si]:
                                wants[key] = dcount[e][si]
                    for key, val in wants.items():
                        wait(e, key, val)
                    inst = None if dry else fn()
                    if dma:
                        si = ndma[e] % NQ[e]
                        ndma[e] += 1
                        dcount[e][si] += 16
                        if not dry:
                            inst.then_inc(dsem[e][si], 16)
                        sig[i] = (("d", e, si), dcount[e][si])
                    elif need[i]:
                        ccount[e] += 1
                        if not dry:
                            inst.then_inc(csem[e], 1)
                        sig[i] = (("c", e), ccount[e])
                        sigop[sig[i]] = i
            if not dry:
                for q in NQ:
                    for si in range(NQ[q]):
                        if dcount[q][si] > 0:
                            nc.sync.wait_ge(dsem[q][si], dcount[q][si])
            return used, nw[0], ndma

        used, _, _ = run(True, need)
        need2 = [False] * len(ops)
        for i in used:
            need2[i] = True
        used2, nwaits, ndma = run(False, need2)
        self.nsig = sum(need2)
        self.stats = dict(n=n, nwaits=nwaits, nsig=self.nsig, ndma=ndma, est_us=getattr(self, "est_ns", 0.0) / 1e3)


def _fs(ap):
    n = 1
    for d in ap.shape[1:]:
        n *= d
    return n


class KB:
    def __init__(self, nc):
        self.nc = nc
        self.S = Sched(nc)
        self.uid = 0
        self.stack = None
        self.marks = []

    def mark(self, name):
        self.marks.append((name, len(self.S.ops)))

    def sb(self, name, shape, dt=F32):
        if self.stack is None:
            return self.nc.alloc_sbuf_tensor(name, list(shape), dt)
        self.uid += 1
        return self.stack.enter_context(self.nc.sbuf_tensor(f"{name}_u{self.uid}", list(shape), dt))

    def phase_begin(self):
        import contextlib
        assert self.stack is None
        self.stack = contextlib.ExitStack()

    def phase_end(self):
        self.S.barrier()
        self.stack.close()
        self.stack = None

    def ps(self, name, shape, dt=F32):
        return self.nc.alloc_psum_tensor(name, list(shape), dt)

    def mm(self, out, lhsT, rhs, start=True, stop=True, rk=None, wk=None):
        nc = self.nc
        cost = max(64, _fs(rhs)) / 2.4 * (4 if lhsT.dtype == F32 else 1) + 45
        self.S.op("pe", lambda: nc.tensor.matmul(out, lhsT=lhsT, rhs=rhs, start=start, stop=stop),
                  reads=rk if rk is not None else [lhsT, rhs], writes=wk if wk is not None else [out], cost=cost)

    def tr(self, out, in_, ident, rk=None, wk=None):
        nc = self.nc
        self.S.op("pe", lambda: nc.tensor.transpose(out, in_, ident),
                  reads=rk if rk is not None else [in_, ident], writes=wk if wk is not None else [out], cost=110)

    def act(self, out, in_, func, bias=None, scale=None, accum=None, rk=None, wk=None):
        nc = self.nc
        kw = {}
        reads = [in_]
        if bias is not None:
            kw["bias"] = bias
            if not isinstance(bias, (int, float)):
                reads.append(bias)
        if scale is not None:
            kw["scale"] = scale
            if not isinstance(scale, (int, float)):
                reads.append(scale)
        writes = [out]
        if accum is not None:
            kw["accum_out"] = accum
            writes.append(accum)
        self.S.op("act", lambda: nc.scalar.activation(out=out, in_=in_, func=func, **kw),
                  reads=rk if rk is not None else reads, writes=wk if wk is not None else writes,
                  cost=(224 + _fs(out)) / 1.2)

    def tt(self, out, in0, in1, op, eng="dve", rk=None, wk=None):
        E = self.S.eng[eng]
        cost = (170 + _fs(out)) / 0.96 if eng == "dve" else (350 + 2.0 * _fs(out)) / 1.2
        self.S.op(eng, lambda: E.tensor_tensor(out=out, in0=in0, in1=in1, op=op),
                  reads=rk if rk is not None else [in0, in1], writes=wk if wk is not None else [out], cost=cost)

    def ts(self, out, in0, s1, op0, s2=None, op1=None, eng="dve", rk=None, wk=None):
        E = self.S.eng[eng]
        reads = [in0] + [s for s in (s1, s2) if s is not None and not isinstance(s, (int, float))]
        if op1 is None:
            fn = lambda: E.tensor_scalar(out=out, in0=in0, scalar1=s1, scalar2=None, op0=op0)
        else:
            fn = lambda: E.tensor_scalar(out=out, in0=in0, scalar1=s1, scalar2=s2, op0=op0, op1=op1)
        cost = (170 + 0.7 * _fs(out)) / 0.96 if eng == "dve" else (350 + 2.0 * _fs(out)) / 1.2
        self.S.op(eng, fn, reads=rk if rk is not None else reads, writes=wk if wk is not None else [out], cost=cost)

    def stt(self, out, in0, scalar, in1, op0, op1, rk=None, wk=None):
        nc = self.nc
        reads = [in0, in1] + ([] if isinstance(scalar, (int, float)) else [scalar])
        self.S.op("dve", lambda: nc.vector.scalar_tensor_tensor(out=out, in0=in0, scalar=scalar, in1=in1, op0=op0, op1=op1),
                  reads=rk if rk is not None else reads, writes=wk if wk is not None else [out],
                  cost=(170 + _fs(out)) / 0.96)

    def cp(self, out, in_, eng="dve", rk=None, wk=None):
        nc = self.nc
        if eng == "act":
            fn = lambda: nc.scalar.activation(out=out, in_=in_, func=AF.Copy)
        else:
            E = self.S.eng[eng]
            fn = lambda: E.tensor_copy(out=out, in_=in_)
        if eng == "act":
            cost = (224 + _fs(out)) / 1.2
        elif eng == "dve":
            cost = (170 + 0.7 * _fs(out)) / 0.96
        else:
            cost = (350 + 2.0 * _fs(out)) / 1.2
        self.S.op(eng, fn, reads=rk if rk is not None else [in_], writes=wk if wk is not None else [out], cost=cost)

    def recip(self, out, in_, rk=None, wk=None):
        nc = self.nc
        self.S.op("dve", lambda: nc.vector.reciprocal(out=out, in_=in_),
                  reads=rk if rk is not None else [in_], writes=wk if wk is not None else [out],
                  cost=(62 + 8 * _fs(out)) / 0.96)

    def memset(self, ap, val, eng="dve"):
        E = self.S.eng[eng]
        self.S.op(eng, lambda: E.memset(ap, val), writes=[ap], cost=(62 + _fs(ap)) / 0.96)

    def dma(self, out, in_, eng="sp", rk=(), wk=()):
        E = self.S.eng[eng]
        nbytes = out.shape[0] * _fs(out) * 4
        self.S.op(eng, lambda: E.dma_start(out=out, in_=in_), reads=list(rk), writes=list(wk), dma=True,
                  cost=2000 + nbytes / 120.0)


def build_program(debug=False, nlayers=DEPTH, parts="abcd", limit=None):
    nc = bass.Bass("TRN2", target_bir_lowering=False)
    K = KB(nc)

    def din(name, shape):
        return nc.dram_tensor(name, list(shape), F32, kind="ExternalInput").ap()

    x_d = din("x", [T, D])
    w_in_d = din("w_in", [DEPTH, D, IN_COLS])
    w_out_d = din("w_out", [DEPTH, D, D])
    w_up_d = din("w_up", [DEPTH, D, 4 * D])
    w_down_d = din("w_down", [DEPTH, 4 * D, D])
    ln1w_d = din("ln1_w", [DEPTH, D]); ln1b_d = din("ln1_b", [DEPTH, D])
    ln2w_d = din("ln2_w", [DEPTH, D]); ln2b_d = din("ln2_b", [DEPTH, D])
    ident_d = din("c_ident", [128, 128])
    amask_d = din("c_amask", [128, 5 * 4 * 128])
    triT_d = din("c_triT", [128, 128]); maskneg_d = din("c_maskneg", [128, 128])
    ssdconv_d = din("c_ssdconv", [DEPTH, 128, 30]); ssdD_d = din("c_ssdD", [DEPTH, 256])
    msl_d = din("c_msl", [128, 128]); msu_d = din("c_msu", [128, 128])
    shift_d = din("c_shift", [128, 128]); carry_d = din("c_carry", [128, 128])
    rmucol_d = din("c_rmucol", [DEPTH, 128, 1]); vmucol_d = din("c_vmucol", [128, 1])
    mush_d = din("mu_shift", [DEPTH, 896])
    rw0_d = din("rwkv_w0", [DEPTH, 256]); ra0_d = din("rwkv_a0", [DEPTH, 256]); rkk_d = din("rwkv_k_k", [DEPTH, 256])
    rka_d = din("rwkv_k_a", [DEPTH, 256]); rlw_d = din("rwkv_lnx_w", [DEPTH, 256]); rlb_d = din("rwkv_lnx_b", [DEPTH, 256])
    rrk_d = din("c_rrk", [DEPTH, 256]); rw2_d = din("rwkv_w2", [DEPTH, 32, 256]); ra2_d = din("rwkv_a2", [DEPTH, 32, 256])
    rg2_d = din("rwkv_g2", [DEPTH, 64, 256]); rv0_d = din("rwkv_v0", [1, 256]); rv2_d = din("rwkv_v2", [1, 32, 256])
    wvres_d = din("w_in_vres", [1, D, 32])
    vfirst_d = nc.dram_tensor("vfirst", [T, 256], F32, kind="Internal").ap()
    lscr_d = nc.dram_tensor("lscr", [128, T], BF16, kind="Internal").ap()
    vrscr_d = nc.dram_tensor("vrscr", [32, T], BF16, kind="Internal").ap()
    tri16_d = din("c_tri16", [128, 128]); blk16_d = din("c_blk16", [128, 128]); bm_d = din("c_bm", [128, 8])
    bones64_d = din("c_bones64", [128, 128]); hgnw_d = din("c_hgnw", [DEPTH, 128, 2]); lowb_d = din("lower_bounds", [DEPTH, 256])
    dtb_d = din("ssd_dt_bias", [DEPTH, 4]); alog_d = din("ssd_A_log", [DEPTH, 4]); ssdnw_d = din("ssd_norm_w", [DEPTH, 256])
    out_d = nc.dram_tensor("out", [T, D], F32, kind="ExternalOutput").ap()
    vscr_d = nc.dram_tensor("vscr", [T, 256], BF16, kind="Internal").ap()
    dbg_d = None
    if debug:
        dbg_d = nc.dram_tensor("dbg", [4, 128, 2 * T], BF16, kind="ExternalOutput").ap()

    xres = [K.sb(f"xres{t}", [128, D]) for t in range(NT)]
    xT = K.sb("xT", [128, 8, T], BF16)
    ident = K.sb("ident", [128, 128])
    ones_bf = K.sb("ones_bf", [128, 64], BF16)
    wbuf = [K.sb(f"wbuf{i}", [128, 8192], BF16) for i in range(2)]
    wstate = {"i": 0}
    psb = [K.ps(f"ps_{i}", [128, 512]) for i in range(7)]
    pbf = K.ps("ps_bf", [128, 1024], BF16)
    ident_bf = K.sb("ident_bf", [128, 128], BF16)
    triT = K.sb("triT", [128, 128]); ones_f = K.sb("ones_f", [128, 128]); maskneg = K.sb("maskneg", [128, 128])

    def xk(t0, t1):
        return [("xT", b) for b in range(t0 // 512, (t1 - 1) // 512 + 1)]
    XALL = [("xT", b) for b in range(4)]

    pre = {}

    def prefetch(tag, src_ap, kc, ncols):
        pre[tag] = wload(src_ap, kc, ncols)

    def getw(tag, src_ap, kc, ncols):
        if tag in pre:
            return pre.pop(tag)
        return wload(src_ap, kc, ncols)

    def wsmall(name, src_ap, kc, ncols):
        t_ = K.sb(name, [128, kc * ncols], BF16)
        view = t_[:, :].rearrange("p (k c) -> p k c", k=kc)
        K.dma(view, src_ap.rearrange("(k p) c -> p k c", p=128), eng="pool", rk=[t_], wk=[t_])
        return t_, view

    def wload(src_ap, kc, ncols, off=0, new=True):
        if new:
            wstate["i"] += 1
        wb = wbuf[wstate["i"] % 2]
        view = wb[:, off:off + kc * ncols].rearrange("p (k c) -> p k c", k=kc)
        K.dma(view, src_ap.rearrange("(k p) c -> p k c", p=128), eng="pool", rk=[wb], wk=[wb])
        return wb, view

    K.dma(ident[:], ident_d, wk=[ident])
    K.cp(ident_bf[:], ident[:])
    K.dma(triT[:], triT_d, wk=[triT])
    K.dma(maskneg[:], maskneg_d, wk=[maskneg])
    K.memset(ones_f[:], 1.0)
    K.memset(ones_bf[:], 1.0)
    mhalf = K.sb("mhalf", [128, 4])
    K.memset(mhalf[:], -0.5)

    for t in range(NT):
        K.dma(xres[t][:], x_d[t * 128:(t + 1) * 128, :], wk=[xres[t]])

    def build_xT(t):
        for half in range(2):
            pb = psb[5 + half]
            for j in range(4):
                kc = half * 4 + j
                K.tr(pb[:, j * 128:(j + 1) * 128], xres[t][:, kc * 128:(kc + 1) * 128], ident[:])
            K.cp(xT[:, half * 4:(half + 1) * 4, t * 128:(t + 1) * 128],
                 pb[:].rearrange("p (j c) -> p j c", j=4), eng=("act" if half else "dve"),
                 wk=xk(t * 128, (t + 1) * 128))

    for t in range(NT):
        build_xT(t)

    def layer_norm(t, w_t, b_t):
        xt = xres[t]
        stats = K.sb(f"lnst{K.uid}", [128, 12]); mv = K.sb(f"lnmv{K.uid}", [128, 2]); rs = K.sb(f"lnrs{K.uid}", [128, 1])
        K.uid += 1
        nc_ = nc
        K.S.op("dve", lambda: nc_.vector.bn_stats(out=stats[:, 0:6], in_=xt[:, 0:512]), reads=[xt], writes=[stats])
        K.S.op("dve", lambda: nc_.vector.bn_stats(out=stats[:, 6:12], in_=xt[:, 512:1024]), reads=[xt], writes=[stats])
        K.S.op("dve", lambda: nc_.vector.bn_aggr(out=mv[:], in_=stats[:]), reads=[stats], writes=[mv])
        K.ts(rs[:], mv[:, 1:2], LN_EPS, ALU.add)
        K.tt(rs[:], rs[:], mhalf[:, 0:1], ALU.pow, eng="pool")
        K.stt(mv[:, 1:2], mv[:, 0:1], -1.0, rs[:, 0:1], ALU.mult, ALU.mult)
        K.act(xt[:], xt[:], AF.Identity, bias=mv[:, 1:2], scale=rs[:, 0:1])
        K.tt(xt[:], xt[:], w_t[:], ALU.mult)
        K.tt(xt[:], xt[:], b_t[:], ALU.add, eng="pool")

    yT = K.sb("yT", [128, 2, T], BF16)
    wo_t = K.sb("wo_t", [128, 2 * D], BF16)

    def out_proj(l, m):
        wb = wo_t
        wv = wo_t[:, :].rearrange("p (k c) -> p k c", k=2)
        K.dma(wv, w_out_d[l, m * 256:(m + 1) * 256, :].rearrange("(k p) c -> p k c", p=128), eng="pool", rk=[wo_t], wk=[wo_t])
        for t in range(NT):
            for nb in range(2):
                pb = psb[4 + (t * 2 + nb) % 2]
                for c in range(2):
                    K.mm(pb[:, :], lhsT=yT[:, c, t * 128:(t + 1) * 128], rhs=wv[:, c, nb * 512:(nb + 1) * 512],
                         start=(c == 0), stop=(c == 1), rk=[yT, wb])
                K.tt(xres[t][:, nb * 512:(nb + 1) * 512], pb[:, :], xres[t][:, nb * 512:(nb + 1) * 512], ALU.add)

    def ffn(l):
        hT = [K.sb(f"hT{i}", [128, 4, T], BF16) for i in range(2)]
        rtmp = [K.sb(f"rtmp{i}", [128, 512]) for i in range(2)]
        for t in range(NT):
            K.act(xres[t][:], xres[t][:], AF.Copy, scale=ALPHA)
        for j in range(8):
            if j == 0 and ("f0", l) in pre:
                (wub, wuv), (wdb, wdv) = pre.pop(("f0", l))
            else:
                wub, wuv = wload(w_up_d[l, :, j * 512:(j + 1) * 512], 8, 512)
                wdb, wdv = wload(w_down_d[l, j * 512:(j + 1) * 512, :], 4, D, off=4096, new=False)
            h = hT[j % 2]
            cnt = 0
            for m in range(4):
                for nb in range(4):
                    pb = psb[cnt % 2]
                    rt = rtmp[cnt % 2]
                    cnt += 1
                    for kc in range(8):
                        K.mm(pb[:, :], lhsT=wuv[:, kc, m * 128:(m + 1) * 128], rhs=xT[:, kc, nb * 512:(nb + 1) * 512],
                             start=(kc == 0), stop=(kc == 7), rk=[wub, ("xT", nb)])
                    K.act(rt[:], pb[:, :], AF.Relu)
                    K.tt(h[:, m, nb * 512:(nb + 1) * 512], rt[:], rt[:], ALU.mult, eng="pool", wk=[(h.name, nb)])
            for t in range(NT):
                for nb in range(2):
                    pb = psb[2 + (t * 2 + nb) % 2]
                    for c in range(4):
                        K.mm(pb[:, :], lhsT=h[:, c, t * 128:(t + 1) * 128], rhs=wdv[:, c, nb * 512:(nb + 1) * 512],
                             start=(c == 0), stop=(c == 3), rk=[(h.name, t // 4), wdb])
                    K.tt(xres[t][:, nb * 512:(nb + 1) * 512], pb[:, :], xres[t][:, nb * 512:(nb + 1) * 512], ALU.add)

    def attention(l):
        K.mark("attn_begin")
        K.phase_begin()
        qT = K.sb("qT", [128, 2, T], BF16); kT = K.sb("kT", [128, 2, T], BF16)
        amask = K.sb("amask", [128, 2, 512])
        accn = K.sb("accn", [128, 2, T]); accd = K.sb("accd", [128, 2, T], BF16)
        Vb = K.sb("Vb", [128, 16, 256], BF16)
        Vn = [K.sb(f"Vn{i}", [128, 256], BF16) for i in range(2)]
        Ebuf = [K.sb(f"Ebuf{i}", [128, 512]) for i in range(2)]
        Pbuf = [K.sb(f"Pbuf{i}", [128, 512], BF16) for i in range(4)]
        amask_v = amask_d.rearrange("p (m f) -> p m f", m=5)
        wb, wv = getw(("a", l), w_in_d[l, :, C_ATT:C_ATT + 768], 8, 768)
        if "c" in parts:
            prefetch(("s", l), w_in_d[l, :, C_SSD:C_SSD + 1024], 8, 1024)
        cnt = 0
        for which, dst, scale in ((0, qT, 0.125), (1, kT, 1.0)):
            for c in range(2):
                for nb in range(4):
                    pb = psb[cnt % 2]; cnt += 1
                    for kc in range(8):
                        K.mm(pb[:, :], lhsT=wv[:, kc, which * 256 + c * 128: which * 256 + (c + 1) * 128],
                             rhs=xT[:, kc, nb * 512:(nb + 1) * 512], start=(kc == 0), stop=(kc == 7),
                             rk=[wb, ("xT", nb)])
                    K.act(dst[:, c, nb * 512:(nb + 1) * 512], pb[:, :], AF.Copy, scale=scale)
        K.mark("attn_V")
        for t in range(NT):
            pb = psb[t % 2]
            for kc in range(8):
                K.mm(pb[:, 0:256], lhsT=xT[:, kc, t * 128:(t + 1) * 128], rhs=wv[:, kc, 512:768],
                     start=(kc == 0), stop=(kc == 7), rk=[wb, ("xT", t // 4)])
            K.cp(Vn[t % 2][:], pb[:, 0:256], eng="act")
            K.dma(vscr_d[t * 128:(t + 1) * 128, :], Vn[t % 2][:], rk=[Vn[t % 2]], wk=["vscr"])
        ecnt = 0; qcnt = 0
        opbanks = [psb[6], pbf[:].bitcast(F32), psb[0], psb[1]]
        for bi, d in enumerate((1, 4, 16)):
            K.mark(f"attn_branch{bi}")
            nblk = T // d // 128
            if bi < 2:
                K.dma(amask[:], amask_v[:, 2 * bi:2 * bi + 2, :], rk=[amask], wk=[amask])
            else:
                K.dma(amask[:, 1, :], amask_v[:, 4, :], rk=[amask], wk=[amask])

            def tsl(r, n, d=d):
                st = r + d * 128 * n
                return slice(st, st + d * 127 + 1, d)
            if d == 1:
                for q in range(4):
                    K.dma(Vb[:, q * 4:(q + 1) * 4, :], vscr_d.rearrange("(n j) f -> j n f", j=128)[:, q * 4:(q + 1) * 4, :],
                          rk=["vscr", Vb], wk=[Vb])
            elif d == 4:
                for r in range(4):
                    K.dma(Vb[:, r * 4:(r + 1) * 4, :], vscr_d.rearrange("(n j r) f -> r j n f", n=4, j=128, r=4)[r],
                          rk=["vscr", Vb], wk=[Vb])
            else:
                for q in range(4):
                    K.dma(Vb[:, q * 4:(q + 1) * 4, :], vscr_d.rearrange("(j r) f -> j r f", r=16)[:, q * 4:(q + 1) * 4, :],
                          rk=["vscr", Vb], wk=[Vb])
            for r in range(d):
                for n in range(nblk):
                    qsl = tsl(r, n)
                    kbl = []
                    if n > 0:
                        kbl.append((r * nblk + n - 1, tsl(r, n - 1), 0))
                    kbl.append((r * nblk + n, qsl, 1))
                    Ps = []
                    for ki, (vidx, ksl, mid) in enumerate(kbl):
                        spa = psb[2 + 2 * ki]; spb = psb[3 + 2 * ki]
                        for h in range(4):
                            po = (h % 2) * 64
                            sp = spa if h % 2 == 0 else spb
                            K.mm(sp[:, (h // 2) * 128:(h // 2 + 1) * 128], lhsT=kT[po:po + 64, h // 2, ksl],
                                 rhs=qT[po:po + 64, h // 2, qsl])
                        E = Ebuf[ecnt % 2]; P = Pbuf[ecnt % 4]; ecnt += 1
                        K.tt(E[:, 0:256], spa[:, 0:256], amask[:, mid, 0:256], ALU.add)
                        K.tt(E[:, 256:512], spb[:, 0:256], amask[:, mid, 256:512], ALU.add)
                        K.act(P[:], E[:], AF.Exp)
                        Ps.append((vidx, P))
                    op = opbanks[qcnt % 4]; qcnt += 1
                    for h in range(4):
                        po = (h % 2) * 64; hh = h // 2
                        pc = (h % 2) * 256 + hh * 128
                        for ki, (vidx, P) in enumerate(Ps):
                            K.mm(op[po:po + 64, hh * 128:(hh + 1) * 128], lhsT=Vb[:, vidx, h * 64:(h + 1) * 64],
                                 rhs=P[:, pc:pc + 128], start=(ki == 0), stop=(ki == len(Ps) - 1))
                        for ki, (vidx, P) in enumerate(Ps):
                            K.mm(op[po:po + 64, (2 + hh) * 128:(3 + hh) * 128], lhsT=ones_bf[:, 0:64],
                                 rhs=P[:, pc:pc + 128], start=(ki == 0), stop=(ki == len(Ps) - 1))
                    nv = op[:, 0:256].rearrange("p (a q) -> p a q", a=2)
                    dv = op[:, 256:512].rearrange("p (a q) -> p a q", a=2)
                    if bi == 0:
                        K.cp(accn[:, :, qsl], nv, eng="act")
                        K.cp(accd[:, :, qsl], dv, eng="dve")
                    else:
                        K.tt(accn[:, :, qsl], nv, accn[:, :, qsl], ALU.add)
                        K.tt(accd[:, :, qsl], dv, accd[:, :, qsl], ALU.add)
        K.mark("attn_final")
        chunks = [(hh, q4) for hh in range(2) for q4 in range(4)]
        for p0 in range(0, 8, 2):
            pair = chunks[p0:p0 + 2]
            for i_, (hh, q4) in enumerate(pair):
                K.act(Ebuf[i_][:], accd[:, hh, q4 * 512:(q4 + 1) * 512], AF.Ln)
            for i_, (hh, q4) in enumerate(pair):
                K.act(Ebuf[i_][:], Ebuf[i_][:], AF.Exp, scale=-1.0)
            for i_, (hh, q4) in enumerate(pair):
                sl_ = slice(q4 * 512, (q4 + 1) * 512)
                K.tt(yT[:, hh, sl_], accn[:, hh, sl_], Ebuf[i_][:], ALU.mult)
        K.phase_end()


    def ssd(l):
        K.phase_begin()
        wb, wv = getw(("s", l), w_in_d[l, :, C_SSD:C_SSD + 1024], 8, 1024)
        wb2, wv2 = wsmall("wdt", w_in_d[l, :, C_SSD + 1024:C_SSD + 1028], 8, 4)
        if "d" in parts:
            prefetch(("h", l), w_in_d[l, :, C_HGRN:C_HGRN + 1024], 8, 1024)
        cpar = K.sb("cpar", [128, 30]); dtb = K.sb("dtb", [128, 4]); aneg = K.sb("aneg", [128, 4])
        Dfull = K.sb("Dfull", [128, 256]); normw = K.sb("normw", [128, 256])
        K.dma(cpar[:], ssdconv_d[l], wk=[cpar])
        K.dma(dtb[:], dtb_d[l:l + 1, :].partition_broadcast(128), wk=[dtb])
        K.dma(aneg[:], alog_d[l:l + 1, :].partition_broadcast(128), wk=[aneg])
        K.dma(Dfull[:], ssdD_d[l:l + 1, :].partition_broadcast(128), wk=[Dfull])
        K.dma(normw[:], ssdnw_d[l:l + 1, :].partition_broadcast(128), wk=[normw])
        K.act(aneg[:], aneg[:], AF.Exp)
        K.ts(aneg[:], aneg[:], -1.0, ALU.mult)
        xpads = [K.sb(f"xpad{i}", [128, T + 3], BF16) for i in range(2)]
        cdiag = K.sb("cdiag", [128, 24, 128], BF16)
        for cj in range(24):
            c_, j_ = divmod(cj, 4)
            K.act(cdiag[:, cj, :], ident[:], AF.Copy, scale=cpar[:, c_ * 5 + j_:c_ * 5 + j_ + 1])
        xsT = K.sb("xsT", [128, 2, T], BF16); BT = K.sb("BT", [128, 2, T], BF16); CT = K.sb("CT", [128, 2, T], BF16)
        for xp_ in xpads:
            K.memset(xp_[:, 0:3], 0.0)
        for c in range(6):
            xpad = xpads[c % 2]
            for nb in range(4):
                pb = psb[nb % 2]
                for kc in range(8):
                    K.mm(pb[:, :], lhsT=wv[:, kc, 256 + c * 128:256 + (c + 1) * 128], rhs=xT[:, kc, nb * 512:(nb + 1) * 512],
                         start=(kc == 0), stop=(kc == 7), rk=[wb, ("xT", nb)])
                K.act(xpad[:, 3 + nb * 512:3 + (nb + 1) * 512], pb[:, :], AF.Copy)
            dst = (xsT, BT, CT)[c // 2]
            for nb in range(4):
                pc = psb[2 + nb % 2]
                for j in range(4):
                    K.mm(pc[:, :], lhsT=cdiag[:, c * 4 + j, :], rhs=xpad[:, nb * 512 + j:nb * 512 + j + 512],
                         start=(j == 0), stop=(j == 3))
                K.act(dst[:, c % 2, nb * 512:(nb + 1) * 512], pc[:, :], AF.Silu, bias=cpar[:, c * 5 + 4:c * 5 + 5])
        dt = K.sb("dt", [128, 64]); dA = K.sb("dA", [128, 64]); cs = K.sb("cs", [128, 64]); ncs = K.sb("ncs", [128, 64])
        expcs = K.sb("expcs", [128, 64]); dst_ = K.sb("dsts", [128, 64]); cdec = K.sb("cdec", [128, 64])
        for t in range(NT):
            pb = psb[t % 2]
            for kc in range(8):
                K.mm(pb[:, 0:4], lhsT=xT[:, kc, t * 128:(t + 1) * 128], rhs=wv2[:, kc, 0:4],
                     start=(kc == 0), stop=(kc == 7), rk=[wb2, ("xT", t // 4)])
            K.tt(dt[:, t * 4:(t + 1) * 4], pb[:, 0:4], dtb[:], ALU.add)
        K.act(dt[:], dt[:], AF.Exp)
        K.act(dt[:], dt[:], AF.Ln, bias=1.0)
        K.tt(dA[:].rearrange("p (c h) -> p c h", h=4), dt[:].rearrange("p (c h) -> p c h", h=4),
             aneg[:].unsqueeze(1).to_broadcast([128, 16, 4]), ALU.mult)
        K.mm(psb[2][:, 0:64], lhsT=triT[:], rhs=dA[:])
        K.mm(psb[2][:, 64:128], lhsT=ones_f[:], rhs=dA[:])
        K.cp(cs[:], psb[2][:, 0:64])
        K.ts(ncs[:], cs[:], -1.0, ALU.mult)
        K.act(expcs[:], cs[:], AF.Exp)
        K.tt(dst_[:], psb[2][:, 64:128], cs[:], ALU.subtract)
        K.act(dst_[:], dst_[:], AF.Exp)
        K.act(cdec[:], psb[2][:, 64:128], AF.Exp)
        state = K.sb("sstate", [128, 256]); K.memset(state[:], 0.0)
        prevb = [K.sb(f"prevb{i}", [128, 256], BF16) for i in range(2)]
        Rb = [K.sb(f"Rb{i}", [128, 128]) for i in range(2)]
        decT = [K.sb(f"decT{i}", [128, 128]) for i in range(2)]
        MT = [K.sb(f"MT{i}", [128, 4, 128], BF16) for i in range(2)]
        xs_tm = [K.sb(f"xstm{i}", [128, 256]) for i in range(2)]
        xdt = [K.sb(f"xdt{i}", [128, 256], BF16) for i in range(2)]
        xdt2 = [K.sb(f"xdt2{i}", [128, 256], BF16) for i in range(2)]
        B_tm = [K.sb(f"Btm{i}", [128, 256], BF16) for i in range(2)]
        zs = [K.sb(f"zs{i}", [128, 256]) for i in range(2)]
        yy = [K.sb(f"yy{i}", [128, 256]) for i in range(2)]
        y2 = [K.sb(f"y2{i}", [128, 256]) for i in range(2)]
        ss = [K.sb(f"ssq{i}", [128, 2]) for i in range(2)]
        rc = 0
        for c in range(NT):
            b = c % 2
            tk = slice(c * 128, (c + 1) * 128)
            for g in range(2):
                K.mm(psb[3][:, g * 128:(g + 1) * 128], lhsT=BT[:, g, tk], rhs=CT[:, g, tk])
            for h in range(4):
                col = c * 4 + h
                R = Rb[rc % 2]; dT = decT[rc % 2]; rc += 1
                K.act(R[:], triT[:], AF.Copy, scale=dA[:, col:col + 1])
                K.mm(psb[4][:, 0:128], lhsT=ones_f[:], rhs=R[:], start=True, stop=False)
                K.mm(psb[4][:, 0:128], lhsT=ident[:], rhs=maskneg[:], start=False, stop=True)
                K.act(dT[:], psb[4][:, 0:128], AF.Exp, bias=ncs[:, col:col + 1])
                K.tt(MT[b][:, h, :], psb[3][:, (h // 2) * 128:(h // 2 + 1) * 128], dT[:], ALU.mult)
            for i, (src, cc) in enumerate(((xsT, 0), (xsT, 1), (BT, 0), (BT, 1))):
                K.tr(pbf[:, i * 128:(i + 1) * 128], src[:, cc, tk], ident_bf[:])
            K.cp(xs_tm[b][:], pbf[:, 0:256], eng="act")
            K.cp(B_tm[b][:], pbf[:, 256:512], eng="act")
            K.tt(xdt[b][:].rearrange("p (h e) -> p h e", h=4), xs_tm[b][:].rearrange("p (h e) -> p h e", h=4),
                 dt[:, c * 4:(c + 1) * 4].unsqueeze(2).to_broadcast([128, 4, 64]), ALU.mult)
            K.tt(xdt2[b][:].rearrange("p (h e) -> p h e", h=4), xdt[b][:].rearrange("p (h e) -> p h e", h=4),
                 dst_[:, c * 4:(c + 1) * 4].unsqueeze(2).to_broadcast([128, 4, 64]), ALU.mult)
            K.cp(prevb[b][:], state[:], eng="act")
            for h in range(4):
                K.mm(psb[5][:, h * 64:(h + 1) * 64], lhsT=MT[b][:, h, :], rhs=xdt[b][:, h * 64:(h + 1) * 64])
            for h in range(4):
                K.mm(psb[5][:, 256 + h * 64:256 + (h + 1) * 64], lhsT=CT[:, h // 2, tk], rhs=prevb[b][:, h * 64:(h + 1) * 64])
            for h in range(4):
                K.mm(psb[6][:, h * 64:(h + 1) * 64], lhsT=B_tm[b][:, (h // 2) * 128:(h // 2 + 1) * 128],
                     rhs=xdt2[b][:, h * 64:(h + 1) * 64])
            K.tt(state[:].rearrange("p (h e) -> p h e", h=4), state[:].rearrange("p (h e) -> p h e", h=4),
                 cdec[:, c * 4:(c + 1) * 4].unsqueeze(2).to_broadcast([128, 4, 64]), ALU.mult)
            K.tt(state[:], psb[6][:, 0:256], state[:], ALU.add)
            y = yy[b]
            K.tt(y[:].rearrange("p (h e) -> p h e", h=4), psb[5][:, 256:512].rearrange("p (h e) -> p h e", h=4),
                 expcs[:, c * 4:(c + 1) * 4].unsqueeze(2).to_broadcast([128, 4, 64]), ALU.mult)
            K.tt(y[:], psb[5][:, 0:256], y[:], ALU.add)
            K.tt(y2[b][:], xs_tm[b][:], Dfull[:], ALU.mult, eng="pool")
            K.tt(y[:], y[:], y2[b][:], ALU.add)
            pb = psb[c % 2]
            for kc in range(8):
                K.mm(pb[:, 0:256], lhsT=xT[:, kc, tk], rhs=wv[:, kc, 0:256], start=(kc == 0), stop=(kc == 7),
                     rk=[wb, ("xT", c // 4)])
            K.act(zs[b][:], pb[:, 0:256], AF.Tanh, scale=0.5)
            K.stt(zs[b][:], zs[b][:], 1.0, pb[:, 0:256], ALU.add, ALU.mult)
            K.stt(y[:], y[:], 0.5, zs[b][:], ALU.mult, ALU.mult)
            K.act(y2[b][:], y[:], AF.Square)
            nc_ = nc
            sq = y2[b]; sso = ss[b]
            K.S.op("dve", lambda sq=sq, sso=sso: nc_.vector.tensor_reduce(out=sso[:], in_=sq[:].rearrange("p (g e) -> p g e", g=2), axis=AX.X, op=ALU.add),
                   reads=[sq], writes=[sso])
            K.ts(sso[:], sso[:], 1.0 / 128, ALU.mult, RMS_EPS, ALU.add)
            K.tt(sso[:], sso[:], mhalf[:, 0:2], ALU.pow, eng="pool")
            for g in range(2):
                K.ts(y[:, g * 128:(g + 1) * 128], y[:, g * 128:(g + 1) * 128], sso[:, g:g + 1], ALU.mult)
            K.tt(y[:], y[:], normw[:], ALU.mult)
            for g in range(2):
                K.tr(psb[2][:, g * 128:(g + 1) * 128], y[:, g * 128:(g + 1) * 128], ident[:])
            K.cp(yT[:, :, tk], psb[2][:, 0:256].rearrange("p (g e) -> p g e", g=2))
        K.phase_end()


    def hgrn(l):
        K.phase_begin()
        wb, wv = getw(("h", l), w_in_d[l, :, C_HGRN:C_HGRN + 1024], 8, 1024)
        pre[("f0", l)] = (wload(w_up_d[l, :, 0:512], 8, 512), wload(w_down_d[l, 0:512, :], 4, D, off=4096, new=False))
        tri16 = K.sb("tri16", [128, 128]); blk16 = K.sb("blk16", [128, 128]); bm = K.sb("bm", [128, 8])
        bones = K.sb("bones", [128, 128]); hgnw = K.sb("hgnw", [128, 2])
        lbb = K.sb("lbb", [128, 256]); omlb = K.sb("omlb", [128, 256])
        K.dma(tri16[:], tri16_d, wk=[tri16]); K.dma(blk16[:], blk16_d, wk=[blk16]); K.dma(bm[:], bm_d, wk=[bm])
        dif16 = K.sb("dif16", [128, 128])
        K.tt(dif16[:], blk16[:], tri16[:], ALU.subtract)
        K.dma(bones[:], bones64_d, wk=[bones]); K.dma(hgnw[:], hgnw_d[l], wk=[hgnw])
        if l == 0:
            K.memset(lbb[:], 0.0)
        else:
            K.dma(lbb[:], lowb_d[1:2, :].partition_broadcast(128), wk=[lbb])
            K.dma(omlb[:], lowb_d[0:1, :].partition_broadcast(128), wk=[omlb])
            K.tt(lbb[:], lbb[:], omlb[:], ALU.subtract)
            K.act(lbb[:], lbb[:], AF.Sigmoid)
        K.ts(omlb[:], lbb[:], -0.5, ALU.mult, 0.5, ALU.add)
        K.tt(lbb[:], lbb[:], omlb[:], ALU.add)
        K.ts(hgnw[:], hgnw[:], 0.5, ALU.mult)
        epsc = K.sb("epsc", [128, 1]); K.memset(epsc[:], RMS_EPS)
        Sst = [K.sb(f"Sst{i}", [128, 9, 64]) for i in range(2)]; SbBD = K.sb("SbBD", [128, 8, 2, 128], BF16)
        for i in range(2):
            K.memset(Sst[i][:, 0, :], 0.0)
        K.memset(SbBD[:], 0.0, eng="pool")
        A = lambda nm, shp, dt_=F32: [K.sb(f"{nm}{i}", shp, dt_) for i in range(2)]
        sg = A("hsg", [128, 256]); fg = A("hfg", [128, 256]); logf = A("hlogf", [128, 256]); kk = A("hkk", [128, 256])
        qs = A("hqs", [128, 256]); V = A("hV", [128, 256], BF16); bb = A("hb", [128, 256]); eb = A("heb", [128, 256])
        enb = A("henb", [128, 256]); ed = A("hed", [128, 256]); tot = A("htot", [128, 256])
        qbar = A("hqbar", [128, 256]); kbar = A("hkbar", [128, 256]); kdec = A("hkdec", [128, 256], BF16)
        qkT = A("hqkT", [128, 4, 128], BF16); totT = A("htotT", [128, 2, 128]); attT = A("hattT", [128, 4, 128], BF16)
        kdb = A("hkdb", [128, 8, 256], BF16); sq = A("hsq", [128, 256]); rstd = A("hrstd", [128, 256]); oo = A("hoo", [128, 256])
        for t in range(NT):
            b = t % 2
            tk = slice(t * 128, (t + 1) * 128)
            if t > 0:
                for i in range(2):
                    K.cp(Sst[i][:, 0, :], Sst[i][:, 8, :], eng=("dve" if i == 0 else "pool"))
            for kc in range(8):
                K.mm(psb[0][:, :], lhsT=xT[:, kc, tk], rhs=wv[:, kc, 0:512], start=(kc == 0), stop=(kc == 7),
                     rk=[wb, ("xT", t // 4)])
            for kc in range(8):
                K.mm(psb[1][:, 0:256], lhsT=xT[:, kc, tk], rhs=wv[:, kc, 512:768], start=(kc == 0), stop=(kc == 7),
                     rk=[wb, ("xT", t // 4)])
            for c2 in range(2):
                for kc in range(8):
                    K.mm(psb[1][:, 256 + c2 * 128:256 + (c2 + 1) * 128], lhsT=wv[:, kc, 768 + c2 * 128:768 + (c2 + 1) * 128],
                         rhs=xT[:, kc, tk], start=(kc == 0), stop=(kc == 7), rk=[wb, ("xT", t // 4)])
            K.act(sg[b][:], psb[1][:, 256:512], AF.Tanh, scale=0.5)
            K.stt(sg[b][:], sg[b][:], 1.0, psb[1][:, 256:512], ALU.add, ALU.mult)
            for hh in range(2):
                K.ts(sg[b][:, hh * 128:(hh + 1) * 128], sg[b][:, hh * 128:(hh + 1) * 128], hgnw[:, hh:hh + 1], ALU.mult)
            K.act(fg[b][:], psb[0][:, 256:512], AF.Tanh, scale=0.5)
            K.tt(fg[b][:], fg[b][:], omlb[:], ALU.mult)
            K.tt(fg[b][:], fg[b][:], lbb[:], ALU.add)
            K.act(logf[b][:], fg[b][:], AF.Ln)
            K.ts(kk[b][:], fg[b][:], -1.0, ALU.mult, 1.0, ALU.add)
            K.act(qs[b][:], psb[0][:, 0:256], AF.Tanh, scale=0.5)
            K.stt(qs[b][:], qs[b][:], 1.0, psb[0][:, 0:256], ALU.add, ALU.mult)
            K.cp(V[b][:], psb[1][:, 0:256], eng="act")
            K.mm(psb[2][:, 0:256], lhsT=tri16[:], rhs=logf[b][:])
            K.mm(psb[2][:, 256:512], lhsT=blk16[:], rhs=logf[b][:])
            K.mm(psb[3][:, 0:256], lhsT=dif16[:], rhs=logf[b][:])
            K.act(eb[b][:], psb[2][:, 0:256], AF.Exp)
            K.act(enb[b][:], psb[2][:, 0:256], AF.Exp, scale=-1.0)
            K.act(ed[b][:], psb[3][:, 0:256], AF.Exp)
            K.act(tot[b][:], psb[2][:, 256:512], AF.Exp)
            K.stt(qbar[b][:], qs[b][:], 0.5, eb[b][:], ALU.mult, ALU.mult)
            K.tt(kbar[b][:], kk[b][:], enb[b][:], ALU.mult, eng="pool")
            K.tt(kdec[b][:], kk[b][:], ed[b][:], ALU.mult)
            for i, src in enumerate((qbar[b], qbar[b], kbar[b], kbar[b])):
                K.tr(psb[3][:, i * 128:(i + 1) * 128], src[:, (i % 2) * 128:(i % 2 + 1) * 128], ident[:])
            for i in range(2):
                K.tr(psb[2][:, i * 128:(i + 1) * 128], tot[b][:, i * 128:(i + 1) * 128], ident[:])
            K.cp(qkT[b][:].rearrange("p a t -> p (a t)"), psb[3][:, :], eng="act")
            K.cp(totT[b][:].rearrange("p a t -> p (a t)"), psb[2][:, 0:256], eng="act")
            for h in range(4):
                po = (h % 2) * 64; hh = h // 2
                bank = psb[4] if h % 2 == 0 else psb[5]
                K.mm(bank[:, hh * 128:(hh + 1) * 128], lhsT=qkT[b][po:po + 64, 2 + hh, :], rhs=qkT[b][po:po + 64, hh, :])
            for hl in range(2):
                bank = psb[4] if hl == 0 else psb[5]
                K.tt(attT[b][:, hl * 2:(hl + 1) * 2, :], bank[:, 0:256].rearrange("p (a t) -> p a t", a=2),
                     tri16[:].unsqueeze(1).to_broadcast([128, 2, 128]), ALU.mult)
            K.tt(kdb[b][:], kdec[b][:].unsqueeze(1).to_broadcast([128, 8, 256]),
                 bm[:].unsqueeze(2).to_broadcast([128, 8, 256]), ALU.mult)
            for half in range(2):
                for c in range(half * 4, half * 4 + 4):
                    for h in range(4):
                        hl = h % 2; hh = h // 2
                        co = ((c % 4) * 2 + hh) * 64
                        K.mm(psb[6][hl * 64:(hl + 1) * 64, co:co + 64], lhsT=kdb[b][:, c, h * 64:(h + 1) * 64],
                             rhs=V[b][:, h * 64:(h + 1) * 64])
                for c in range(half * 4, half * 4 + 4):
                    for hh in range(2):
                        co = ((c % 4) * 2 + hh) * 64
                        K.stt(Sst[hh][:, c + 1, :], Sst[hh][:, c, :], totT[b][:, hh, c * 16:c * 16 + 1], psb[6][:, co:co + 64],
                              ALU.mult, ALU.add)
            for hh in range(2):
                K.cp(SbBD[0:64, :, hh, 0:64], Sst[hh][0:64, 0:8, :], eng="act")
                K.cp(SbBD[64:128, :, hh, 64:128], Sst[hh][64:128, 0:8, :])
            for hh in range(2):
                for hl in range(2):
                    h = 2 * hh + hl
                    K.mm(psb[4][hl * 64:(hl + 1) * 64, 256 + hh * 128:256 + (hh + 1) * 128], lhsT=V[b][:, h * 64:(h + 1) * 64],
                         rhs=attT[b][:, hl * 2 + hh, :], start=True, stop=False)
                for c in range(8):
                    K.mm(psb[4][:, 256 + hh * 128 + c * 16:256 + hh * 128 + (c + 1) * 16], lhsT=SbBD[:, c, hh, :],
                         rhs=qkT[b][:, hh, c * 16:(c + 1) * 16], start=False, stop=(c == 7))
            K.act(sq[b][:], psb[4][:, 256:512], AF.Square)
            K.mm(psb[5][:, 256:512], lhsT=bones[:], rhs=sq[b][:])
            K.act(rstd[b][:], psb[5][:, 256:512], AF.Ln, bias=epsc[:, 0:1], scale=1.0 / 64)
            K.act(rstd[b][:], rstd[b][:], AF.Exp, scale=-0.5)
            K.tt(oo[b][:], psb[4][:, 256:512], rstd[b][:], ALU.mult)
            K.tt(yT[:, :, tk], oo[b][:].rearrange("p (a t) -> p a t", a=2), sg[b][:].rearrange("p (a t) -> p a t", a=2), ALU.mult)
        K.phase_end()


    def rwkv(l):
        K.phase_begin()
        v4 = lambda ap: ap.rearrange("p (h e) -> p h e", h=4)
        b4 = lambda ap: ap.unsqueeze(2).to_broadcast([128, 4, 64])
        wb, wv = getw(("r", l), w_in_d[l, :, 0:896], 8, 896)
        if "b" in parts:
            prefetch(("a", l), w_in_d[l, :, C_ATT:C_ATT + 768], 8, 768)
        names = {}

        def bc(name, src, n=256):
            t_ = K.sb(name, [128, n]); K.dma(t_[:], src.partition_broadcast(128), wk=[t_]); return t_
        mucol = K.sb("mucol", [128, 1]); omucol = K.sb("omucol", [128, 1])
        K.dma(mucol[:], rmucol_d[l], wk=[mucol])
        K.ts(omucol[:], mucol[:], -1.0, ALU.mult, 1.0, ALU.add)
        fT = K.sb("fT", [128, T + 1]); loraT = K.sb("loraT", [128, T], BF16); ftmp = K.sb("ftmp", [128, T])
        K.memset(fT[:, 0:1], 0.0)
        for nb in range(4):
            pb = psb[nb % 2]
            for kc in range(8):
                K.mm(pb[:, :], lhsT=wv[:, kc, 768:896], rhs=xT[:, kc, nb * 512:(nb + 1) * 512], start=(kc == 0), stop=(kc == 7),
                     rk=[wb, ("xT", nb)])
            K.act(fT[:, 1 + nb * 512:1 + (nb + 1) * 512], pb[:, :], AF.Copy)
        K.ts(ftmp[:], fT[:, 0:T], mucol[:, 0:1], ALU.mult)
        K.stt(ftmp[:], fT[:, 1:T + 1], omucol[:, 0:1], ftmp[:], ALU.mult, ALU.add)
        K.act(loraT[0:32, :], ftmp[0:32, :], AF.Tanh)
        K.act(loraT[32:64, :], ftmp[32:64, :], AF.Copy)
        K.act(ftmp[64:128, :], ftmp[64:128, :], AF.Tanh, scale=0.5)
        K.ts(loraT[64:128, :], ftmp[64:128, :], 0.5, ALU.mult, 0.5, ALU.add)
        K.dma(lscr_d, loraT[:], rk=[loraT], wk=["lscr"])
        if l > 0:
            wb2, wv2 = wsmall("wvres", wvres_d[0], 8, 32)
            vmu = K.sb("vmu", [128, 1]); ovmu = K.sb("ovmu", [128, 1])
            K.dma(vmu[:], vmucol_d, wk=[vmu])
            K.ts(ovmu[:], vmu[:], -1.0, ALU.mult, 1.0, ALU.add)
            vrT = K.sb("vrT", [32, T], BF16)
            for nb in range(4):
                pb = psb[nb % 2]
                for kc in range(8):
                    K.mm(pb[0:32, :], lhsT=wv2[:, kc, 0:32], rhs=xT[:, kc, nb * 512:(nb + 1) * 512], start=(kc == 0), stop=(kc == 7),
                         rk=[wb2, ("xT", nb)])
                K.act(fT[0:32, 1 + nb * 512:1 + (nb + 1) * 512], pb[0:32, :], AF.Copy)
            K.ts(ftmp[0:32, :], fT[0:32, 0:T], vmu[0:32, 0:1], ALU.mult)
            K.stt(ftmp[0:32, :], fT[0:32, 1:T + 1], ovmu[0:32, 0:1], ftmp[0:32, :], ALU.mult, ALU.add)
            K.act(vrT[:], ftmp[0:32, :], AF.Copy)
            K.dma(vrscr_d, vrT[:], rk=[vrT], wk=["vrscr"])
        K.phase_end()
        K.phase_begin()
        mu_b = bc("mu_b", mush_d[l:l + 1, 0:768], 768)
        w0_b = bc("w0_b", rw0_d[l:l + 1, :]); a0_b = bc("a0_b", ra0_d[l:l + 1, :]); kk_b = bc("kk_b", rkk_d[l:l + 1, :])
        ka_b = bc("ka_b", rka_d[l:l + 1, :]); lw_b = bc("lnxw_b", rlw_d[l:l + 1, :]); lb_b = bc("lnxb_b", rlb_d[l:l + 1, :])
        rk_b = bc("rk_b", rrk_d[l:l + 1, :])
        msl = K.sb("msl", [128, 128]); msu = K.sb("msu", [128, 128])
        shiftM = K.sb("shiftM", [128, 128], BF16); carryM = K.sb("carryM", [128, 128], BF16)
        for t_, d_ in ((msl, msl_d), (msu, msu_d)):
            K.dma(t_[:], d_, wk=[t_])
        for t_, d_ in ((shiftM, shift_d), (carryM, carry_d)):
            K.dma(t_[:], d_, eng="pool", wk=[t_])
        loraW = K.sb("loraW", [128, 768], BF16)
        K.memset(loraW[:], 0.0, eng="pool")
        K.dma(loraW[0:32, 0:256], rw2_d[l], eng="pool", rk=[loraW], wk=[loraW])
        K.dma(loraW[32:64, 256:512], ra2_d[l], eng="pool", rk=[loraW], wk=[loraW])
        K.dma(loraW[64:128, 512:768], rg2_d[l], eng="pool", rk=[loraW], wk=[loraW])
        loraTc = [K.sb(f"loraTc{i}", [128, 128], BF16) for i in range(2)]
        if l > 0:
            v0_b = bc("v0_b", rv0_d[0:1, :])
            v2W = K.sb("v2W", [32, 256], BF16)
            K.dma(v2W[:], rv2_d[0], eng="pool", wk=[v2W])
            vrTc = [K.sb(f"vrTc{i}", [32, 128], BF16) for i in range(2)]
        F = lambda nm, n=256, dt_=F32: K.sb(nm, [128, n], dt_)
        D2 = lambda nm, n=256, dt_=F32: [K.sb(f"{nm}{i}", [128, n], dt_) for i in range(2)]
        fsb = D2("fsb", 768, BF16)
        fl = F("fl", 768)
        lw = F("lw"); aa = F("aa"); vv = F("vv"); kkn = F("kkn"); t1 = F("t1"); t2 = F("t2"); t3 = F("t3"); kt = F("kt"); be = F("be")
        s4 = K.sb("s4", [128, 4]); s4b = K.sb("s4b", [128, 4])
        Lsb = F("Lsb"); E1 = F("E1"); E2 = F("E2"); E3 = F("E3"); E4 = F("E4")
        gg2 = D2("gg"); bonus2 = D2("bonus")
        btil2 = D2("btil", 256, BF16); ktil2 = D2("ktil", 256, BF16); Vb2 = D2("rVb", 256, BF16)
        gam2 = [K.sb(f"gam{i}", [128, 2]) for i in range(2)]
        abkrT2 = [K.sb(f"abkrT{i}", [128, 8, 128], BF16) for i in range(2)]
        Am = [K.sb(f"Am{i}", [128, 4, 128], BF16) for i in range(2)]
        Bm = [K.sb(f"Bm{i}", [128, 4, 128], BF16) for i in range(2)]
        AKT2 = [K.sb(f"AKT{i}", [128, 4, 128], BF16) for i in range(2)]
        RBT2 = [K.sb(f"RBT{i}", [128, 4, 128], BF16) for i in range(2)]
        RKT2 = [K.sb(f"RKT{i}", [128, 4, 128], BF16) for i in range(2)]
        Pb2 = [K.sb(f"Pb{i}", [128, 4, 128], BF16) for i in range(2)]
        Sf = K.sb("Sf", [128, 2, 64]); SBD = K.sb("SBD", [128, 2, 128], BF16)
        K.memset(Sf[:], 0.0); K.memset(SBD[:], 0.0)
        Wb_ = F("Wb", 256, BF16); Ub_ = F("Ub", 256, BF16); yv = F("yv"); ysq = F("ysq", 256, BF16)
        mean4 = K.sb("mean4", [128, 4]); var4 = K.sb("var4", [128, 4])
        m3 = lambda mk: mk[:].unsqueeze(1).to_broadcast([128, 2, 128])
        X = pbf[:].bitcast(F32)
        for c in range(NT):
            par = c % 2
            tk = slice(c * 128, (c + 1) * 128)
            f = fsb[c % 2]; fp_ = fsb[(c + 1) % 2]
            gg = gg2[par]; bonus = bonus2[par]; btil = btil2[par]; ktil = ktil2[par]; Vb_ = Vb2[par]; gam = gam2[par]
            abkrT = abkrT2[par]; AKT = AKT2[par]; RBT = RBT2[par]; RKT = RKT2[par]; Pb = Pb2[par]
            for (bank, o0, c0, c1) in ((psb[0], 0, 0, 512), (psb[1], 0, 512, 768)):
                for kc in range(8):
                    K.mm(bank[:, o0:o0 + c1 - c0], lhsT=xT[:, kc, tk], rhs=wv[:, kc, c0:c1], start=(kc == 0), stop=(kc == 7),
                         rk=[wb, ("xT", c // 4)])
            K.cp(f[:, 0:512], psb[0][:, :], eng="act")
            K.cp(f[:, 512:768], psb[1][:, 0:256], eng="act")
            for (bank, o0, c0, c1) in ((psb[2], 0, 0, 512), (psb[1], 256, 512, 768)):
                K.mm(bank[:, o0:o0 + c1 - c0], lhsT=shiftM[:], rhs=f[:, c0:c1], start=True, stop=(c == 0))
                if c > 0:
                    K.mm(bank[:, o0:o0 + c1 - c0], lhsT=carryM[:], rhs=fp_[:, c0:c1], start=False, stop=True)
            K.tt(fl[:, 0:512], psb[2][:, :], mu_b[:, 0:512], ALU.mult)
            K.tt(fl[:, 512:768], psb[1][:, 256:512], mu_b[:, 512:768], ALU.mult)
            K.tt(fl[:], fl[:], f[:], ALU.add, eng="pool")
            r_ = fl[:, 0:256]; k_ = fl[:, 256:512]; v_ = fl[:, 512:768]
            lt = loraTc[c % 2]
            K.dma(lt[:], lscr_d[:, tk], rk=["lscr", lt], wk=[lt])
            K.mm(psb[3][:, 0:512], lhsT=lt[:], rhs=loraW[:, 0:512])
            K.mm(psb[0][:, 0:256], lhsT=lt[:], rhs=loraW[:, 512:768])
            K.tt(lw[:], psb[3][:, 0:256], w0_b[:], ALU.add)
            K.act(lw[:], lw[:], AF.Tanh, scale=0.5)
            K.ts(lw[:], lw[:], -0.3032653298563167, ALU.mult, -0.3032653298563167, ALU.add, eng="pool")
            K.tt(aa[:], psb[3][:, 256:512], a0_b[:], ALU.add)
            K.act(aa[:], aa[:], AF.Tanh, scale=0.5)
            K.ts(aa[:], aa[:], 0.5, ALU.mult, 0.5, ALU.add, eng="pool")
            K.cp(gg[:], psb[0][:, 0:256], eng="act")
            if l == 0:
                K.cp(vv[:], v_, eng="pool")
                K.dma(vfirst_d[tk, :], vv[:], rk=[vv], wk=[("vfirst", c)])
            else:
                vt_ = vrTc[c % 2]
                K.dma(vt_[:], vrscr_d[:, tk], rk=["vrscr", vt_], wk=[vt_])
                K.mm(psb[0][:, 256:512], lhsT=vt_[0:32, :], rhs=v2W[0:32, :])
                K.tt(t1[:], psb[0][:, 256:512], v0_b[:], ALU.add)
                K.act(t1[:], t1[:], AF.Tanh, scale=0.5)
                K.ts(t1[:], t1[:], 0.5, ALU.mult, 0.5, ALU.add, eng="pool")
                K.dma(t2[:], vfirst_d[tk, :], rk=[("vfirst", c), t2], wk=[t2])
                K.tt(t2[:], t2[:], v_, ALU.subtract, eng="pool")
                K.tt(t2[:], t2[:], t1[:], ALU.mult, eng="pool")
                K.tt(vv[:], t2[:], v_, ALU.add, eng="pool")
            K.cp(Vb_[:], vv[:], eng="act")
            K.tt(kkn[:], k_, kk_b[:], ALU.mult, eng="pool")
            K.act(t1[:], kkn[:], AF.Square)
            K.S.op("dve", lambda t1=t1: nc.vector.tensor_reduce(out=s4[:], in_=v4(t1[:]), axis=AX.X, op=ALU.add), reads=[t1], writes=[s4])
            K.ts(s4[:], s4[:], 1e-24, ALU.max)
            K.tt(s4[:], s4[:], mhalf[:, 0:4], ALU.pow, eng="pool")
            K.tt(v4(kkn[:]), v4(kkn[:]), b4(s4[:]), ALU.mult)
            K.stt(t3[:], aa[:], -1.0, ka_b[:], ALU.add, ALU.mult)
            K.stt(kt[:], t3[:], 1.0, k_, ALU.add, ALU.mult)
            K.tt(be[:], kkn[:], aa[:], ALU.mult, eng="pool")
            K.tt(t3[:], r_, kt[:], ALU.mult, eng="pool")
            K.tt(t3[:], t3[:], rk_b[:], ALU.mult, eng="pool")
            K.S.op("dve", lambda t3=t3: nc.vector.tensor_reduce(out=s4b[:], in_=v4(t3[:]), axis=AX.X, op=ALU.add), reads=[t3], writes=[s4b])
            K.tt(v4(bonus[:]), v4(vv[:]), b4(s4b[:]), ALU.mult)
            K.mm(psb[2][:, 0:256], lhsT=triT[:], rhs=lw[:])
            K.mm(psb[2][:, 256:512], lhsT=msu[:], rhs=lw[:])
            K.mm(psb[3][:, 256:512], lhsT=msl[:], rhs=lw[:])
            for hh in range(2):
                K.mm(psb[3][:, hh:hh + 1], lhsT=lw[:, hh * 128:(hh + 1) * 128], rhs=ones_f[:, 0:1])
            K.act(gam[:], psb[3][:, 0:2], AF.Exp)
            K.act(E1[:], psb[2][:, 256:512], AF.Exp)
            K.act(E2[:], psb[2][:, 0:256], AF.Exp, scale=-1.0)
            K.act(E3[:], psb[2][:, 0:256], AF.Exp)
            K.act(E4[:], psb[3][:, 256:512], AF.Exp)
            K.tt(btil[:], be[:], E4[:], ALU.mult, eng="pool")
            K.tt(ktil[:], kt[:], E4[:], ALU.mult)
            K.stt(E1[:], kkn[:], -1.0, E1[:], ALU.mult, ALU.mult)
            K.tt(be[:], be[:], E2[:], ALU.mult, eng="pool")
            K.tt(kt[:], kt[:], E2[:], ALU.mult)
            K.tt(E3[:], r_, E3[:], ALU.mult, eng="pool")
            for qi, src in enumerate((E1, be, kt, E3)):
                bank = psb[0] if qi < 2 else psb[1]
                for hh in range(2):
                    K.tr(bank[:, ((qi % 2) * 2 + hh) * 128:((qi % 2) * 2 + hh + 1) * 128], src[:, hh * 128:(hh + 1) * 128], ident[:])
            K.cp(abkrT[:, 0:4, :].rearrange("p a t -> p (a t)"), psb[0][:, :], eng="act")
            K.cp(abkrT[:, 4:8, :].rearrange("p a t -> p (a t)"), psb[1][:, :])
            aT = lambda h, abkrT=abkrT: abkrT[(h % 2) * 64:(h % 2) * 64 + 64, 0 + h // 2, :]
            bT = lambda h, abkrT=abkrT: abkrT[(h % 2) * 64:(h % 2) * 64 + 64, 2 + h // 2, :]
            kT_ = lambda h, abkrT=abkrT: abkrT[(h % 2) * 64:(h % 2) * 64 + 64, 4 + h // 2, :]
            rT = lambda h, abkrT=abkrT: abkrT[(h % 2) * 64:(h % 2) * 64 + 64, 6 + h // 2, :]

            def amat(L_, R_, mask, dst):
                for h in range(4):
                    bank = psb[4] if h % 2 == 0 else psb[5]
                    K.mm(bank[:, (h // 2) * 128:(h // 2 + 1) * 128], lhsT=L_(h), rhs=R_(h))
                for hl in range(2):
                    bank = psb[4] if hl == 0 else psb[5]
                    K.tt(dst[:, hl * 2:hl * 2 + 2, :], bank[:, 0:256].rearrange("p (a t) -> p a t", a=2), m3(mask), ALU.mult)
            hidx = lambda h: (h % 2) * 2 + h // 2
            amat(aT, bT, msl, Am[0])
            amat(bT, aT, msu, Bm[0])
            amat(kT_, aT, msu, AKT)
            amat(bT, rT, triT, RBT)
            amat(kT_, rT, triT, RKT)
            K.tt(Pb[:], Bm[0][:], ident[:].unsqueeze(1).to_broadcast([128, 4, 128]), ALU.add, eng="pool")
            cur = 0
            for j in range(1, 7):
                nxt = 1 - cur
                for hi in range(4):
                    K.mm(psb[4][:, hi * 128:(hi + 1) * 128], lhsT=Bm[cur][:, hi, :], rhs=Am[cur][:, hi, :])
                if j < 6:
                    for hi in range(4):
                        K.mm(psb[5][:, hi * 128:(hi + 1) * 128], lhsT=Am[cur][:, hi, :], rhs=Bm[cur][:, hi, :])
                K.cp(Am[nxt][:].rearrange("p a t -> p (a t)"), psb[4][:, :], eng="act")
                if j < 6:
                    K.cp(Bm[nxt][:].rearrange("p a t -> p (a t)"), psb[5][:, :])
                for hi in range(4):
                    K.mm(psb[6][:, hi * 128:(hi + 1) * 128], lhsT=Am[nxt][:, hi, :], rhs=Pb[:, hi, :])
                K.tt(Pb[:].rearrange("p a t -> p (a t)"), psb[6][:, :], Pb[:].rearrange("p a t -> p (a t)"), ALU.add)
                cur = nxt
            for hh in range(2):
                K.mm(X[:, hh * 128:(hh + 1) * 128], lhsT=abkrT[:, 0 + hh, :], rhs=SBD[:, hh, :], start=True, stop=False)
                for hl in range(2):
                    h = 2 * hh + hl
                    K.mm(X[:, h * 64:(h + 1) * 64], lhsT=AKT[:, hidx(h), :], rhs=Vb_[:, h * 64:(h + 1) * 64], start=False, stop=(hl == 1))
            K.cp(Wb_[:], X[:, 0:256], eng="act")
            for h in range(4):
                K.mm(X[:, 256 + h * 64:256 + (h + 1) * 64], lhsT=Pb[:, hidx(h), :], rhs=Wb_[:, h * 64:(h + 1) * 64])
            K.cp(Ub_[:], X[:, 256:512], eng="act")
            for hh in range(2):
                K.mm(X[:, hh * 128:(hh + 1) * 128], lhsT=abkrT[:, 6 + hh, :], rhs=SBD[:, hh, :], start=True, stop=False)
                for hl in range(2):
                    h = 2 * hh + hl
                    K.mm(X[:, h * 64:(h + 1) * 64], lhsT=RBT[:, hidx(h), :], rhs=Ub_[:, h * 64:(h + 1) * 64], start=False, stop=False)
                    K.mm(X[:, h * 64:(h + 1) * 64], lhsT=RKT[:, hidx(h), :], rhs=Vb_[:, h * 64:(h + 1) * 64], start=False, stop=(hl == 1))
            for h in range(4):
                hl = h % 2; hh = h // 2
                K.mm(X[hl * 64:(hl + 1) * 64, 256 + hh * 64:256 + (hh + 1) * 64], lhsT=btil[:, h * 64:(h + 1) * 64],
                     rhs=Ub_[:, h * 64:(h + 1) * 64], start=True, stop=False)
                K.mm(X[hl * 64:(hl + 1) * 64, 256 + hh * 64:256 + (hh + 1) * 64], lhsT=ktil[:, h * 64:(h + 1) * 64],
                     rhs=Vb_[:, h * 64:(h + 1) * 64], start=False, stop=True)
            K.cp(yv[:], X[:, 0:256], eng="act")
            for hh in range(2):
                K.stt(Sf[:, hh, :], Sf[:, hh, :], gam[:, hh:hh + 1], X[:, 256 + hh * 64:256 + (hh + 1) * 64], ALU.mult, ALU.add)
            K.cp(SBD[0:64, :, 0:64], Sf[0:64, :, :], eng="act")
            K.cp(SBD[64:128, :, 64:128], Sf[64:128, :, :])
            K.S.op("dve", lambda: nc.vector.tensor_reduce(out=mean4[:], in_=v4(yv[:]), axis=AX.X, op=ALU.add), reads=[yv], writes=[mean4])
            K.ts(mean4[:], mean4[:], 1.0 / 64, ALU.mult)
            K.tt(v4(yv[:]), v4(yv[:]), b4(mean4[:]), ALU.subtract)
            K.act(ysq[:], yv[:], AF.Square)
            K.S.op("dve", lambda: nc.vector.tensor_reduce(out=var4[:], in_=v4(ysq[:]), axis=AX.X, op=ALU.add), reads=[ysq], writes=[var4])
            K.ts(var4[:], var4[:], 1.0 / 64, ALU.mult, 64e-5, ALU.add)
            K.tt(var4[:], var4[:], mhalf[:, 0:4], ALU.pow, eng="pool")
            K.tt(v4(yv[:]), v4(yv[:]), b4(var4[:]), ALU.mult)
            K.tt(yv[:], yv[:], lw_b[:], ALU.mult, eng="pool")
            K.tt(yv[:], yv[:], lb_b[:], ALU.add, eng="pool")
            K.tt(yv[:], yv[:], bonus[:], ALU.add, eng="pool")
            K.tt(yv[:], yv[:], gg[:], ALU.mult, eng="pool")
            for g in range(2):
                K.tr(X[:, g * 128:(g + 1) * 128], yv[:, g * 128:(g + 1) * 128], ident[:])
            K.cp(yT[:, :, tk], X[:, 0:256].rearrange("p (g e) -> p g e", g=2))
        K.phase_end()
        K.mark("rwkv_end")

    def dump(m):
        if debug:
            K.dma(dbg_d[m], yT[:].rearrange("p c t -> p (c t)"), rk=[yT], wk=[("dbg", m)])

    for l in range(nlayers):
        for t in range(NT):
            K.act(xres[t][:], xres[t][:], AF.Copy, scale=ALPHA)
        if "a" in parts:
            rwkv(l); dump(0); out_proj(l, 0)
        if "b" in parts:
            attention(l); dump(1); out_proj(l, 1)
        if "c" in parts:
            ssd(l); dump(2); out_proj(l, 2)
        if "d" in parts:
            hgrn(l); dump(3); out_proj(l, 3)
        K.phase_begin()
        lnw = K.sb("lnw", [128, D]); lnb = K.sb("lnb", [128, D])
        K.dma(lnw[:], ln1w_d[l:l + 1, :].partition_broadcast(128), wk=[lnw])
        K.dma(lnb[:], ln1b_d[l:l + 1, :].partition_broadcast(128), wk=[lnb])
        for t in range(NT):
            layer_norm(t, lnw, lnb)
            build_xT(t)
        ffn(l)
        if l + 1 < nlayers and "a" in parts:
            prefetch(("r", l + 1), w_in_d[l + 1, :, 0:896], 8, 896)
        lnw = K.sb("lnw2", [128, D]); lnb = K.sb("lnb2", [128, D])
        K.dma(lnw[:], ln2w_d[l:l + 1, :].partition_broadcast(128), wk=[lnw])
        K.dma(lnb[:], ln2b_d[l:l + 1, :].partition_broadcast(128), wk=[lnb])
        for t in range(NT):
            layer_norm(t, lnw, lnb)
            if l == nlayers - 1:
                K.dma(out_d[t * 128:(t + 1) * 128, :], xres[t][:], rk=[xres[t]], wk=[("out", t)])
            else:
                build_xT(t)
        K.phase_end()
    K.S.emit(limit)
    K.S.stats["marks"] = dict(K.marks)
    return nc, K.S.stats


def make_consts():
    c = {}
    c["c_ident"] = np.eye(128, dtype=np.float32)
    j = np.arange(128)[:, None].astype(np.float64)
    i = np.arange(128)[None, :].astype(np.float64)
    am = np.zeros((128, 5, 4, 128), np.float32)
    for mid, (d, prev) in enumerate([(1, True), (1, False), (4, True), (4, False), (16, False)]):
        for h in range(4):
            if prev:
                dist = 128 + i - j
                valid = dist <= 128
            else:
                dist = i - j
                valid = dist >= 0
            am[:, mid, (h % 2) * 2 + h // 2, :] = np.where(valid, -SLOPES[h] * d * dist, NEG)
    c["c_amask"] = am.reshape(128, -1)
    ii = np.arange(128)
    c["c_triT"] = (ii[:, None] <= ii[None, :]).astype(np.float32)
    c["c_maskneg"] = np.where(ii[None, :] >= ii[:, None], 0.0, NEG).astype(np.float32)
    c["c_msl"] = (ii[None, :] < ii[:, None]).astype(np.float32)
    c["c_msu"] = (ii[:, None] < ii[None, :]).astype(np.float32)
    c["c_shift"] = (ii[None, :] == ii[:, None] + 1).astype(np.float32) - np.eye(128, dtype=np.float32)
    cm = np.zeros((128, 128), np.float32); cm[127, 0] = 1.0
    c["c_carry"] = cm
    same16 = (ii[:, None] // 16) == (ii[None, :] // 16)
    c["c_tri16"] = (same16 & (ii[:, None] <= ii[None, :])).astype(np.float32)
    c["c_blk16"] = same16.astype(np.float32)
    c["c_bm"] = ((ii[:, None] // 16) == np.arange(8)[None, :]).astype(np.float32)
    c["c_bones64"] = ((ii[:, None] // 64) == (ii[None, :] // 64)).astype(np.float32)
    return c


def make_params(inputs):
    p = {}
    cw = np.asarray(inputs["ssd_conv_w"], np.float32)
    cb = np.asarray(inputs["ssd_conv_b"], np.float32)
    pk = np.zeros((DEPTH, 128, 6, 5), np.float32)
    pk[:, :, :, 0:4] = cw.reshape(DEPTH, 4, 6, 128).transpose(0, 3, 2, 1)
    pk[:, :, :, 4] = cb.reshape(DEPTH, 6, 128).transpose(0, 2, 1)
    p["c_ssdconv"] = pk.reshape(DEPTH, 128, 30)
    p["c_rmucol"] = np.ascontiguousarray(np.asarray(inputs["mu_shift"], np.float32)[:, 768:896].reshape(DEPTH, 128, 1))
    vm = np.zeros((128, 1), np.float32); vm[0:32, 0] = np.asarray(inputs["mu_vres"], np.float32)[0]
    p["c_vmucol"] = vm
    p["c_rrk"] = np.ascontiguousarray(np.asarray(inputs["rwkv_r_k"], np.float32).reshape(DEPTH, 256))
    p["c_hgnw"] = np.ascontiguousarray(np.asarray(inputs["hgrn_norm_w"], np.float32).reshape(DEPTH, 2, 128).transpose(0, 2, 1))
    p["c_ssdD"] = np.repeat(np.asarray(inputs["ssd_D"], np.float32), 64, axis=1)
    return p


_CACHE = {}


SHARED = ("w_in", "w_out", "w_up", "w_down", "ln1_w", "ln1_b", "ln2_w", "ln2_b",
          "ssd_dt_bias", "ssd_A_log", "ssd_norm_w", "lower_bounds",
          "mu_shift", "rwkv_w0", "rwkv_a0", "rwkv_k_k", "rwkv_k_a", "rwkv_lnx_w", "rwkv_lnx_b", "rwkv_w2", "rwkv_a2",
          "rwkv_g2", "rwkv_v0", "rwkv_v2", "w_in_vres")


def make_inmap(inputs, b, consts=None, shared=None):
    if consts is None:
        consts = make_consts()
    if shared is None:
        shared = {k: np.ascontiguousarray(inputs[k], dtype=np.float32) for k in SHARED}
        shared.update(make_params(inputs))
    m = {"x": np.ascontiguousarray(inputs["x"][b], dtype=np.float32)}
    m.update(shared)
    m.update(consts)
    return m


def kernel(**inputs):
    if "prog" not in _CACHE:
        _CACHE["prog"] = build_program()
    nc, stats = _CACHE["prog"]
    consts = make_consts()
    shared = {k: np.ascontiguousarray(inputs[k], dtype=np.float32) for k in SHARED}
    shared.update(make_params(inputs))
    in_maps = [make_inmap(inputs, b, consts, shared) for b in range(8)]
    res = run_bass_kernel_spmd(nc, in_maps, core_ids=list(range(8)))
    return np.stack([np.asarray(r["out"], dtype=np.float32) for r in res.results], axis=0)
```

```python
import numpy as np
import concourse.bass as bass
import concourse.mybir as mybir
from concourse.bass_utils import run_bass_kernel_spmd

F32 = mybir.dt.float32
BF16 = mybir.dt.bfloat16
AF = mybir.ActivationFunctionType
ALU = mybir.AluOpType
AX = mybir.AxisListType

T = 2048
D = 1024
NT = 16
DEPTH = 2
ALPHA = (2.0 * DEPTH) ** 0.25
LN_EPS = 1e-5
RMS_EPS = 1e-5
IN_COLS = 3716
C_RWKV, C_ATT, C_SSD, C_HGRN = 0, 896, 1664, 2692
SLOPES = [2.0 ** (-8.0 * (h + 1) / 4) for h in range(4)]
NEG = -30000.0
STRICT = False


class Sched:
    LAT_SAME = 450.0
    LAT = 300.0

    def __init__(self, nc):
        self.nc = nc
        self.eng = {"pe": nc.tensor, "dve": nc.vector, "act": nc.scalar,
                    "pool": nc.gpsimd, "sp": nc.sync}
        self.ops = []
        self.info = []
        self.lastw = {}
        self.reads = {}
        self.fences = []
        self.reorder = True
        self.strict = STRICT

    @staticmethod
    def _key(a):
        if isinstance(a, (str, tuple)):
            return a
        return a.name

    def op(self, engine, fn, reads=(), writes=(), dma=False, cost=100.0):
        idx = len(self.ops)
        sem = set()
        order = set()
        rk = [self._key(a) for a in reads]
        wk = [self._key(a) for a in writes]
        for k in rk:
            if k in self.lastw:
                sem.add(self.lastw[k])
            if isinstance(k, str) and k.startswith("ps_"):
                for (e, i, d) in self.reads.get(k, ()):
                    if e != engine:
                        sem.add(i)
        for k in wk:
            if k in self.lastw:
                j = self.lastw[k]
                if dma or self.ops[j][3] or self.ops[j][0] != engine or (self.strict and engine != "pe"):
                    sem.add(j)
                else:
                    order.add(j)
            for (e, i, d) in self.reads.get(k, ()):
                if dma or d or e != engine or (self.strict and engine != "pe"):
                    sem.add(i)
                else:
                    order.add(i)
        sem.discard(idx)
        order.discard(idx)
        order -= sem
        self.info.append((engine, "dma" if dma else "", rk, wk))
        self.ops.append((engine, fn, sem, dma, order, float(cost)))
        for k in wk:
            self.lastw[k] = idx
            self.reads[k] = []
        for k in rk:
            self.reads.setdefault(k, []).append((engine, idx, dma))
        return idx

    def barrier(self):
        self.fences.append(len(self.ops))
        self.lastw = {}
        self.reads = {}

    def _schedule(self, seg):
        if not self.reorder or len(seg) < 3:
            return list(seg)
        ops = self.ops
        segset = set(seg)
        preds = {}
        succs = {i: [] for i in seg}
        indeg = {}
        for i in seg:
            p = [d for d in (ops[i][2] | ops[i][4]) if d in segset]
            preds[i] = p
            indeg[i] = len(p)
            for d in p:
                succs[d].append(i)
        finish = {}
        free = {e: 0.0 for e in self.eng}
        ready = {e: [] for e in self.eng}
        for i in seg:
            if indeg[i] == 0:
                ready[ops[i][0]].append((0.0, i))
        order = []
        nleft = len(seg)
        while nleft:
            best = None
            for e, lst in ready.items():
                if not lst:
                    continue
                f = free[e]
                cand = None
                for (dr, i) in lst:
                    st = dr if dr > f else f
                    key = (st, i)
                    if cand is None or key < cand:
                        cand = key
                if best is None or cand < best[0]:
                    best = (cand, e)
            (st, i), e = best
            ready[e] = [x for x in ready[e] if x[1] != i]
            eng, fn, sem, dma, od, cost = ops[i]
            if dma:
                issue = 1500.0 if e == "pool" else 150.0
                free[e] = st + issue
                finish[i] = st + issue + cost
            else:
                free[e] = st + cost
                finish[i] = st + cost
            order.append(i)
            nleft -= 1
            for s_ in succs[i]:
                indeg[s_] -= 1
                if indeg[s_] == 0:
                    dr = 0.0
                    for p in preds[s_]:
                        t = finish[p] + (self.LAT if ops[p][0] != ops[s_][0] else (self.LAT_SAME if p in ops[s_][2] else 0.0))
                        if t > dr:
                            dr = t
                    ready[ops[s_][0]].append((dr, s_))
        mk = max(list(finish.values()) + [0.0])
        self.est_ns = getattr(self, "est_ns", 0.0) + mk
        busy = {e: 0.0 for e in self.eng}
        for i in seg:
            if not ops[i][3]:
                busy[ops[i][0]] += ops[i][5]
        self.seglog = getattr(self, "seglog", [])
        self.seglog.append((len(seg), round(mk / 1e3, 1), {e: round(b / 1e3, 1) for e, b in busy.items()}))
        return order

    def emit(self, limit=None):
        nc = self.nc
        ops = self.ops
        n = len(ops)
        elimit = limit
        emitted = 0
        bounds = [0] + [f for f in self.fences if f < n] + [n]
        segs = [list(range(bounds[i], bounds[i + 1])) for i in range(len(bounds) - 1)]
        sched = [self._schedule(sg) for sg in segs]
        need = [False] * len(ops)
        for i in range(n):
            for d in ops[i][2]:
                need[d] = True
        for od in sched:
            last = {}
            for i in od:
                if not ops[i][3]:
                    last[ops[i][0]] = i
            for i in last.values():
                need[i] = True
        NQ = {"sp": 16, "pool": 8, "act": 4}

        def run(dry, need):
            csem = dsem = None
            if not dry:
                csem = {e: nc.alloc_semaphore(f"sem_{e}") for e in self.eng}
                dsem = {q: [nc.alloc_semaphore(f"dsem_{q}_{i}") for i in range(k)] for q, k in NQ.items()}
            dcount = {q: [0] * k for q, k in NQ.items()}
            ndma = {q: 0 for q in NQ}
            sig = [None] * len(ops)
            sigop = {}
            used = set()
            ccount = {e: 0 for e in self.eng}
            seen = {e: {} for e in self.eng}
            nw = [0]
            emitted = 0

            def wait(e, key, val):
                if val <= 0 or seen[e].get(key, 0) >= val:
                    return
                seen[e][key] = val
                nw[0] += 1
                if (key, val) in sigop:
                    used.add(sigop[(key, val)])
                if not dry:
                    semh = dsem[key[1]][key[2]] if key[0] == "d" else csem[key[1]]
                    self.eng[e].wait_ge(semh, val)

            for si_, od in enumerate(sched):
                if si_ > 0:
                    for e in self.eng:
                        for e2 in self.eng:
                            wait(e, ("c", e2), ccount[e2])
                        for q in NQ:
                            for k in range(NQ[q]):
                                wait(e, ("d", q, k), dcount[q][k])
                for i in od:
                    if elimit is not None and emitted >= elimit:
                        break
                    emitted += 1
                    e, fn, deps, dma, _, _ = ops[i]
                    wants = {}
                    for d in deps:
                        s_ = sig[d]
                        if s_ is None:
                            continue
                        key, val = s_
                        if wants.get(key, 0) < val:
                            wants[key] = val
                    if dma:
                        si = ndma[e] % NQ[e]
                        if dcount[e][si] > 0:
                            key = ("d", e, si)
                            if wants.get(key, 0) < dcount[e][si]:
                                wants[key] = dcount[e][si]
                    for key, val in wants.items():
                        wait(e, key, val)
                    inst = None if dry else fn()
                    if dma:
                        si = ndma[e] % NQ[e]
                        ndma[e] += 1
                        dcount[e][si] += 16
                        if not dry:
                            inst.then_inc(dsem[e][si], 16)
                        sig[i] = (("d", e, si), dcount[e][si])
                    elif need[i]:
                        ccount[e] += 1
                        if not dry:
                            inst.then_inc(csem[e], 1)
                        sig[i] = (("c", e), ccount[e])
                        sigop[sig[i]] = i
            if not dry:
                for q in NQ:
                    for si in range(NQ[q]):
                        if dcount[q][si] > 0:
                            nc.sync.wait_ge(dsem[q][si], dcount[q][si])
            return used, nw[0], ndma

        used, _, _ = run(True, need)
        need2 = [False] * len(ops)
        for i in used:
            need2[i] = True
        used2, nwaits, ndma = run(False, need2)
        self.nsig = sum(need2)
        self.stats = dict(n=n, nwaits=nwaits, nsig=self.nsig, ndma=ndma, est_us=getattr(self, "est_ns", 0.0) / 1e3)


def _fs(ap):
    n = 1
    for d in ap.shape[1:]:
        n *= d
    return n


class KB:
    def __init__(self, nc):
        self.nc = nc
        self.S = Sched(nc)
        self.uid = 0
        self.stack = None
        self.marks = []

    def mark(self, name):
        self.marks.append((name, len(self.S.ops)))

    def sb(self, name, shape, dt=F32):
        if self.stack is None:
            return self.nc.alloc_sbuf_tensor(name, list(shape), dt)
        self.uid += 1
        return self.stack.enter_context(self.nc.sbuf_tensor(f"{name}_u{self.uid}", list(shape), dt))

    def phase_begin(self):
        import contextlib
        assert self.stack is None
        self.stack = contextlib.ExitStack()

    def phase_end(self):
        self.S.barrier()
        self.stack.close()
        self.stack = None

    def ps(self, name, shape, dt=F32):
        return self.nc.alloc_psum_tensor(name, list(shape), dt)

    def mm(self, out, lhsT, rhs, start=True, stop=True, rk=None, wk=None):
        nc = self.nc
        cost = max(64, _fs(rhs)) / 2.4 * (4 if lhsT.dtype == F32 else 1) + 45
        self.S.op("pe", lambda: nc.tensor.matmul(out, lhsT=lhsT, rhs=rhs, start=start, stop=stop),
                  reads=rk if rk is not None else [lhsT, rhs], writes=wk if wk is not None else [out], cost=cost)

    def tr(self, out, in_, ident, rk=None, wk=None):
        nc = self.nc
        self.S.op("pe", lambda: nc.tensor.transpose(out, in_, ident),
                  reads=rk if rk is not None else [in_, ident], writes=wk if wk is not None else [out], cost=110)

    def act(self, out, in_, func, bias=None, scale=None, accum=None, rk=None, wk=None):
        nc = self.nc
        kw = {}
        reads = [in_]
        if bias is not None:
            kw["bias"] = bias
            if not isinstance(bias, (int, float)):
                reads.append(bias)
        if scale is not None:
            kw["scale"] = scale
            if not isinstance(scale, (int, float)):
                reads.append(scale)
        writes = [out]
        if accum is not None:
            kw["accum_out"] = accum
            writes.append(accum)
        self.S.op("act", lambda: nc.scalar.activation(out=out, in_=in_, func=func, **kw),
                  reads=rk if rk is not None else reads, writes=wk if wk is not None else writes,
                  cost=(224 + _fs(out)) / 1.2)

    def tt(self, out, in0, in1, op, eng="dve", rk=None, wk=None):
        E = self.S.eng[eng]
        cost = (170 + _fs(out)) / 0.96 if eng == "dve" else (350 + 2.0 * _fs(out)) / 1.2
        self.S.op(eng, lambda: E.tensor_tensor(out=out, in0=in0, in1=in1, op=op),
                  reads=rk if rk is not None else [in0, in1], writes=wk if wk is not None else [out], cost=cost)

    def ts(self, out, in0, s1, op0, s2=None, op1=None, eng="dve", rk=None, wk=None):
        E = self.S.eng[eng]
        reads = [in0] + [s for s in (s1, s2) if s is not None and not isinstance(s, (int, float))]
        if op1 is None:
            fn = lambda: E.tensor_scalar(out=out, in0=in0, scalar1=s1, scalar2=None, op0=op0)
        else:
            fn = lambda: E.tensor_scalar(out=out, in0=in0, scalar1=s1, scalar2=s2, op0=op0, op1=op1)
        cost = (170 + 0.7 * _fs(out)) / 0.96 if eng == "dve" else (350 + 2.0 * _fs(out)) / 1.2
        self.S.op(eng, fn, reads=rk if rk is not None else reads, writes=wk if wk is not None else [out], cost=cost)

    def stt(self, out, in0, scalar, in1, op0, op1, rk=None, wk=None):
        nc = self.nc
        reads = [in0, in1] + ([] if isinstance(scalar, (int, float)) else [scalar])
        self.S.op("dve", lambda: nc.vector.scalar_tensor_tensor(out=out, in0=in0, scalar=scalar, in1=in1, op0=op0, op1=op1),
                  reads=rk if rk is not None else reads, writes=wk if wk is not None else [out],
                  cost=(170 + _fs(out)) / 0.96)

    def cp(self, out, in_, eng="dve", rk=None, wk=None):
        nc = self.nc
        if eng == "act":
            fn = lambda: nc.scalar.activation(out=out, in_=in_, func=AF.Copy)
        else:
            E = self.S.eng[eng]
            fn = lambda: E.tensor_copy(out=out, in_=in_)
        if eng == "act":
            cost = (224 + _fs(out)) / 1.2
        elif eng == "dve":
            cost = (170 + 0.7 * _fs(out)) / 0.96
        else:
            cost = (350 + 2.0 * _fs(out)) / 1.2
        self.S.op(eng, fn, reads=rk if rk is not None else [in_], writes=wk if wk is not None else [out], cost=cost)

    def recip(self, out, in_, rk=None, wk=None):
        nc = self.nc
        self.S.op("dve", lambda: nc.vector.reciprocal(out=out, in_=in_),
                  reads=rk if rk is not None else [in_], writes=wk if wk is not None else [out],
                  cost=(62 + 8 * _fs(out)) / 0.96)

    def memset(self, ap, val, eng="dve"):
        E = self.S.eng[eng]
        self.S.op(eng, lambda: E.memset(ap, val), writes=[ap], cost=(62 + _fs(ap)) / 0.96)

    def dma(self, out, in_, eng="sp", rk=(), wk=()):
        E = self.S.eng[eng]
        nbytes = out.shape[0] * _fs(out) * 4
        self.S.op(eng, lambda: E.dma_start(out=out, in_=in_), reads=list(rk), writes=list(wk), dma=True,
                  cost=2000 + nbytes / 120.0)


def build_program(debug=False, nlayers=DEPTH, parts="abcd", limit=None):
    nc = bass.Bass("TRN2", target_bir_lowering=False)
    K = KB(nc)

    def din(name, shape):
        return nc.dram_tensor(name, list(shape), F32, kind="ExternalInput").ap()

    x_d = din("x", [T, D])
    w_in_d = din("w_in", [DEPTH, D, IN_COLS])
    w_out_d = din("w_out", [DEPTH, D, D])
    w_up_d = din("w_up", [DEPTH, D, 4 * D])
    w_down_d = din("w_down", [DEPTH, 4 * D, D])
    ln1w_d = din("ln1_w", [DEPTH, D]); ln1b_d = din("ln1_b", [DEPTH, D])
    ln2w_d = din("ln2_w", [DEPTH, D]); ln2b_d = din("ln2_b", [DEPTH, D])
    ident_d = din("c_ident", [128, 128])
    amask_d = din("c_amask", [128, 5 * 4 * 128])
    triT_d = din("c_triT", [128, 128]); maskneg_d = din("c_maskneg", [128, 128])
    ssdconv_d = din("c_ssdconv", [DEPTH, 128, 30]); ssdD_d = din("c_ssdD", [DEPTH, 256])
    msl_d = din("c_msl", [128, 128]); msu_d = din("c_msu", [128, 128])
    shift_d = din("c_shift", [128, 128]); carry_d = din("c_carry", [128, 128])
    rmucol_d = din("c_rmucol", [DEPTH, 128, 1]); vmucol_d = din("c_vmucol", [128, 1])
    mush_d = din("mu_shift", [DEPTH, 896])
    rw0_d = din("rwkv_w0", [DEPTH, 256]); ra0_d = din("rwkv_a0", [DEPTH, 256]); rkk_d = din("rwkv_k_k", [DEPTH, 256])
    rka_d = din("rwkv_k_a", [DEPTH, 256]); rlw_d = din("rwkv_lnx_w", [DEPTH, 256]); rlb_d = din("rwkv_lnx_b", [DEPTH, 256])
    rrk_d = din("c_rrk", [DEPTH, 256]); rw2_d = din("rwkv_w2", [DEPTH, 32, 256]); ra2_d = din("rwkv_a2", [DEPTH, 32, 256])
    rg2_d = din("rwkv_g2", [DEPTH, 64, 256]); rv0_d = din("rwkv_v0", [1, 256]); rv2_d = din("rwkv_v2", [1, 32, 256])
    wvres_d = din("w_in_vres", [1, D, 32])
    vfirst_d = nc.dram_tensor("vfirst", [T, 256], F32, kind="Internal").ap()
    lscr_d = nc.dram_tensor("lscr", [128, T], BF16, kind="Internal").ap()
    vrscr_d = nc.dram_tensor("vrscr", [32, T], BF16, kind="Internal").ap()
    tri16_d = din("c_tri16", [128, 128]); blk16_d = din("c_blk16", [128, 128]); bm_d = din("c_bm", [128, 8])
    bones64_d = din("c_bones64", [128, 128]); hgnw_d = din("c_hgnw", [DEPTH, 128, 2]); lowb_d = din("lower_bounds", [DEPTH, 256])
    dtb_d = din("ssd_dt_bias", [DEPTH, 4]); alog_d = din("ssd_A_log", [DEPTH, 4]); ssdnw_d = din("ssd_norm_w", [DEPTH, 256])
    out_d = nc.dram_tensor("out", [T, D], F32, kind="ExternalOutput").ap()
    vscr_d = nc.dram_tensor("vscr", [T, 256], BF16, kind="Internal").ap()
    dbg_d = None
    if debug:
        dbg_d = nc.dram_tensor("dbg", [4, 128, 2 * T], BF16, kind="ExternalOutput").ap()

    xres = [K.sb(f"xres{t}", [128, D]) for t in range(NT)]
    xT = K.sb("xT", [128, 8, T], BF16)
    ident = K.sb("ident", [128, 128])
    ones_bf = K.sb("ones_bf", [128, 64], BF16)
    wbuf = [K.sb(f"wbuf{i}", [128, 8192], BF16) for i in range(2)]
    wstate = {"i": 0}
    psb = [K.ps(f"ps_{i}", [128, 512]) for i in range(7)]
    pbf = K.ps("ps_bf", [128, 1024], BF16)
    ident_bf = K.sb("ident_bf", [128, 128], BF16)
    triT = K.sb("triT", [128, 128]); ones_f = K.sb("ones_f", [128, 128]); maskneg = K.sb("maskneg", [128, 128])

    def xk(t0, t1):
        return [("xT", b) for b in range(t0 // 512, (t1 - 1) // 512 + 1)]
    XALL = [("xT", b) for b in range(4)]

    pre = {}

    def prefetch(tag, src_ap, kc, ncols):
        pre[tag] = wload(src_ap, kc, ncols)

    def getw(tag, src_ap, kc, ncols):
        if tag in pre:
            return pre.pop(tag)
        return wload(src_ap, kc, ncols)

    def wsmall(name, src_ap, kc, ncols):
        t_ = K.sb(name, [128, kc * ncols], BF16)
        view = t_[:, :].rearrange("p (k c) -> p k c", k=kc)
        K.dma(view, src_ap.rearrange("(k p) c -> p k c", p=128), eng="pool", rk=[t_], wk=[t_])
        return t_, view

    def wload(src_ap, kc, ncols, off=0, new=True):
        if new:
            wstate["i"] += 1
        wb = wbuf[wstate["i"] % 2]
        view = wb[:, off:off + kc * ncols].rearrange("p (k c) -> p k c", k=kc)
        K.dma(view, src_ap.rearrange("(k p) c -> p k c", p=128), eng="pool", rk=[wb], wk=[wb])
        return wb, view

    K.dma(ident[:], ident_d, wk=[ident])
    K.cp(ident_bf[:], ident[:])
    K.dma(triT[:], triT_d, wk=[triT])
    K.dma(maskneg[:], maskneg_d, wk=[maskneg])
    K.memset(ones_f[:], 1.0)
    K.memset(ones_bf[:], 1.0)
    mhalf = K.sb("mhalf", [128, 4])
    K.memset(mhalf[:], -0.5)

    for t in range(NT):
        K.dma(xres[t][:], x_d[t * 128:(t + 1) * 128, :], wk=[xres[t]])

    def build_xT(t):
        for half in range(2):
            pb = psb[5 + half]
            for j in range(4):
                kc = half * 4 + j
                K.tr(pb[:, j * 128:(j + 1) * 128], xres[t][:, kc * 128:(kc + 1) * 128], ident[:])
            K.cp(xT[:, half * 4:(half + 1) * 4, t * 128:(t + 1) * 128],
                 pb[:].rearrange("p (j c) -> p j c", j=4), eng=("act" if half else "dve"),
                 wk=xk(t * 128, (t + 1) * 128))

    for t in range(NT):
        build_xT(t)

    def layer_norm(t, w_t, b_t):
        xt = xres[t]
        stats = K.sb(f"lnst{K.uid}", [128, 12]); mv = K.sb(f"lnmv{K.uid}", [128, 2]); rs = K.sb(f"lnrs{K.uid}", [128, 1])
        K.uid += 1
        nc_ = nc
        K.S.op("dve", lambda: nc_.vector.bn_stats(out=stats[:, 0:6], in_=xt[:, 0:512]), reads=[xt], writes=[stats])
        K.S.op("dve", lambda: nc_.vector.bn_stats(out=stats[:, 6:12], in_=xt[:, 512:1024]), reads=[xt], writes=[stats])
        K.S.op("dve", lambda: nc_.vector.bn_aggr(out=mv[:], in_=stats[:]), reads=[stats], writes=[mv])
        K.ts(rs[:], mv[:, 1:2], LN_EPS, ALU.add)
        K.tt(rs[:], rs[:], mhalf[:, 0:1], ALU.pow, eng="pool")
        K.stt(mv[:, 1:2], mv[:, 0:1], -1.0, rs[:, 0:1], ALU.mult, ALU.mult)
        K.act(xt[:], xt[:], AF.Identity, bias=mv[:, 1:2], scale=rs[:, 0:1])
        K.tt(xt[:], xt[:], w_t[:], ALU.mult)
        K.tt(xt[:], xt[:], b_t[:], ALU.add, eng="pool")

    yT = K.sb("yT", [128, 2, T], BF16)
    wo_t = K.sb("wo_t", [128, 2 * D], BF16)

    def out_proj(l, m):
        wb = wo_t
        wv = wo_t[:, :].rearrange("p (k c) -> p k c", k=2)
        K.dma(wv, w_out_d[l, m * 256:(m + 1) * 256, :].rearrange("(k p) c -> p k c", p=128), eng="pool", rk=[wo_t], wk=[wo_t])
        for t in range(NT):
            for nb in range(2):
                pb = psb[4 + (t * 2 + nb) % 2]
                for c in range(2):
                    K.mm(pb[:, :], lhsT=yT[:, c, t * 128:(t + 1) * 128], rhs=wv[:, c, nb * 512:(nb + 1) * 512],
                         start=(c == 0), stop=(c == 1), rk=[yT, wb])
                K.tt(xres[t][:, nb * 512:(nb + 1) * 512], pb[:, :], xres[t][:, nb * 512:(nb + 1) * 512], ALU.add)

    def ffn(l):
        hT = [K.sb(f"hT{i}", [128, 4, T], BF16) for i in range(2)]
        rtmp = [K.sb(f"rtmp{i}", [128, 512]) for i in range(2)]
        for t in range(NT):
            K.act(xres[t][:], xres[t][:], AF.Copy, scale=ALPHA)
        for j in range(8):
            if j == 0 and ("f0", l) in pre:
                (wub, wuv), (wdb, wdv) = pre.pop(("f0", l))
            else:
                wub, wuv = wload(w_up_d[l, :, j * 512:(j + 1) * 512], 8, 512)
                wdb, wdv = wload(w_down_d[l, j * 512:(j + 1) * 512, :], 4, D, off=4096, new=False)
            h = hT[j % 2]
            cnt = 0
            for m in range(4):
                for nb in range(4):
                    pb = psb[cnt % 2]
                    rt = rtmp[cnt % 2]
                    cnt += 1
                    for kc in range(8):
                        K.mm(pb[:, :], lhsT=wuv[:, kc, m * 128:(m + 1) * 128], rhs=xT[:, kc, nb * 512:(nb + 1) * 512],
                             start=(kc == 0), stop=(kc == 7), rk=[wub, ("xT", nb)])
                    K.act(rt[:], pb[:, :], AF.Relu)
                    K.tt(h[:, m, nb * 512:(nb + 1) * 512], rt[:], rt[:], ALU.mult, eng="pool", wk=[(h.name, nb)])
            for t in range(NT):
                for nb in range(2):
                    pb = psb[2 + (t * 2 + nb) % 2]
                    for c in range(4):
                        K.mm(pb[:, :], lhsT=h[:, c, t * 128:(t + 1) * 128], rhs=wdv[:, c, nb * 512:(nb + 1) * 512],
                             start=(c == 0), stop=(c == 3), rk=[(h.name, t // 4), wdb])
                    K.tt(xres[t][:, nb * 512:(nb + 1) * 512], pb[:, :], xres[t][:, nb * 512:(nb + 1) * 512], ALU.add)

    def attention(l):
        K.mark("attn_begin")
        K.phase_begin()
        qT = K.sb("qT", [128, 2, T], BF16); kT = K.sb("kT", [128, 2, T], BF16)
        amask = K.sb("amask", [128, 2, 512])
        accn = K.sb("accn", [128, 2, T]); accd = K.sb("accd", [128, 2, T], BF16)
        Vb = K.sb("Vb", [128, 16, 256], BF16)
        Vn = [K.sb(f"Vn{i}", [128, 256], BF16) for i in range(2)]
        Ebuf = [K.sb(f"Ebuf{i}", [128, 512]) for i in range(2)]
        Pbuf = [K.sb(f"Pbuf{i}", [128, 512], BF16) for i in range(4)]
        amask_v = amask_d.rearrange("p (m f) -> p m f", m=5)
        wb, wv = getw(("a", l), w_in_d[l, :, C_ATT:C_ATT + 768], 8, 768)
        if "c" in parts:
            prefetch(("s", l), w_in_d[l, :, C_SSD:C_SSD + 1024], 8, 1024)
        cnt = 0
        for which, dst, scale in ((0, qT, 0.125), (1, kT, 1.0)):
            for c in range(2):
                for nb in range(4):
                    pb = psb[cnt % 2]; cnt += 1
                    for kc in range(8):
                        K.mm(pb[:, :], lhsT=wv[:, kc, which * 256 + c * 128: which * 256 + (c + 1) * 128],
                             rhs=xT[:, kc, nb * 512:(nb + 1) * 512], start=(kc == 0), stop=(kc == 7),
                             rk=[wb, ("xT", nb)])
                    K.act(dst[:, c, nb * 512:(nb + 1) * 512], pb[:, :], AF.Copy, scale=scale)
        K.mark("attn_V")
        for t in range(NT):
            pb = psb[t % 2]
            for kc in range(8):
                K.mm(pb[:, 0:256], lhsT=xT[:, kc, t * 128:(t + 1) * 128], rhs=wv[:, kc, 512:768],
                     start=(kc == 0), stop=(kc == 7), rk=[wb, ("xT", t // 4)])
            K.cp(Vn[t % 2][:], pb[:, 0:256], eng="act")
            K.dma(vscr_d[t * 128:(t + 1) * 128, :], Vn[t % 2][:], rk=[Vn[t % 2]], wk=["vscr"])
        ecnt = 0; qcnt = 0
        opbanks = [psb[6], pbf[:].bitcast(F32), psb[0], psb[1]]
        for bi, d in enumerate((1, 4, 16)):
            K.mark(f"attn_branch{bi}")
            nblk = T // d // 128
            if bi < 2:
                K.dma(amask[:], amask_v[:, 2 * bi:2 * bi + 2, :], rk=[amask], wk=[amask])
            else:
                K.dma(amask[:, 1, :], amask_v[:, 4, :], rk=[amask], wk=[amask])

            def tsl(r, n, d=d):
                st = r + d * 128 * n
                return slice(st, st + d * 127 + 1, d)
            if d == 1:
                for q in range(4):
                    K.dma(Vb[:, q * 4:(q + 1) * 4, :], vscr_d.rearrange("(n j) f -> j n f", j=128)[:, q * 4:(q + 1) * 4, :],
                          rk=["vscr", Vb], wk=[Vb])
            elif d == 4:
                for r in range(4):
                    K.dma(Vb[:, r * 4:(r + 1) * 4, :], vscr_d.rearrange("(n j r) f -> r j n f", n=4, j=128, r=4)[r],
                          rk=["vscr", Vb], wk=[Vb])
            else:
                for q in range(4):
                    K.dma(Vb[:, q * 4:(q + 1) * 4, :], vscr_d.rearrange("(j r) f -> j r f", r=16)[:, q * 4:(q + 1) * 4, :],
                          rk=["vscr", Vb], wk=[Vb])
            for r in range(d):
                for n in range(nblk):
                    qsl = tsl(r, n)
                    kbl = []
                    if n > 0:
                        kbl.append((r * nblk + n - 1, tsl(r, n - 1), 0))
                    kbl.append((r * nblk + n, qsl, 1))
                    Ps = []
                    for ki, (vidx, ksl, mid) in enumerate(kbl):
                        spa = psb[2 + 2 * ki]; spb = psb[3 + 2 * ki]
                        for h in range(4):
                            po = (h % 2) * 64
                            sp = spa if h % 2 == 0 else spb
                            K.mm(sp[:, (h // 2) * 128:(h // 2 + 1) * 128], lhsT=kT[po:po + 64, h // 2, ksl],
                                 rhs=qT[po:po + 64, h // 2, qsl])
                        E = Ebuf[ecnt % 2]; P = Pbuf[ecnt % 4]; ecnt += 1
                        K.tt(E[:, 0:256], spa[:, 0:256], amask[:, mid, 0:256], ALU.add)
                        K.tt(E[:, 256:512], spb[:, 0:256], amask[:, mid, 256:512], ALU.add)
                        K.act(P[:], E[:], AF.Exp)
                        Ps.append((vidx, P))
                    op = opbanks[qcnt % 4]; qcnt += 1
                    for h in range(4):
                        po = (h % 2) * 64; hh = h // 2
                        pc = (h % 2) * 256 + hh * 128
                        for ki, (vidx, P) in enumerate(Ps):
                            K.mm(op[po:po + 64, hh * 128:(hh + 1) * 128], lhsT=Vb[:, vidx, h * 64:(h + 1) * 64],
                                 rhs=P[:, pc:pc + 128], start=(ki == 0), stop=(ki == len(Ps) - 1))
                        for ki, (vidx, P) in enumerate(Ps):
                            K.mm(op[po:po + 64, (2 + hh) * 128:(3 + hh) * 128], lhsT=ones_bf[:, 0:64],
                                 rhs=P[:, pc:pc + 128], start=(ki == 0), stop=(ki == len(Ps) - 1))
                    nv = op[:, 0:256].rearrange("p (a q) -> p a q", a=2)
                    dv = op[:, 256:512].rearrange("p (a q) -> p a q", a=2)
                    if bi == 0:
                        K.cp(accn[:, :, qsl], nv, eng="act")
                        K.cp(accd[:, :, qsl], dv, eng="dve")
                    else:
                        K.tt(accn[:, :, qsl], nv, accn[:, :, qsl], ALU.add)
                        K.tt(accd[:, :, qsl], dv, accd[:, :, qsl], ALU.add)
        K.mark("attn_final")
        chunks = [(hh, q4) for hh in range(2) for q4 in range(4)]
        for p0 in range(0, 8, 2):
            pair = chunks[p0:p0 + 2]
            for i_, (hh, q4) in enumerate(pair):
                K.act(Ebuf[i_][:], accd[:, hh, q4 * 512:(q4 + 1) * 512], AF.Ln)
            for i_, (hh, q4) in enumerate(pair):
                K.act(Ebuf[i_][:], Ebuf[i_][:], AF.Exp, scale=-1.0)
            for i_, (hh, q4) in enumerate(pair):
                sl_ = slice(q4 * 512, (q4 + 1) * 512)
                K.tt(yT[:, hh, sl_], accn[:, hh, sl_], Ebuf[i_][:], ALU.mult)
        K.phase_end()


    def ssd(l):
        K.phase_begin()
        wb, wv = getw(("s", l), w_in_d[l, :, C_SSD:C_SSD + 1024], 8, 1024)
        wb2, wv2 = wsmall("wdt", w_in_d[l, :, C_SSD + 1024:C_SSD + 1028], 8, 4)
        if "d" in parts:
            prefetch(("h", l), w_in_d[l, :, C_HGRN:C_HGRN + 1024], 8, 1024)
        cpar = K.sb("cpar", [128, 30]); dtb = K.sb("dtb", [128, 4]); aneg = K.sb("aneg", [128, 4])
        Dfull = K.sb("Dfull", [128, 256]); normw = K.sb("normw", [128, 256])
        K.dma(cpar[:], ssdconv_d[l], wk=[cpar])
        K.dma(dtb[:], dtb_d[l:l + 1, :].partition_broadcast(128), wk=[dtb])
        K.dma(aneg[:], alog_d[l:l + 1, :].partition_broadcast(128), wk=[aneg])
        K.dma(Dfull[:], ssdD_d[l:l + 1, :].partition_broadcast(128), wk=[Dfull])
        K.dma(normw[:], ssdnw_d[l:l + 1, :].partition_broadcast(128), wk=[normw])
        K.act(aneg[:], aneg[:], AF.Exp)
        K.ts(aneg[:], aneg[:], -1.0, ALU.mult)
        xpads = [K.sb(f"xpad{i}", [128, T + 3], BF16) for i in range(2)]
        cdiag = K.sb("cdiag", [128, 24, 128], BF16)
        for cj in range(24):
            c_, j_ = divmod(cj, 4)
            K.act(cdiag[:, cj, :], ident[:], AF.Copy, scale=cpar[:, c_ * 5 + j_:c_ * 5 + j_ + 1])
        xsT = K.sb("xsT", [128, 2, T], BF16); BT = K.sb("BT", [128, 2, T], BF16); CT = K.sb("CT", [128, 2, T], BF16)
        for xp_ in xpads:
            K.memset(xp_[:, 0:3], 0.0)
        for c in range(6):
            xpad = xpads[c % 2]
            for nb in range(4):
                pb = psb[nb % 2]
                for kc in range(8):
                    K.mm(pb[:, :], lhsT=wv[:, kc, 256 + c * 128:256 + (c + 1) * 128], rhs=xT[:, kc, nb * 512:(nb + 1) * 512],
                         start=(kc == 0), stop=(kc == 7), rk=[wb, ("xT", nb)])
                K.act(xpad[:, 3 + nb * 512:3 + (nb + 1) * 512], pb[:, :], AF.Copy)
            dst = (xsT, BT, CT)[c // 2]
            for nb in range(4):
                pc = psb[2 + nb % 2]
                for j in range(4):
                    K.mm(pc[:, :], lhsT=cdiag[:, c * 4 + j, :], rhs=xpad[:, nb * 512 + j:nb * 512 + j + 512],
                         start=(j == 0), stop=(j == 3))
                K.act(dst[:, c % 2, nb * 512:(nb + 1) * 512], pc[:, :], AF.Silu, bias=cpar[:, c * 5 + 4:c * 5 + 5])
        dt = K.sb("dt", [128, 64]); dA = K.sb("dA", [128, 64]); cs = K.sb("cs", [128, 64]); ncs = K.sb("ncs", [128, 64])
        expcs = K.sb("expcs", [128, 64]); dst_ = K.sb("dsts", [128, 64]); cdec = K.sb("cdec", [128, 64])
        for t in range(NT):
            pb = psb[t % 2]
            for kc in range(8):
                K.mm(pb[:, 0:4], lhsT=xT[:, kc, t * 128:(t + 1) * 128], rhs=wv2[:, kc, 0:4],
                     start=(kc == 0), stop=(kc == 7), rk=[wb2, ("xT", t // 4)])
            K.tt(dt[:, t * 4:(t + 1) * 4], pb[:, 0:4], dtb[:], ALU.add)
        K.act(dt[:], dt[:], AF.Exp)
        K.act(dt[:], dt[:], AF.Ln, bias=1.0)
        K.tt(dA[:].rearrange("p (c h) -> p c h", h=4), dt[:].rearrange("p (c h) -> p c h", h=4),
             aneg[:].unsqueeze(1).to_broadcast([128, 16, 4]), ALU.mult)
        K.mm(psb[2][:, 0:64], lhsT=triT[:], rhs=dA[:])
        K.mm(psb[2][:, 64:128], lhsT=ones_f[:], rhs=dA[:])
        K.cp(cs[:], psb[2][:, 0:64])
        K.ts(ncs[:], cs[:], -1.0, ALU.mult)
        K.act(expcs[:], cs[:], AF.Exp)
        K.tt(dst_[:], psb[2][:, 64:128], cs[:], ALU.subtract)
        K.act(dst_[:], dst_[:], AF.Exp)
        K.act(cdec[:], psb[2][:, 64:128], AF.Exp)
        state = K.sb("sstate", [128, 256]); K.memset(state[:], 0.0)
        prevb = [K.sb(f"prevb{i}", [128, 256], BF16) for i in range(2)]
        Rb = [K.sb(f"Rb{i}", [128, 128]) for i in range(2)]
        decT = [K.sb(f"decT{i}", [128, 128]) for i in range(2)]
        MT = [K.sb(f"MT{i}", [128, 4, 128], BF16) for i in range(2)]
        xs_tm = [K.sb(f"xstm{i}", [128, 256]) for i in range(2)]
        xdt = [K.sb(f"xdt{i}", [128, 256], BF16) for i in range(2)]
        xdt2 = [K.sb(f"xdt2{i}", [128, 256], BF16) for i in range(2)]
        B_tm = [K.sb(f"Btm{i}", [128, 256], BF16) for i in range(2)]
        zs = [K.sb(f"zs{i}", [128, 256]) for i in range(2)]
        yy = [K.sb(f"yy{i}", [128, 256]) for i in range(2)]
        y2 = [K.sb(f"y2{i}", [128, 256]) for i in range(2)]
        ss = [K.sb(f"ssq{i}", [128, 2]) for i in range(2)]
        rc = 0
        for c in range(NT):
            b = c % 2
            tk = slice(c * 128, (c + 1) * 128)
            for g in range(2):
                K.mm(psb[3][:, g * 128:(g + 1) * 128], lhsT=BT[:, g, tk], rhs=CT[:, g, tk])
            for h in range(4):
                col = c * 4 + h
                R = Rb[rc % 2]; dT = decT[rc % 2]; rc += 1
                K.act(R[:], triT[:], AF.Copy, scale=dA[:, col:col + 1])
                K.mm(psb[4][:, 0:128], lhsT=ones_f[:], rhs=R[:], start=True, stop=False)
                K.mm(psb[4][:, 0:128], lhsT=ident[:], rhs=maskneg[:], start=False, stop=True)
                K.act(dT[:], psb[4][:, 0:128], AF.Exp, bias=ncs[:, col:col + 1])
                K.tt(MT[b][:, h, :], psb[3][:, (h // 2) * 128:(h // 2 + 1) * 128], dT[:], ALU.mult)
            for i, (src, cc) in enumerate(((xsT, 0), (xsT, 1), (BT, 0), (BT, 1))):
                K.tr(pbf[:, i * 128:(i + 1) * 128], src[:, cc, tk], ident_bf[:])
            K.cp(xs_tm[b][:], pbf[:, 0:256], eng="act")
            K.cp(B_tm[b][:], pbf[:, 256:512], eng="act")
            K.tt(xdt[b][:].rearrange("p (h e) -> p h e", h=4), xs_tm[b][:].rearrange("p (h e) -> p h e", h=4),
                 dt[:, c * 4:(c + 1) * 4].unsqueeze(2).to_broadcast([128, 4, 64]), ALU.mult)
            K.tt(xdt2[b][:].rearrange("p (h e) -> p h e", h=4), xdt[b][:].rearrange("p (h e) -> p h e", h=4),
                 dst_[:, c * 4:(c + 1) * 4].unsqueeze(2).to_broadcast([128, 4, 64]), ALU.mult)
            K.cp(prevb[b][:], state[:], eng="act")
            for h in range(4):
                K.mm(psb[5][:, h * 64:(h + 1) * 64], lhsT=MT[b][:, h, :], rhs=xdt[b][:, h * 64:(h + 1) * 64])
            for h in range(4):
                K.mm(psb[5][:, 256 + h * 64:256 + (h + 1) * 64], lhsT=CT[:, h // 2, tk], rhs=prevb[b][:, h * 64:(h + 1) * 64])
            for h in range(4):
                K.mm(psb[6][:, h * 64:(h + 1) * 64], lhsT=B_tm[b][:, (h // 2) * 128:(h // 2 + 1) * 128],
                     rhs=xdt2[b][:, h * 64:(h + 1) * 64])
            K.tt(state[:].rearrange("p (h e) -> p h e", h=4), state[:].rearrange("p (h e) -> p h e", h=4),
                 cdec[:, c * 4:(c + 1) * 4].unsqueeze(2).to_broadcast([128, 4, 64]), ALU.mult)
            K.tt(state[:], psb[6][:, 0:256], state[:], ALU.add)
            y = yy[b]
            K.tt(y[:].rearrange("p (h e) -> p h e", h=4), psb[5][:, 256:512].rearrange("p (h e) -> p h e", h=4),
                 expcs[:, c * 4:(c + 1) * 4].unsqueeze(2).to_broadcast([128, 4, 64]), ALU.mult)
            K.tt(y[:], psb[5][:, 0:256], y[:], ALU.add)
            K.tt(y2[b][:], xs_tm[b][:], Dfull[:], ALU.mult, eng="pool")
            K.tt(y[:], y[:], y2[b][:], ALU.add)
            pb = psb[c % 2]
            for kc in range(8):
                K.mm(pb[:, 0:256], lhsT=xT[:, kc, tk], rhs=wv[:, kc, 0:256], start=(kc == 0), stop=(kc == 7),
                     rk=[wb, ("xT", c // 4)])
            K.act(zs[b][:], pb[:, 0:256], AF.Tanh, scale=0.5)
            K.stt(zs[b][:], zs[b][:], 1.0, pb[:, 0:256], ALU.add, ALU.mult)
            K.stt(y[:], y[:], 0.5, zs[b][:], ALU.mult, ALU.mult)
            K.act(y2[b][:], y[:], AF.Square)
            nc_ = nc
            sq = y2[b]; sso = ss[b]
            K.S.op("dve", lambda sq=sq, sso=sso: nc_.vector.tensor_reduce(out=sso[:], in_=sq[:].rearrange("p (g e) -> p g e", g=2), axis=AX.X, op=ALU.add),
                   reads=[sq], writes=[sso])
            K.ts(sso[:], sso[:], 1.0 / 128, ALU.mult, RMS_EPS, ALU.add)
            K.tt(sso[:], sso[:], mhalf[:, 0:2], ALU.pow, eng="pool")
            for g in range(2):
                K.ts(y[:, g * 128:(g + 1) * 128], y[:, g * 128:(g + 1) * 128], sso[:, g:g + 1], ALU.mult)
            K.tt(y[:], y[:], normw[:], ALU.mult)
            for g in range(2):
                K.tr(psb[2][:, g * 128:(g + 1) * 128], y[:, g * 128:(g + 1) * 128], ident[:])
            K.cp(yT[:, :, tk], psb[2][:, 0:256].rearrange("p (g e) -> p g e", g=2))
        K.phase_end()


    def hgrn(l):
        K.phase_begin()
        wb, wv = getw(("h", l), w_in_d[l, :, C_HGRN:C_HGRN + 1024], 8, 1024)
        pre[("f0", l)] = (wload(w_up_d[l, :, 0:512], 8, 512), wload(w_down_d[l, 0:512, :], 4, D, off=4096, new=False))
        tri16 = K.sb("tri16", [128, 128]); blk16 = K.sb("blk16", [128, 128]); bm = K.sb("bm", [128, 8])
        bones = K.sb("bones", [128, 128]); hgnw = K.sb("hgnw", [128, 2])
        lbb = K.sb("lbb", [128, 256]); omlb = K.sb("omlb", [128, 256])
        K.dma(tri16[:], tri16_d, wk=[tri16]); K.dma(blk16[:], blk16_d, wk=[blk16]); K.dma(bm[:], bm_d, wk=[bm])
        dif16 = K.sb("dif16", [128, 128])
        K.tt(dif16[:], blk16[:], tri16[:], ALU.subtract)
        K.dma(bones[:], bones64_d, wk=[bones]); K.dma(hgnw[:], hgnw_d[l], wk=[hgnw])
        if l == 0:
            K.memset(lbb[:], 0.0)
        else:
            K.dma(lbb[:], lowb_d[1:2, :].partition_broadcast(128), wk=[lbb])
            K.dma(omlb[:], lowb_d[0:1, :].partition_broadcast(128), wk=[omlb])
            K.tt(lbb[:], lbb[:], omlb[:], ALU.subtract)
            K.act(lbb[:], lbb[:], AF.Sigmoid)
        K.ts(omlb[:], lbb[:], -0.5, ALU.mult, 0.5, ALU.add)
        K.tt(lbb[:], lbb[:], omlb[:], ALU.add)
        K.ts(hgnw[:], hgnw[:], 0.5, ALU.mult)
        epsc = K.sb("epsc", [128, 1]); K.memset(epsc[:], RMS_EPS)
        Sst = [K.sb(f"Sst{i}", [128, 9, 64]) for i in range(2)]; SbBD = K.sb("SbBD", [128, 8, 2, 128], BF16)
        for i in range(2):
            K.memset(Sst[i][:, 0, :], 0.0)
        K.memset(SbBD[:], 0.0, eng="pool")
        A = lambda nm, shp, dt_=F32: [K.sb(f"{nm}{i}", shp, dt_) for i in range(2)]
        sg = A("hsg", [128, 256]); fg = A("hfg", [128, 256]); logf = A("hlogf", [128, 256]); kk = A("hkk", [128, 256])
        qs = A("hqs", [128, 256]); V = A("hV", [128, 256], BF16); bb = A("hb", [128, 256]); eb = A("heb", [128, 256])
        enb = A("henb", [128, 256]); ed = A("hed", [128, 256]); tot = A("htot", [128, 256])
        qbar = A("hqbar", [128, 256]); kbar = A("hkbar", [128, 256]); kdec = A("hkdec", [128, 256], BF16)
        qkT = A("hqkT", [128, 4, 128], BF16); totT = A("htotT", [128, 2, 128]); attT = A("hattT", [128, 4, 128], BF16)
        kdb = A("hkdb", [128, 8, 256], BF16); sq = A("hsq", [128, 256]); rstd = A("hrstd", [128, 256]); oo = A("hoo", [128, 256])
        for t in range(NT):
            b = t % 2
            tk = slice(t * 128, (t + 1) * 128)
            if t > 0:
                for i in range(2):
                    K.cp(Sst[i][:, 0, :], Sst[i][:, 8, :], eng=("dve" if i == 0 else "pool"))
            for kc in range(8):
                K.mm(psb[0][:, :], lhsT=xT[:, kc, tk], rhs=wv[:, kc, 0:512], start=(kc == 0), stop=(kc == 7),
                     rk=[wb, ("xT", t // 4)])
            for kc in range(8):
                K.mm(psb[1][:, 0:256], lhsT=xT[:, kc, tk], rhs=wv[:, kc, 512:768], start=(kc == 0), stop=(kc == 7),
                     rk=[wb, ("xT", t // 4)])
            for c2 in range(2):
                for kc in range(8):
                    K.mm(psb[1][:, 256 + c2 * 128:256 + (c2 + 1) * 128], lhsT=wv[:, kc, 768 + c2 * 128:768 + (c2 + 1) * 128],
                         rhs=xT[:, kc, tk], start=(kc == 0), stop=(kc == 7), rk=[wb, ("xT", t // 4)])
            K.act(sg[b][:], psb[1][:, 256:512], AF.Tanh, scale=0.5)
            K.stt(sg[b][:], sg[b][:], 1.0, psb[1][:, 256:512], ALU.add, ALU.mult)
            for hh in range(2):
                K.ts(sg[b][:, hh * 128:(hh + 1) * 128], sg[b][:, hh * 128:(hh + 1) * 128], hgnw[:, hh:hh + 1], ALU.mult)
            K.act(fg[b][:], psb[0][:, 256:512], AF.Tanh, scale=0.5)
            K.tt(fg[b][:], fg[b][:], omlb[:], ALU.mult)
            K.tt(fg[b][:], fg[b][:], lbb[:], ALU.add)
            K.act(logf[b][:], fg[b][:], AF.Ln)
            K.ts(kk[b][:], fg[b][:], -1.0, ALU.mult, 1.0, ALU.add)
            K.act(qs[b][:], psb[0][:, 0:256], AF.Tanh, scale=0.5)
            K.stt(qs[b][:], qs[b][:], 1.0, psb[0][:, 0:256], ALU.add, ALU.mult)
            K.cp(V[b][:], psb[1][:, 0:256], eng="act")
            K.mm(psb[2][:, 0:256], lhsT=tri16[:], rhs=logf[b][:])
            K.mm(psb[2][:, 256:512], lhsT=blk16[:], rhs=logf[b][:])
            K.mm(psb[3][:, 0:256], lhsT=dif16[:], rhs=logf[b][:])
            K.act(eb[b][:], psb[2][:, 0:256], AF.Exp)
            K.act(enb[b][:], psb[2][:, 0:256], AF.Exp, scale=-1.0)
            K.act(ed[b][:], psb[3][:, 0:256], AF.Exp)
            K.act(tot[b][:], psb[2][:, 256:512], AF.Exp)
            K.stt(qbar[b][:], qs[b][:], 0.5, eb[b][:], ALU.mult, ALU.mult)
            K.tt(kbar[b][:], kk[b][:], enb[b][:], ALU.mult, eng="pool")
            K.tt(kdec[b][:], kk[b][:], ed[b][:], ALU.mult)
            for i, src in enumerate((qbar[b], qbar[b], kbar[b], kbar[b])):
                K.tr(psb[3][:, i * 128:(i + 1) * 128], src[:, (i % 2) * 128:(i % 2 + 1) * 128], ident[:])
            for i in range(2):
                K.tr(psb[2][:, i * 128:(i + 1) * 128], tot[b][:, i * 128:(i + 1) * 128], ident[:])
            K.cp(qkT[b][:].rearrange("p a t -> p (a t)"), psb[3][:, :], eng="act")
            K.cp(totT[b][:].rearrange("p a t -> p (a t)"), psb[2][:, 0:256], eng="act")
            for h in range(4):
                po = (h % 2) * 64; hh = h // 2
                bank = psb[4] if h % 2 == 0 else psb[5]
                K.mm(bank[:, hh * 128:(hh + 1) * 128], lhsT=qkT[b][po:po + 64, 2 + hh, :], rhs=qkT[b][po:po + 64, hh, :])
            for hl in range(2):
                bank = psb[4] if hl == 0 else psb[5]
                K.tt(attT[b][:, hl * 2:(hl + 1) * 2, :], bank[:, 0:256].rearrange("p (a t) -> p a t", a=2),
                     tri16[:].unsqueeze(1).to_broadcast([128, 2, 128]), ALU.mult)
            K.tt(kdb[b][:], kdec[b][:].unsqueeze(1).to_broadcast([128, 8, 256]),
                 bm[:].unsqueeze(2).to_broadcast([128, 8, 256]), ALU.mult)
            for half in range(2):
                for c in range(half * 4, half * 4 + 4):
                    for h in range(4):
                        hl = h % 2; hh = h // 2
                        co = ((c % 4) * 2 + hh) * 64
                        K.mm(psb[6][hl * 64:(hl + 1) * 64, co:co + 64], lhsT=kdb[b][:, c, h * 64:(h + 1) * 64],
                             rhs=V[b][:, h * 64:(h + 1) * 64])
                for c in range(half * 4, half * 4 + 4):
                    for hh in range(2):
                        co = ((c % 4) * 2 + hh) * 64
                        K.stt(Sst[hh][:, c + 1, :], Sst[hh][:, c, :], totT[b][:, hh, c * 16:c * 16 + 1], psb[6][:, co:co + 64],
                              ALU.mult, ALU.add)
            for hh in range(2):
                K.cp(SbBD[0:64, :, hh, 0:64], Sst[hh][0:64, 0:8, :], eng="act")
                K.cp(SbBD[64:128, :, hh, 64:128], Sst[hh][64:128, 0:8, :])
            for hh in range(2):
                for hl in range(2):
                    h = 2 * hh + hl
                    K.mm(psb[4][hl * 64:(hl + 1) * 64, 256 + hh * 128:256 + (hh + 1) * 128], lhsT=V[b][:, h * 64:(h + 1) * 64],
                         rhs=attT[b][:, hl * 2 + hh, :], start=True, stop=False)
                for c in range(8):
                    K.mm(psb[4][:, 256 + hh * 128 + c * 16:256 + hh * 128 + (c + 1) * 16], lhsT=SbBD[:, c, hh, :],
                         rhs=qkT[b][:, hh, c * 16:(c + 1) * 16], start=False, stop=(c == 7))
            K.act(sq[b][:], psb[4][:, 256:512], AF.Square)
            K.mm(psb[5][:, 256:512], lhsT=bones[:], rhs=sq[b][:])
            K.act(rstd[b][:], psb[5][:, 256:512], AF.Ln, bias=epsc[:, 0:1], scale=1.0 / 64)
            K.act(rstd[b][:], rstd[b][:], AF.Exp, scale=-0.5)
            K.tt(oo[b][:], psb[4][:, 256:512], rstd[b][:], ALU.mult)
            K.tt(yT[:, :, tk], oo[b][:].rearrange("p (a t) -> p a t", a=2), sg[b][:].rearrange("p (a t) -> p a t", a=2), ALU.mult)
        K.phase_end()


    def rwkv(l):
        K.phase_begin()
        v4 = lambda ap: ap.rearrange("p (h e) -> p h e", h=4)
        b4 = lambda ap: ap.unsqueeze(2).to_broadcast([128, 4, 64])
        wb, wv = getw(("r", l), w_in_d[l, :, 0:896], 8, 896)
        if "b" in parts:
            prefetch(("a", l), w_in_d[l, :, C_ATT:C_ATT + 768], 8, 768)
        names = {}

        def bc(name, src, n=256):
            t_ = K.sb(name, [128, n]); K.dma(t_[:], src.partition_broadcast(128), wk=[t_]); return t_
        mucol = K.sb("mucol", [128, 1]); omucol = K.sb("omucol", [128, 1])
        K.dma(mucol[:], rmucol_d[l], wk=[mucol])
        K.ts(omucol[:], mucol[:], -1.0, ALU.mult, 1.0, ALU.add)
        fT = K.sb("fT", [128, T + 1]); loraT = K.sb("loraT", [128, T], BF16); ftmp = K.sb("ftmp", [128, T])
        K.memset(fT[:, 0:1], 0.0)
        for nb in range(4):
            pb = psb[nb % 2]
            for kc in range(8):
                K.mm(pb[:, :], lhsT=wv[:, kc, 768:896], rhs=xT[:, kc, nb * 512:(nb + 1) * 512], start=(kc == 0), stop=(kc == 7),
                     rk=[wb, ("xT", nb)])
            K.act(fT[:, 1 + nb * 512:1 + (nb + 1) * 512], pb[:, :], AF.Copy)
        K.ts(ftmp[:], fT[:, 0:T], mucol[:, 0:1], ALU.mult)
        K.stt(ftmp[:], fT[:, 1:T + 1], omucol[:, 0:1], ftmp[:], ALU.mult, ALU.add)
        K.act(loraT[0:32, :], ftmp[0:32, :], AF.Tanh)
        K.act(loraT[32:64, :], ftmp[32:64, :], AF.Copy)
        K.act(ftmp[64:128, :], ftmp[64:128, :], AF.Tanh, scale=0.5)
        K.ts(loraT[64:128, :], ftmp[64:128, :], 0.5, ALU.mult, 0.5, ALU.add)
        K.dma(lscr_d, loraT[:], rk=[loraT], wk=["lscr"])
        if l > 0:
            wb2, wv2 = wsmall("wvres", wvres_d[0], 8, 32)
            vmu = K.sb("vmu", [128, 1]); ovmu = K.sb("ovmu", [128, 1])
            K.dma(vmu[:], vmucol_d, wk=[vmu])
            K.ts(ovmu[:], vmu[:], -1.0, ALU.mult, 1.0, ALU.add)
            vrT = K.sb("vrT", [32, T], BF16)
            for nb in range(4):
                pb = psb[nb % 2]
                for kc in range(8):
                    K.mm(pb[0:32, :], lhsT=wv2[:, kc, 0:32], rhs=xT[:, kc, nb * 512:(nb + 1) * 512], start=(kc == 0), stop=(kc == 7),
                         rk=[wb2, ("xT", nb)])
                K.act(fT[0:32, 1 + nb * 512:1 + (nb + 1) * 512], pb[0:32, :], AF.Copy)
            K.ts(ftmp[0:32, :], fT[0:32, 0:T], vmu[0:32, 0:1], ALU.mult)
            K.stt(ftmp[0:32, :], fT[0:32, 1:T + 1], ovmu[0:32, 0:1], ftmp[0:32, :], ALU.mult, ALU.add)
            K.act(vrT[:], ftmp[0:32, :], AF.Copy)
            K.dma(vrscr_d, vrT[:], rk=[vrT], wk=["vrscr"])
        K.phase_end()
        K.phase_begin()
        mu_b = bc("mu_b", mush_d[l:l + 1, 0:768], 768)
        w0_b = bc("w0_b", rw0_d[l:l + 1, :]); a0_b = bc("a0_b", ra0_d[l:l + 1, :]); kk_b = bc("kk_b", rkk_d[l:l + 1, :])
        ka_b = bc("ka_b", rka_d[l:l + 1, :]); lw_b = bc("lnxw_b", rlw_d[l:l + 1, :]); lb_b = bc("lnxb_b", rlb_d[l:l + 1, :])
        rk_b = bc("rk_b", rrk_d[l:l + 1, :])
        msl = K.sb("msl", [128, 128]); msu = K.sb("msu", [128, 128])
        shiftM = K.sb("shiftM", [128, 128], BF16); carryM = K.sb("carryM", [128, 128], BF16)
        for t_, d_ in ((msl, msl_d), (msu, msu_d)):
            K.dma(t_[:], d_, wk=[t_])
        for t_, d_ in ((shiftM, shift_d), (carryM, carry_d)):
            K.dma(t_[:], d_, eng="pool", wk=[t_])
        loraW = K.sb("loraW", [128, 768], BF16)
        K.memset(loraW[:], 0.0, eng="pool")
        K.dma(loraW[0:32, 0:256], rw2_d[l], eng="pool", rk=[loraW], wk=[loraW])
        K.dma(loraW[32:64, 256:512], ra2_d[l], eng="pool", rk=[loraW], wk=[loraW])
        K.dma(loraW[64:128, 512:768], rg2_d[l], eng="pool", rk=[loraW], wk=[loraW])
        loraTc = [K.sb(f"loraTc{i}", [128, 128], BF16) for i in range(2)]
        if l > 0:
            v0_b = bc("v0_b", rv0_d[0:1, :])
            v2W = K.sb("v2W", [32, 256], BF16)
            K.dma(v2W[:], rv2_d[0], eng="pool", wk=[v2W])
            vrTc = [K.sb(f"vrTc{i}", [32, 128], BF16) for i in range(2)]
        F = lambda nm, n=256, dt_=F32: K.sb(nm, [128, n], dt_)
        D2 = lambda nm, n=256, dt_=F32: [K.sb(f"{nm}{i}", [128, n], dt_) for i in range(2)]
        fsb = D2("fsb", 768, BF16)
        fl = F("fl", 768)
        lw = F("lw"); aa = F("aa"); vv = F("vv"); kkn = F("kkn"); t1 = F("t1"); t2 = F("t2"); t3 = F("t3"); kt = F("kt"); be = F("be")
        s4 = K.sb("s4", [128, 4]); s4b = K.sb("s4b", [128, 4])
        Lsb = F("Lsb"); E1 = F("E1"); E2 = F("E2"); E3 = F("E3"); E4 = F("E4")
        gg2 = D2("gg"); bonus2 = D2("bonus")
        btil2 = D2("btil", 256, BF16); ktil2 = D2("ktil", 256, BF16); Vb2 = D2("rVb", 256, BF16)
        gam2 = [K.sb(f"gam{i}", [128, 2]) for i in range(2)]
        abkrT2 = [K.sb(f"abkrT{i}", [128, 8, 128], BF16) for i in range(2)]
        Am = [K.sb(f"Am{i}", [128, 4, 128], BF16) for i in range(2)]
        Bm = [K.sb(f"Bm{i}", [128, 4, 128], BF16) for i in range(2)]
        AKT2 = [K.sb(f"AKT{i}", [128, 4, 128], BF16) for i in range(2)]
        RBT2 = [K.sb(f"RBT{i}", [128, 4, 128], BF16) for i in range(2)]
        RKT2 = [K.sb(f"RKT{i}", [128, 4, 128], BF16) for i in range(2)]
        Pb2 = [K.sb(f"Pb{i}", [128, 4, 128], BF16) for i in range(2)]
        Sf = K.sb("Sf", [128, 2, 64]); SBD = K.sb("SBD", [128, 2, 128], BF16)
        K.memset(Sf[:], 0.0); K.memset(SBD[:], 0.0)
        Wb_ = F("Wb", 256, BF16); Ub_ = F("Ub", 256, BF16); yv = F("yv"); ysq = F("ysq", 256, BF16)
        mean4 = K.sb("mean4", [128, 4]); var4 = K.sb("var4", [128, 4])
        m3 = lambda mk: mk[:].unsqueeze(1).to_broadcast([128, 2, 128])
        X = pbf[:].bitcast(F32)
        for c in range(NT):
            par = c % 2
            tk = slice(c * 128, (c + 1) * 128)
            f = fsb[c % 2]; fp_ = fsb[(c + 1) % 2]
            gg = gg2[par]; bonus = bonus2[par]; btil = btil2[par]; ktil = ktil2[par]; Vb_ = Vb2[par]; gam = gam2[par]
            abkrT = abkrT2[par]; AKT = AKT2[par]; RBT = RBT2[par]; RKT = RKT2[par]; Pb = Pb2[par]
            for (bank, o0, c0, c1) in ((psb[0], 0, 0, 512), (psb[1], 0, 512, 768)):
                for kc in range(8):
                    K.mm(bank[:, o0:o0 + c1 - c0], lhsT=xT[:, kc, tk], rhs=wv[:, kc, c0:c1], start=(kc == 0), stop=(kc == 7),
                         rk=[wb, ("xT", c // 4)])
            K.cp(f[:, 0:512], psb[0][:, :], eng="act")
            K.cp(f[:, 512:768], psb[1][:, 0:256], eng="act")
            for (bank, o0, c0, c1) in ((psb[2], 0, 0, 512), (psb[1], 256, 512, 768)):
                K.mm(bank[:, o0:o0 + c1 - c0], lhsT=shiftM[:], rhs=f[:, c0:c1], start=True, stop=(c == 0))
                if c > 0:
                    K.mm(bank[:, o0:o0 + c1 - c0], lhsT=carryM[:], rhs=fp_[:, c0:c1], start=False, stop=True)
            K.tt(fl[:, 0:512], psb[2][:, :], mu_b[:, 0:512], ALU.mult)
            K.tt(fl[:, 512:768], psb[1][:, 256:512], mu_b[:, 512:768], ALU.mult)
            K.tt(fl[:], fl[:], f[:], ALU.add, eng="pool")
            r_ = fl[:, 0:256]; k_ = fl[:, 256:512]; v_ = fl[:, 512:768]
            lt = loraTc[c % 2]
            K.dma(lt[:], lscr_d[:, tk], rk=["lscr", lt], wk=[lt])
            K.mm(psb[3][:, 0:512], lhsT=lt[:], rhs=loraW[:, 0:512])
            K.mm(psb[0][:, 0:256], lhsT=lt[:], rhs=loraW[:, 512:768])
            K.tt(lw[:], psb[3][:, 0:256], w0_b[:], ALU.add)
            K.act(lw[:], lw[:], AF.Tanh, scale=0.5)
            K.ts(lw[:], lw[:], -0.3032653298563167, ALU.mult, -0.3032653298563167, ALU.add, eng="pool")
            K.tt(aa[:], psb[3][:, 256:512], a0_b[:], ALU.add)
            K.act(aa[:], aa[:], AF.Tanh, scale=0.5)
            K.ts(aa[:], aa[:], 0.5, ALU.mult, 0.5, ALU.add, eng="pool")
            K.cp(gg[:], psb[0][:, 0:256], eng="act")
            if l == 0:
                K.cp(vv[:], v_, eng="pool")
                K.dma(vfirst_d[tk, :], vv[:], rk=[vv], wk=[("vfirst", c)])
            else:
                vt_ = vrTc[c % 2]
                K.dma(vt_[:], vrscr_d[:, tk], rk=["vrscr", vt_], wk=[vt_])
                K.mm(psb[0][:, 256:512], lhsT=vt_[0:32, :], rhs=v2W[0:32, :])
                K.tt(t1[:], psb[0][:, 256:512], v0_b[:], ALU.add)
                K.act(t1[:], t1[:], AF.Tanh, scale=0.5)
                K.ts(t1[:], t1[:], 0.5, ALU.mult, 0.5, ALU.add, eng="pool")
                K.dma(t2[:], vfirst_d[tk, :], rk=[("vfirst", c), t2], wk=[t2])
                K.tt(t2[:], t2[:], v_, ALU.subtract, eng="pool")
                K.tt(t2[:], t2[:], t1[:], ALU.mult, eng="pool")
                K.tt(vv[:], t2[:], v_, ALU.add, eng="pool")
            K.cp(Vb_[:], vv[:], eng="act")
            K.tt(kkn[:], k_, kk_b[:], ALU.mult, eng="pool")
            K.act(t1[:], kkn[:], AF.Square)
            K.S.op("dve", lambda t1=t1: nc.vector.tensor_reduce(out=s4[:], in_=v4(t1[:]), axis=AX.X, op=ALU.add), reads=[t1], writes=[s4])
            K.ts(s4[:], s4[:], 1e-24, ALU.max)
            K.tt(s4[:], s4[:], mhalf[:, 0:4], ALU.pow, eng="pool")
            K.tt(v4(kkn[:]), v4(kkn[:]), b4(s4[:]), ALU.mult)
            K.stt(t3[:], aa[:], -1.0, ka_b[:], ALU.add, ALU.mult)
            K.stt(kt[:], t3[:], 1.0, k_, ALU.add, ALU.mult)
            K.tt(be[:], kkn[:], aa[:], ALU.mult, eng="pool")
            K.tt(t3[:], r_, kt[:], ALU.mult, eng="pool")
            K.tt(t3[:], t3[:], rk_b[:], ALU.mult, eng="pool")
            K.S.op("dve", lambda t3=t3: nc.vector.tensor_reduce(out=s4b[:], in_=v4(t3[:]), axis=AX.X, op=ALU.add), reads=[t3], writes=[s4b])
            K.tt(v4(bonus[:]), v4(vv[:]), b4(s4b[:]), ALU.mult)
            K.mm(psb[2][:, 0:256], lhsT=triT[:], rhs=lw[:])
            K.mm(psb[2][:, 256:512], lhsT=msu[:], rhs=lw[:])
            K.mm(psb[3][:, 256:512], lhsT=msl[:], rhs=lw[:])
            for hh in range(2):
                K.mm(psb[3][:, hh:hh + 1], lhsT=lw[:, hh * 128:(hh + 1) * 128], rhs=ones_f[:, 0:1])
            K.act(gam[:], psb[3][:, 0:2], AF.Exp)
            K.act(E1[:], psb[2][:, 256:512], AF.Exp)
            K.act(E2[:], psb[2][:, 0:256], AF.Exp, scale=-1.0)
            K.act(E3[:], psb[2][:, 0:256], AF.Exp)
            K.act(E4[:], psb[3][:, 256:512], AF.Exp)
            K.tt(btil[:], be[:], E4[:], ALU.mult, eng="pool")
            K.tt(ktil[:], kt[:], E4[:], ALU.mult)
            K.stt(E1[:], kkn[:], -1.0, E1[:], ALU.mult, ALU.mult)
            K.tt(be[:], be[:], E2[:], ALU.mult, eng="pool")
            K.tt(kt[:], kt[:], E2[:], ALU.mult)
            K.tt(E3[:], r_, E3[:], ALU.mult, eng="pool")
            for qi, src in enumerate((E1, be, kt, E3)):
                bank = psb[0] if qi < 2 else psb[1]
                for hh in range(2):
                    K.tr(bank[:, ((qi % 2) * 2 + hh) * 128:((qi % 2) * 2 + hh + 1) * 128], src[:, hh * 128:(hh + 1) * 128], ident[:])
            K.cp(abkrT[:, 0:4, :].rearrange("p a t -> p (a t)"), psb[0][:, :], eng="act")
            K.cp(abkrT[:, 4:8, :].rearrange("p a t -> p (a t)"), psb[1][:, :])
            aT = lambda h, abkrT=abkrT: abkrT[(h % 2) * 64:(h % 2) * 64 + 64, 0 + h // 2, :]
            bT = lambda h, abkrT=abkrT: abkrT[(h % 2) * 64:(h % 2) * 64 + 64, 2 + h // 2, :]
            kT_ = lambda h, abkrT=abkrT: abkrT[(h % 2) * 64:(h % 2) * 64 + 64, 4 + h // 2, :]
            rT = lambda h, abkrT=abkrT: abkrT[(h % 2) * 64:(h % 2) * 64 + 64, 6 + h // 2, :]

            def amat(L_, R_, mask, dst):
                for h in range(4):
                    bank = psb[4] if h % 2 == 0 else psb[5]
                    K.mm(bank[:, (h // 2) * 128:(h // 2 + 1) * 128], lhsT=L_(h), rhs=R_(h))
                for hl in range(2):
                    bank = psb[4] if hl == 0 else psb[5]
                    K.tt(dst[:, hl * 2:hl * 2 + 2, :], bank[:, 0:256].rearrange("p (a t) -> p a t", a=2), m3(mask), ALU.mult)
            hidx = lambda h: (h % 2) * 2 + h // 2
            amat(aT, bT, msl, Am[0])
            amat(bT, aT, msu, Bm[0])
            amat(kT_, aT, msu, AKT)
            amat(bT, rT, triT, RBT)
            amat(kT_, rT, triT, RKT)
            K.tt(Pb[:], Bm[0][:], ident[:].unsqueeze(1).to_broadcast([128, 4, 128]), ALU.add, eng="pool")
            cur = 0
            for j in range(1, 7):
                nxt = 1 - cur
                for hi in range(4):
                    K.mm(psb[4][:, hi * 128:(hi + 1) * 128], lhsT=Bm[cur][:, hi, :], rhs=Am[cur][:, hi, :])
                if j < 6:
                    for hi in range(4):
                        K.mm(psb[5][:, hi * 128:(hi + 1) * 128], lhsT=Am[cur][:, hi, :], rhs=Bm[cur][:, hi, :])
                K.cp(Am[nxt][:].rearrange("p a t -> p (a t)"), psb[4][:, :], eng="act")
                if j < 6:
                    K.cp(Bm[nxt][:].rearrange("p a t -> p (a t)"), psb[5][:, :])
                for hi in range(4):
                    K.mm(psb[6][:, hi * 128:(hi + 1) * 128], lhsT=Am[nxt][:, hi, :], rhs=Pb[:, hi, :])
                K.tt(Pb[:].rearrange("p a t -> p (a t)"), psb[6][:, :], Pb[:].rearrange("p a t -> p (a t)"), ALU.add)
                cur = nxt
            for hh in range(2):
                K.mm(X[:, hh * 128:(hh + 1) * 128], lhsT=abkrT[:, 0 + hh, :], rhs=SBD[:, hh, :], start=True, stop=False)
                for hl in range(2):
                    h = 2 * hh + hl
                    K.mm(X[:, h * 64:(h + 1) * 64], lhsT=AKT[:, hidx(h), :], rhs=Vb_[:, h * 64:(h + 1) * 64], start=False, stop=(hl == 1))
            K.cp(Wb_[:], X[:, 0:256], eng="act")
            for h in range(4):
                K.mm(X[:, 256 + h * 64:256 + (h + 1) * 64], lhsT=Pb[:, hidx(h), :], rhs=Wb_[:, h * 64:(h + 1) * 64])
            K.cp(Ub_[:], X[:, 256:512], eng="act")
            for hh in range(2):
                K.mm(X[:, hh * 128:(hh + 1) * 128], lhsT=abkrT[:, 6 + hh, :], rhs=SBD[:, hh, :], start=True, stop=False)
                for hl in range(2):
                    h = 2 * hh + hl
                    K.mm(X[:, h * 64:(h + 1) * 64], lhsT=RBT[:, hidx(h), :], rhs=Ub_[:, h * 64:(h + 1) * 64], start=False, stop=False)
                    K.mm(X[:, h * 64:(h + 1) * 64], lhsT=RKT[:, hidx(h), :], rhs=Vb_[:, h * 64:(h + 1) * 64], start=False, stop=(hl == 1))
            for h in range(4):
                hl = h % 2; hh = h // 2
                K.mm(X[hl * 64:(hl + 1) * 64, 256 + hh * 64:256 + (hh + 1) * 64], lhsT=btil[:, h * 64:(h + 1) * 64],
                     rhs=Ub_[:, h * 64:(h + 1) * 64], start=True, stop=False)
                K.mm(X[hl * 64:(hl + 1) * 64, 256 + hh * 64:256 + (hh + 1) * 64], lhsT=ktil[:, h * 64:(h + 1) * 64],
                     rhs=Vb_[:, h * 64:(h + 1) * 64], start=False, stop=True)
            K.cp(yv[:], X[:, 0:256], eng="act")
            for hh in range(2):
                K.stt(Sf[:, hh, :], Sf[:, hh, :], gam[:, hh:hh + 1], X[:, 256 + hh * 64:256 + (hh + 1) * 64], ALU.mult, ALU.add)
            K.cp(SBD[0:64, :, 0:64], Sf[0:64, :, :], eng="act")
            K.cp(SBD[64:128, :, 64:128], Sf[64:128, :, :])
            K.S.op("dve", lambda: nc.vector.tensor_reduce(out=mean4[:], in_=v4(yv[:]), axis=AX.X, op=ALU.add), reads=[yv], writes=[mean4])
            K.ts(mean4[:], mean4[:], 1.0 / 64, ALU.mult)
            K.tt(v4(yv[:]), v4(yv[:]), b4(mean4[:]), ALU.subtract)
            K.act(ysq[:], yv[:], AF.Square)
            K.S.op("dve", lambda: nc.vector.tensor_reduce(out=var4[:], in_=v4(ysq[:]), axis=AX.X, op=ALU.add), reads=[ysq], writes=[var4])
            K.ts(var4[:], var4[:], 1.0 / 64, ALU.mult, 64e-5, ALU.add)
            K.tt(var4[:], var4[:], mhalf[:, 0:4], ALU.pow, eng="pool")
            K.tt(v4(yv[:]), v4(yv[:]), b4(var4[:]), ALU.mult)
            K.tt(yv[:], yv[:], lw_b[:], ALU.mult, eng="pool")
            K.tt(yv[:], yv[:], lb_b[:], ALU.add, eng="pool")
            K.tt(yv[:], yv[:], bonus[:], ALU.add, eng="pool")
            K.tt(yv[:], yv[:], gg[:], ALU.mult, eng="pool")
            for g in range(2):
                K.tr(X[:, g * 128:(g + 1) * 128], yv[:, g * 128:(g + 1) * 128], ident[:])
            K.cp(yT[:, :, tk], X[:, 0:256].rearrange("p (g e) -> p g e", g=2))
        K.phase_end()
        K.mark("rwkv_end")

    def dump(m):
        if debug:
            K.dma(dbg_d[m], yT[:].rearrange("p c t -> p (c t)"), rk=[yT], wk=[("dbg", m)])

    for l in range(nlayers):
        for t in range(NT):
            K.act(xres[t][:], xres[t][:], AF.Copy, scale=ALPHA)
        if "a" in parts:
            rwkv(l); dump(0); out_proj(l, 0)
        if "b" in parts:
            attention(l); dump(1); out_proj(l, 1)
        if "c" in parts:
            ssd(l); dump(2); out_proj(l, 2)
        if "d" in parts:
            hgrn(l); dump(3); out_proj(l, 3)
        K.phase_begin()
        lnw = K.sb("lnw", [128, D]); lnb = K.sb("lnb", [128, D])
        K.dma(lnw[:], ln1w_d[l:l + 1, :].partition_broadcast(128), wk=[lnw])
        K.dma(lnb[:], ln1b_d[l:l + 1, :].partition_broadcast(128), wk=[lnb])
        for t in range(NT):
            layer_norm(t, lnw, lnb)
            build_xT(t)
        ffn(l)
        if l + 1 < nlayers and "a" in parts:
            prefetch(("r", l + 1), w_in_d[l + 1, :, 0:896], 8, 896)
        lnw = K.sb("lnw2", [128, D]); lnb = K.sb("lnb2", [128, D])
        K.dma(lnw[:], ln2w_d[l:l + 1, :].partition_broadcast(128), wk=[lnw])
        K.dma(lnb[:], ln2b_d[l:l + 1, :].partition_broadcast(128), wk=[lnb])
        for t in range(NT):
            layer_norm(t, lnw, lnb)
            if l == nlayers - 1:
                K.dma(out_d[t * 128:(t + 1) * 128, :], xres[t][:], rk=[xres[t]], wk=[("out", t)])
            else:
                build_xT(t)
        K.phase_end()
    K.S.emit(limit)
    K.S.stats["marks"] = dict(K.marks)
    return nc, K.S.stats


def make_consts():
    c = {}
    c["c_ident"] = np.eye(128, dtype=np.float32)
    j = np.arange(128)[:, None].astype(np.float64)
    i = np.arange(128)[None, :].astype(np.float64)
    am = np.zeros((128, 5, 4, 128), np.float32)
    for mid, (d, prev) in enumerate([(1, True), (1, False), (4, True), (4, False), (16, False)]):
        for h in range(4):
            if prev:
                dist = 128 + i - j
                valid = dist <= 128
            else:
                dist = i - j
                valid = dist >= 0
            am[:, mid, (h % 2) * 2 + h // 2, :] = np.where(valid, -SLOPES[h] * d * dist, NEG)
    c["c_amask"] = am.reshape(128, -1)
    ii = np.arange(128)
    c["c_triT"] = (ii[:, None] <= ii[None, :]).astype(np.float32)
    c["c_maskneg"] = np.where(ii[None, :] >= ii[:, None], 0.0, NEG).astype(np.float32)
    c["c_msl"] = (ii[None, :] < ii[:, None]).astype(np.float32)
    c["c_msu"] = (ii[:, None] < ii[None, :]).astype(np.float32)
    c["c_shift"] = (ii[None, :] == ii[:, None] + 1).astype(np.float32) - np.eye(128, dtype=np.float32)
    cm = np.zeros((128, 128), np.float32); cm[127, 0] = 1.0
    c["c_carry"] = cm
    same16 = (ii[:, None] // 16) == (ii[None, :] // 16)
    c["c_tri16"] = (same16 & (ii[:, None] <= ii[None, :])).astype(np.float32)
    c["c_blk16"] = same16.astype(np.float32)
    c["c_bm"] = ((ii[:, None] // 16) == np.arange(8)[None, :]).astype(np.float32)
    c["c_bones64"] = ((ii[:, None] // 64) == (ii[None, :] // 64)).astype(np.float32)
    return c


def make_params(inputs):
    p = {}
    cw = np.asarray(inputs["ssd_conv_w"], np.float32)
    cb = np.asarray(inputs["ssd_conv_b"], np.float32)
    pk = np.zeros((DEPTH, 128, 6, 5), np.float32)
    pk[:, :, :, 0:4] = cw.reshape(DEPTH, 4, 6, 128).transpose(0, 3, 2, 1)
    pk[:, :, :, 4] = cb.reshape(DEPTH, 6, 128).transpose(0, 2, 1)
    p["c_ssdconv"] = pk.reshape(DEPTH, 128, 30)
    p["c_rmucol"] = np.ascontiguousarray(np.asarray(inputs["mu_shift"], np.float32)[:, 768:896].reshape(DEPTH, 128, 1))
    vm = np.zeros((128, 1), np.float32); vm[0:32, 0] = np.asarray(inputs["mu_vres"], np.float32)[0]
    p["c_vmucol"] = vm
    p["c_rrk"] = np.ascontiguousarray(np.asarray(inputs["rwkv_r_k"], np.float32).reshape(DEPTH, 256))
    p["c_hgnw"] = np.ascontiguousarray(np.asarray(inputs["hgrn_norm_w"], np.float32).reshape(DEPTH, 2, 128).transpose(0, 2, 1))
    p["c_ssdD"] = np.repeat(np.asarray(inputs["ssd_D"], np.float32), 64, axis=1)
    return p


_CACHE = {}


SHARED = ("w_in", "w_out", "w_up", "w_down", "ln1_w", "ln1_b", "ln2_w", "ln2_b",
          "ssd_dt_bias", "ssd_A_log", "ssd_norm_w", "lower_bounds",
          "mu_shift", "rwkv_w0", "rwkv_a0", "rwkv_k_k", "rwkv_k_a", "rwkv_lnx_w", "rwkv_lnx_b", "rwkv_w2", "rwkv_a2",
          "rwkv_g2", "rwkv_v0", "rwkv_v2", "w_in_vres")


def make_inmap(inputs, b, consts=None, shared=None):
    if consts is None:
        consts = make_consts()
    if shared is None:
        shared = {k: np.ascontiguousarray(inputs[k], dtype=np.float32) for k in SHARED}
        shared.update(make_params(inputs))
    m = {"x": np.ascontiguousarray(inputs["x"][b], dtype=np.float32)}
    m.update(shared)
    m.update(consts)
    return m


def kernel(**inputs):
    if "prog" not in _CACHE:
        _CACHE["prog"] = build_program()
    nc, stats = _CACHE["prog"]
    consts = make_consts()
    shared = {k: np.ascontiguousarray(inputs[k], dtype=np.float32) for k in SHARED}
    shared.update(make_params(inputs))
    in_maps = [make_inmap(inputs, b, consts, shared) for b in range(8)]
    res = run_bass_kernel_spmd(nc, in_maps, core_ids=list(range(8)))
    return np.stack([np.asarray(r["out"], dtype=np.float32) for r in res.results], axis=0)
```
